# Optimizing a Trainium2 kernel written in Bass

```python
import math
import jax, jax.numpy as jnp
from jax import lax
import numpy as np

D_MODEL = 1024
BATCH = 4
SEQ = 4096
DEPTH = 1

HYENA_WIDTH = 1024
SHORT_CONV = 3
FILTER_EMB = 33
FILTER_BANDS = (FILTER_EMB - 1) // 2
FILTER_HIDDEN = 64
FILTER_OUT_SCALE = 0.04
DECAY_TARGET = 1e-2
DECAY_FAST = 0.3
DECAY_SLOW = 1.5
SGU_WIDTH = 1024
SGU_HEADS = 8
SGU_HEAD_DIM = SGU_WIDTH // SGU_HEADS
SGU_CHUNK = 128
FFN_HIDDEN = 2816
FFN_CONV = 3
EPS = 1e-6
IN_WIDTH = 3 * HYENA_WIDTH + 2 * SGU_WIDTH + 2 * D_MODEL

kernel_name = "hyena_sgu_gated_hybrid_encoder"


def rms_norm(x, g):
    xf = x.astype(jnp.float32)
    y = xf * lax.rsqrt(jnp.mean(xf * xf, axis=-1, keepdims=True) + EPS)
    return (y * g.astype(jnp.float32)).astype(x.dtype)


def depthwise_conv_centred(x, w, b):
    y = lax.conv_general_dilated(
        x, w[:, None, :].astype(x.dtype), window_strides=(1,), padding="SAME",
        dimension_numbers=("NWC", "WIO", "NWC"), feature_group_count=x.shape[-1])
    return y + b.astype(x.dtype)


def implicit_filters(L, w1, b1, w2, b2, w3, b3, freq, w4, decay):
    f32 = jnp.float32
    t = jnp.linspace(0.0, 1.0, L, dtype=f32)[:, None]
    bands = jnp.linspace(1e-4, FILTER_BANDS - 1, FILTER_BANDS, dtype=f32)[None, :]
    phase = (2.0 * math.pi / L) * jnp.arange(L, dtype=f32)[:, None] * bands
    z = jnp.concatenate([t, jnp.cos(phase), -jnp.sin(phase)], axis=-1)
    a = freq.astype(f32)
    h = jnp.sin(a * (z @ w1.astype(f32) + b1.astype(f32)))
    h = jnp.sin(a * (h @ w2.astype(f32) + b2.astype(f32)))
    h = jnp.sin(a * (h @ w3.astype(f32) + b3.astype(f32)))
    h = h @ w4.astype(f32)
    return h * jnp.exp(-t * jnp.abs(decay.astype(f32)))


def bidirectional_long_conv(u, h_fwd, h_bwd, skip):
    L, C = u.shape[1], u.shape[2]
    k = jnp.concatenate([h_fwd, jnp.zeros((1, C), jnp.float32), h_bwd[:0:-1]], axis=0)
    uf = u.astype(jnp.float32)
    spec = jnp.fft.rfft(uf, n=2 * L, axis=1) * jnp.fft.rfft(k, axis=0)[None]
    y = jnp.fft.irfft(spec, n=2 * L, axis=1)[:, :L]
    return (y + uf * skip.astype(jnp.float32)).astype(u.dtype)


def hyena_branch(p, conv_w, conv_b, fw1, fb1, fw2, fb2, fw3, fb3, ffreq, fw4, decay, skip):
    p = depthwise_conv_centred(p, conv_w, conv_b)
    x0, x1, v = jnp.split(p, 3, axis=-1)
    filt = implicit_filters(p.shape[1], fw1, fb1, fw2, fb2, fw3, fb3, ffreq, fw4, decay)
    return x0 * bidirectional_long_conv(x1 * v, filt[:, :HYENA_WIDTH], filt[:, HYENA_WIDTH:], skip)


def spatial_gating_branch(p, norm_g, w_s, b_s):
    B, L, _ = p.shape
    u, v = jnp.split(jax.nn.gelu(p, approximate=False), 2, axis=-1)
    v = rms_norm(v, norm_g).reshape(B, L // SGU_CHUNK, SGU_CHUNK, SGU_HEADS, SGU_HEAD_DIM)
    s = jnp.einsum("gpq,bnqgd->bnpgd", w_s.astype(v.dtype), v) + b_s.T.astype(v.dtype)[:, :, None]
    return u * s.reshape(B, L, SGU_WIDTH)


def hybrid_layer(x, norm1_g, w_in, hy_conv_w, hy_conv_b, filt_w1, filt_b1, filt_w2, filt_b2,
                 filt_w3, filt_b3, filt_freq, filt_w4, hy_decay, hy_skip, sgu_norm_g, sgu_w,
                 sgu_b, w_proj_hyena, w_proj_sgu, w_out, norm2_g, w_up, ffn_conv_w, ffn_conv_b,
                 w_down):
    h = rms_norm(x, norm1_g)
    proj = h @ w_in
    p_hy, p_sgu, p_gate = jnp.split(
        proj, [3 * HYENA_WIDTH, 3 * HYENA_WIDTH + 2 * SGU_WIDTH], axis=-1)
    y_a = hyena_branch(p_hy, hy_conv_w, hy_conv_b, filt_w1, filt_b1, filt_w2, filt_b2,
                       filt_w3, filt_b3, filt_freq, filt_w4, hy_decay, hy_skip)
    y_b = spatial_gating_branch(p_sgu, sgu_norm_g, sgu_w, sgu_b)
    g_a, g_b = jnp.split(jax.nn.sigmoid(p_gate), 2, axis=-1)
    merged = g_a * (y_a @ w_proj_hyena) + g_b * (y_b @ w_proj_sgu)
    x = x + merged @ w_out
    h = rms_norm(x, norm2_g)
    a, gate = jnp.split(h @ w_up, 2, axis=-1)
    a = depthwise_conv_centred(a, ffn_conv_w, ffn_conv_b)
    return x + (jax.nn.silu(a) * gate) @ w_down


def setup_inputs(seed: int = 0) -> dict:
    key = jax.random.key(seed)
    ks = iter(jax.random.split(key, 32))
    f32 = jnp.float32

    def nrm(shape, scale):
        return scale * jax.random.normal(next(ks), shape, f32)

    Dp = DEPTH
    decay_lo = math.log(DECAY_TARGET) / DECAY_SLOW
    decay_hi = math.log(DECAY_TARGET) / DECAY_FAST
    decay_base = jnp.tile(jnp.linspace(decay_lo, decay_hi, HYENA_WIDTH, dtype=f32), 2)
    return {
        "x": nrm((BATCH, SEQ, D_MODEL), 1.0),
        "norm1_g": 1.0 + nrm((Dp, D_MODEL), 0.1),
        "w_in": nrm((Dp, D_MODEL, IN_WIDTH), D_MODEL ** -0.5),
        "hy_conv_w": nrm((Dp, SHORT_CONV, 3 * HYENA_WIDTH), SHORT_CONV ** -0.5),
        "hy_conv_b": nrm((Dp, 3 * HYENA_WIDTH), 0.01),
        "filt_w1": nrm((Dp, FILTER_EMB, FILTER_HIDDEN), FILTER_EMB ** -0.5),
        "filt_b1": nrm((Dp, FILTER_HIDDEN), 0.2),
        "filt_w2": nrm((Dp, FILTER_HIDDEN, FILTER_HIDDEN), FILTER_HIDDEN ** -0.5),
        "filt_b2": nrm((Dp, FILTER_HIDDEN), 0.2),
        "filt_w3": nrm((Dp, FILTER_HIDDEN, FILTER_HIDDEN), FILTER_HIDDEN ** -0.5),
        "filt_b3": nrm((Dp, FILTER_HIDDEN), 0.2),
        "filt_freq": 1.0 + nrm((Dp, FILTER_HIDDEN), 0.1),
        "filt_w4": nrm((Dp, FILTER_HIDDEN, 2 * HYENA_WIDTH), FILTER_OUT_SCALE * FILTER_HIDDEN ** -0.5),
        "hy_decay": decay_base * (1.0 + nrm((Dp, 2 * HYENA_WIDTH), 0.05)),
        "hy_skip": nrm((Dp, HYENA_WIDTH), 0.5),
        "sgu_norm_g": 1.0 + nrm((Dp, SGU_WIDTH), 0.1),
        "sgu_w": nrm((Dp, SGU_HEADS, SGU_CHUNK, SGU_CHUNK), SGU_CHUNK ** -0.5),
        "sgu_b": 1.0 + nrm((Dp, SGU_HEADS, SGU_CHUNK), 0.1),
        "w_proj_hyena": nrm((Dp, HYENA_WIDTH, D_MODEL), HYENA_WIDTH ** -0.5),
        "w_proj_sgu": nrm((Dp, SGU_WIDTH, D_MODEL), SGU_WIDTH ** -0.5),
        "w_out": nrm((Dp, D_MODEL, D_MODEL), D_MODEL ** -0.5),
        "norm2_g": 1.0 + nrm((Dp, D_MODEL), 0.1),
        "w_up": nrm((Dp, D_MODEL, 2 * FFN_HIDDEN), D_MODEL ** -0.5),
        "ffn_conv_w": nrm((Dp, FFN_CONV, FFN_HIDDEN), FFN_CONV ** -0.5),
        "ffn_conv_b": nrm((Dp, FFN_HIDDEN), 0.01),
        "w_down": nrm((Dp, FFN_HIDDEN, D_MODEL), FFN_HIDDEN ** -0.5),
        "final_g": 1.0 + nrm((D_MODEL,), 0.1),
    }


def reference(x, norm1_g, w_in, hy_conv_w, hy_conv_b, filt_w1, filt_b1, filt_w2, filt_b2,
              filt_w3, filt_b3, filt_freq, filt_w4, hy_decay, hy_skip, sgu_norm_g, sgu_w, sgu_b,
              w_proj_hyena, w_proj_sgu, w_out, norm2_g, w_up, ffn_conv_w, ffn_conv_b, w_down,
              final_g):
    for i in range(DEPTH):
        x = hybrid_layer(
            x, norm1_g[i], w_in[i], hy_conv_w[i], hy_conv_b[i], filt_w1[i], filt_b1[i],
            filt_w2[i], filt_b2[i], filt_w3[i], filt_b3[i], filt_freq[i], filt_w4[i],
            hy_decay[i], hy_skip[i], sgu_norm_g[i], sgu_w[i], sgu_b[i], w_proj_hyena[i],
            w_proj_sgu[i], w_out[i], norm2_g[i], w_up[i], ffn_conv_w[i], ffn_conv_b[i],
            w_down[i])
    return rms_norm(x, final_g)
```

```python
import contextlib
import numpy as np
import concourse.bass as bass
import concourse.mybir as mybir
from concourse.bass_utils import run_bass_kernel_spmd

F32 = mybir.dt.float32
BF16 = mybir.dt.bfloat16
AF = mybir.ActivationFunctionType
ALU = mybir.AluOpType

D = 1024
SEQ = 4096
NTOK = 2176
NCH = 17
NM1 = 34
FFN = 2816
NFT = 22
EPS = 1e-6
MAGIC = 12582912.0
TWO_PI = float(2 * np.pi)
TOKT = [(0, 512), (512, 512), (1024, 512), (1536, 512), (2048, 128)]


def _blk(a, b, c, d):
    return np.block([[a, b], [c, d]])


def make_consts(h):
    rot = 30 * h
    q = np.arange(64)[:, None, None]
    n2 = np.arange(64)[None, :, None]

    def s1(n1):
        out = np.zeros((n1.shape[0], 64, 128))
        f1 = np.arange(65)[None, None, :]
        phi = 2 * np.pi * (n1 * f1 / 128.0 + n2 * f1 / 8192.0)
        re = np.cos(phi)
        re[:, :, 64] = np.cos(np.pi * n1[:, :, 0]) * np.ones((1, 64))
        out[:, :, 0:65] = re
        f1i = np.arange(1, 64)[None, None, :]
        phii = 2 * np.pi * (n1 * f1i / 128.0 + n2 * f1i / 8192.0)
        out[:, :, 65:128] = -np.sin(phii)
        return out

    S1u0 = s1((q + rot) % 64)
    S1u = np.zeros((128, 64, 128))
    S1u[0::2, 0:32, :] = S1u0[:, 0:32, :]
    S1u[1::2, 32:64, :] = S1u0[:, 32:64, :]
    S1k = s1(np.arange(128)[:, None, None])
    n2v = np.arange(64)[:, None]
    f2 = np.arange(64)[None, :]
    th = 2 * np.pi * n2v * f2 / 64.0
    Fc, Fs = np.cos(th), np.sin(th)
    thp = th + np.pi * n2v / 64.0
    Fcp, Fsp = np.cos(thp), np.sin(thp)
    Z = np.zeros((64, 64))
    S2 = [
        _blk(Fc, -Fs, Fs, Fc), _blk(-Fs, Fc, Fc, Fs), _blk(Fc, Fc, Fs, Fs), _blk(-Fs, -Fs, Fc, Fc),
        _blk(Fc, -Fs, Z, Z), _blk(-Fs, Fc, Z, Z), _blk(Fc, Fc, Z, Z), _blk(-Fs, -Fs, Z, Z),
        _blk(Z, Z, Fcp, -Fsp), _blk(Z, Z, -Fsp, Fcp), _blk(Z, Z, Fcp, Fcp), _blk(Z, Z, -Fsp, -Fsp),
    ]
    f2v = np.arange(64)[:, None]
    m2 = np.arange(64)[None, :]
    ga = 2 * np.pi * m2 * f2v / 64.0
    Gr, Gi = np.cos(ga), np.sin(ga)
    gp = ga + np.pi * m2 / 64.0
    Gc, Gs = np.cos(gp), np.sin(gp)
    T = [
        _blk(Gr, Gi, -Gr, -Gi), _blk(-Gi, Gr, -Gi, Gr),
        _blk(Gr, Z, -Gr, Z), _blk(-Gi, Z, -Gi, Z),
        _blk(Z, Gc, Z, -Gc), _blk(Z, -Gs, Z, -Gs),
    ]
    mats = np.stack(S2 + T, axis=1)
    m1 = (np.arange(NM1) + 30 * h)[None, None, :]
    f1 = np.arange(64)[:, None, None]
    m2b = np.arange(64)[None, :, None]
    psi = 2 * np.pi * (m1 * f1 / 128.0 + m2b * f1 / 8192.0)
    Lre = 2 * np.cos(psi) / 8192.0
    Lim = -2 * np.sin(psi) / 8192.0
    Lre[0] = 1.0 / 8192.0
    Lim[0] = ((-1.0) ** m1[0]) / 8192.0 * np.ones((64, 1))
    Lm = np.concatenate([Lre, Lim], axis=0)
    f32 = np.float32
    t = np.linspace(0.0, 1.0, SEQ, dtype=f32)[:, None]
    bands = np.linspace(1e-4, 15.0, 16, dtype=f32)[None, :]
    phase = (f32(2.0 * np.pi / SEQ) * np.arange(SEQ, dtype=f32)[:, None]) * bands
    z = np.concatenate([t, np.cos(phase), -np.sin(phase)], axis=-1).astype(f32)
    j = np.arange(8192)
    l = np.where(j < 4096, j, 8192 - j)
    l = np.where(j == 4096, 0, l)
    zc = z[l]
    zT2 = np.concatenate([zc[:4096].T, zc[4096:].T], axis=0)
    negt = (-t[l, 0]).reshape(128, 64)
    return dict(
        S1u=S1u.reshape(128, 64 * 128).astype(f32), S1k=S1k.reshape(128, 64 * 128).astype(f32),
        mats=mats.reshape(128, 18 * 128).astype(f32), Lm=Lm.reshape(128, 64 * NM1).astype(f32),
        zT=np.ascontiguousarray(zT2).astype(f32), negt=negt.astype(f32),
        ident=np.eye(128, dtype=f32),
        msk=np.tile(np.array([[1.0 - h, float(h), -float(h)]], dtype=f32), (128, 1)),
    )


class Buf:
    __slots__ = ("name", "w", "r")

    def __init__(self, name):
        self.name = name
        self.w = []
        self.r = []


class Eng:
    def __init__(self, name, e, sem):
        self.name = name
        self.e = e
        self.sem = sem
        self.key = "E" + name
        self.count = 0
        self.waited = {}
        self.slots = []
        self.rr = 0

    def cur(self):
        return (self.key, self.sem, self.count)


class K:
    def __init__(self, nc, es):
        self.nc = nc
        self.es = es
        self.eng = {}
        for name, e in (("pe", nc.tensor), ("act", nc.scalar), ("dve", nc.vector),
                        ("pool", nc.gpsimd), ("sp", nc.sync)):
            self.eng[name] = Eng(name, e, es.enter_context(nc.semaphore("s_" + name)))
        for qn, nslots in (("sp", 24), ("pool", 24)):
            q = self.eng[qn]
            for i in range(nslots):
                q.slots.append([es.enter_context(nc.semaphore("d_%s%d" % (qn, i))), 0, "D%s%d" % (qn, i)])
        self.nbuf = 0

    def buf(self, name=None):
        self.nbuf += 1
        return Buf(name or ("b%d" % self.nbuf))

    def bufs(self, n, name="b"):
        return [self.buf("%s%d" % (name, i)) for i in range(n)]

    def _wait(self, en, ev):
        if ev[2] <= 0:
            return
        if en.name == "pe" and ev[0] == en.key:
            return
        if en.waited.get(ev[0], 0) >= ev[2]:
            return
        en.e.wait_ge(ev[1], ev[2])
        en.waited[ev[0]] = ev[2]

    def _deps(self, en, r, w, acc=False):
        for b in r:
            for ev in b.w:
                self._wait(en, ev)
        for b in w:
            if not acc:
                for ev in b.w:
                    self._wait(en, ev)
            for ev in b.r:
                self._wait(en, ev)

    @staticmethod
    def _addr(lst, ev):
        for i, o in enumerate(lst):
            if o[0] == ev[0]:
                if o[2] < ev[2]:
                    lst[i] = ev
                return
        lst.append(ev)

    def _commit(self, ev, r, w, acc=False):
        for b in r:
            self._addr(b.r, ev)
        for b in w:
            if acc:
                self._addr(b.w, ev)
            else:
                b.w = [ev]
                b.r = []

    def op(self, engname, fn, r=(), w=()):
        en = self.eng[engname]
        self._deps(en, r, w)
        inst = fn(en.e)
        en.count += 1
        inst.then_inc(en.sem, 1)
        self._commit(en.cur(), r, w)

    def dma(self, out, in_, r=(), w=(), q=None, acc=False):
        to_dram = "DRam" in type(out.tensor).__name__
        q = "pool" if (q == "cast" or to_dram) else "sp"
        acc = acc or to_dram
        en = self.eng[q]
        self._deps(en, r, w, acc)
        slot = en.slots[en.rr]
        en.rr = (en.rr + 1) % len(en.slots)
        self._wait(en, (slot[2], slot[0], slot[1]))
        en.e.dma_start(out=out, in_=in_).then_inc(slot[0], 16)
        slot[1] += 16
        ev = (slot[2], slot[0], slot[1])
        self._commit(ev, r, w, acc)
        return ev

    def all_events(self):
        evs = [e.cur() for e in self.eng.values()]
        for e in self.eng.values():
            for s in e.slots:
                evs.append((s[2], s[0], s[1]))
        return evs

    def barrier(self, engs=("pe", "act", "dve", "pool", "sp")):
        evs = self.all_events()
        for n in engs:
            for ev in evs:
                if ev[0] != self.eng[n].key:
                    self._wait(self.eng[n], ev)


class Ring:
    def __init__(self, items):
        self.items = items
        self.i = 0

    def next(self):
        it = self.items[self.i]
        self.i = (self.i + 1) % len(self.items)
        return it


def pipe(n, make):
    its = [make(i) for i in range(n)]
    ns = max(len(x) for x in its)
    for step in range(n + ns - 1):
        for st in range(ns - 1, -1, -1):
            i = step - st
            if 0 <= i < n and st < len(its[i]) and its[i][st] is not None:
                its[i][st]()


DEBUG_OUT = set()


def build_program():
    nc = bass.Bass("TRN2", target_bir_lowering=False)
    es = contextlib.ExitStack()
    k = K(nc, es)

    def din(name, shape, dt=F32):
        return nc.dram_tensor(name, list(shape), dt, kind="ExternalInput").ap()

    def dscr(name, shape, dt=BF16):
        return nc.dram_tensor(name, list(shape), dt, kind=("ExternalOutput" if name in DEBUG_OUT else "Internal")).ap()

    x_in = din("x", [SEQ, D])
    w_in = din("w_in", [D, 7168])
    w_ph = din("w_ph", [D, D])
    w_ps = din("w_ps", [D, D])
    w_o = din("w_o", [D, D])
    w_up = din("w_up", [D, 2 * FFN])
    w_dn = din("w_dn", [FFN, D])
    g1T_d = din("g1T", [128, 8])
    g2T_d = din("g2T", [128, 8])
    hcw_d = din("hcw", [128, 24 * 4])
    fcw_d = din("fcw", [128, NFT * 4])
    gfin_d = din("gfin", [1, D])
    gsgu_d = din("gsgu", [1, D])
    wsT_d = din("wsT", [128, 8 * 128])
    sgub_d = din("sgub", [1, 8 * 128])
    fw1_d = din("fw1", [66, 128])
    fw2_d = din("fw2", [128, 128])
    fw3_d = din("fw3", [128, 128])
    fvec_d = din("fvec", [128, 4])
    fw4_d = din("fw4", [128, D])
    dec_d = din("dec", [2, D])
    skip_d = din("skip", [1, D])
    c_S1u = din("c_S1u", [128, 64 * 128])
    c_S1k = din("c_S1k", [128, 64 * 128])
    c_mats = din("c_mats", [128, 18 * 128])
    c_Lm = din("c_Lm", [128, 64 * NM1])
    c_zT = din("c_zT", [66, 4096])
    c_negt = din("c_negt", [128, 64])
    c_ident = din("c_ident", [128, 128])
    c_msk = din("c_msk", [128, 3])
    out_d = nc.dram_tensor("out", [NTOK, D], F32, kind="ExternalOutput").ap()

    SAk = dscr("SAk", [128, 64, D])
    SU = dscr("SU", [8, 128, SEQ])
    SX0 = dscr("SX0", [8, 128, NTOK])
    SG = dscr("SG", [16, 128, NTOK])
    SUS = dscr("SUS", [8, 128, NTOK])
    SVS = dscr("SVS", [NCH, 128, D])
    SA = dscr("SA", [128, 64, D])
    SC = dscr("SC", [64, 128, D])
    SYA = dscr("SYA", [8, 128, NTOK])
    SYB = dscr("SYB", [8, 128, NTOK])
    SM = dscr("SM", [8, 128, NTOK])
    SX1 = dscr("SX1", [NCH, 128, D], F32)
    SH2 = dscr("SH2", [8, 128, NTOK])
    SACT = dscr("SACT", [NFT, 128, NTOK])
    dSAk, dSK, dSU, dSX0, dSG, dSUS, dSVS, dSA, dSC, dSYA, dSYB, dSM, dSX1, dSH2, dSACT = k.bufs(15, "dram")

    def sb(name, shape, dt=F32, stack=es):
        return stack.enter_context(nc.sbuf_tensor("sb_" + name, list(shape), dt))

    def ring(name, n, shape, dt, stack):
        return Ring([(sb("%s%d" % (name, i), shape, dt, stack), k.buf()) for i in range(n)])

    pst = [es.enter_context(nc.psum_tensor("ps%d" % i, [128, 1024], F32)) for i in range(4)]
    PS = [(pst[i], k.buf("ps%d" % i)) for i in range(4)]

    def psring(idx):
        return Ring([PS[i] for i in idx])

    PSHB = k.bufs(8, "psh")

    def hring(idx):
        return Ring([(pst[i // 2][:, (i % 2) * 512:(i % 2 + 1) * 512], PSHB[i]) for i in idx])

    def ptring(idx):
        return Ring([(pst[i][:].bitcast(BF16)[:, hh * 1024:(hh + 1) * 1024], PS[i][1]) for i in idx for hh in range(1)])

    ident = sb("ident", [128, 128], BF16)
    mats = sb("mats", [128, 18, 128], BF16)
    Lm = sb("Lm", [128, 64, NM1], BF16)
    g1T = sb("g1T", [128, 8])
    g2T = sb("g2T", [128, 8])
    hcw = sb("hcw", [128, 24, 4])
    hcm = sb("hcm", [128, 24, 4])
    fcw = sb("fcw", [128, NFT, 4])
    msk = sb("msk", [128, 3])
    bC = k.buf("consts")
    k.dma(ident[:], c_ident[:, :], w=[bC], q="cast")
    k.dma(mats[:].rearrange("p a b -> p (a b)"), c_mats[:, :], w=[bC], q="cast", acc=True)
    k.dma(Lm[:].rearrange("p a b -> p (a b)"), c_Lm[:, :], w=[bC], q="cast", acc=True)
    for t_, d_ in ((g1T, g1T_d), (g2T, g2T_d), (msk, c_msk)):
        k.dma(t_[:], d_[:, :], w=[bC], acc=True)
    k.dma(hcw[:].rearrange("p a b -> p (a b)"), hcw_d[:, :], w=[bC], acc=True)
    k.dma(fcw[:].rearrange("p a b -> p (a b)"), fcw_d[:, :], w=[bC], acc=True)
    bC2 = k.buf("consts2")
    for i_, (wi, mi) in enumerate(((0, 2), (2, 2), (0, 1), (2, 1))):
        k.op("dve", lambda e, i_=i_, wi=wi, mi=mi: e.tensor_scalar(
            out=hcm[:, :, i_:i_ + 1], in0=hcw[:, :, wi:wi + 1], scalar1=msk[:, mi:mi + 1], scalar2=None,
            op0=ALU.mult), r=[bC], w=[bC2])

    def rstd_ops(ss, tmp, rs, bss, brs, n):
        k.op("dve", lambda e: e.tensor_scalar(out=tmp[:], in0=ss[:], scalar1=1.0 / n, scalar2=EPS,
                                               op0=ALU.mult, op1=ALU.add), r=[bss], w=[brs])
        k.op("act", lambda e: e.activation(out=tmp[:], in_=tmp[:], func=AF.Sqrt), r=[brs], w=[brs])
        k.op("dve", lambda e: e.reciprocal(out=rs[:], in_=tmp[:]), r=[brs], w=[brs])

    def mac(acc_ap, in_ap, sc_ap, bacc, bin_):
        k.op("dve", lambda e: e.scalar_tensor_tensor(out=acc_ap, in0=in_ap, scalar=sc_ap, in1=acc_ap,
                                                      op0=ALU.mult, op1=ALU.add), r=[bin_, bacc, bC, bC2], w=[bacc])

    def conv_taps(xc, praw, bxc, bpr, cw, ci, segs):
        for (a, b) in segs:
            mac(xc[:, a + 1:b], praw[:, a:b - 1], cw[:, ci, 0:1], bxc, bpr)
            mac(xc[:, a:b - 1], praw[:, a + 1:b], cw[:, ci, 2:3], bxc, bpr)

    def ssq(junk, src_ap, ss, bsrc, bss):
        k.op("act", lambda e: e.activation(out=junk[0][:], in_=src_ap, func=AF.Square, accum_out=ss[:]),
             r=[bsrc], w=[junk[1], bss])

    with contextlib.ExitStack() as ph:
        zT = sb("zT", [66, 4096], F32, ph)
        fw1 = sb("fw1", [66, 128], F32, ph)
        fw2 = sb("fw2", [128, 128], F32, ph)
        fw3 = sb("fw3", [128, 128], F32, ph)
        fvec = sb("fvec", [128, 4], F32, ph)
        fw4f = sb("fw4f", [128, D], F32, ph)
        fw4 = sb("fw4", [128, D], BF16, ph)
        decb = sb("decb", [128, D], F32, ph)
        skipr = sb("skipr", [1, D], F32, ph)
        negt = sb("negt", [128, 64], F32, ph)
        S1k = sb("S1k", [128, 64, 128], BF16, ph)
        h3 = sb("h3", [128, 8192], BF16, ph)
        bF = k.buf("fconst")
        k.dma(zT[:], c_zT[:, :], w=[bF])
        k.dma(fw1[:], fw1_d[:, :], w=[bF], acc=True)
        k.dma(fw2[:], fw2_d[:, :], w=[bF], acc=True)
        k.dma(fw3[:], fw3_d[:, :], w=[bF], acc=True)
        k.dma(fvec[:], fvec_d[:, :], w=[bF], acc=True)
        k.dma(fw4f[:], fw4_d[:, :], w=[bF], acc=True)
        k.dma(decb[0:64, :], dec_d[0, :].partition_broadcast(64), w=[bF], acc=True)
        k.dma(decb[64:128, :], dec_d[1, :].partition_broadcast(64), w=[bF], acc=True)
        k.dma(skipr[:], skip_d[:, :], w=[bF], acc=True)
        k.dma(negt[:], c_negt[:, :], w=[bF], acc=True)
        k.dma(S1k[:].rearrange("p a b -> p (a b)"), c_S1k[:, :], w=[bF], q="cast", acc=True)
        bF2 = k.buf("fconst2")
        bh3 = k.buf("h3")
        k.op("dve", lambda e: e.tensor_copy(out=fw4[:], in_=fw4f[:]), r=[bF], w=[bF2])
        k.op("act", lambda e: e.activation(out=decb[:], in_=decb[:], func=AF.Abs), r=[bF], w=[bF])
        k.op("pool", lambda e: e.memset(h3[:], 0.0), w=[bh3])
        AR = ring("farg", 3, [128, 1024], F32, ph)
        RR = ring("frr", 3, [128, 1024], F32, ph)
        HS = [ring("fh0_", 2, [128, 1024], F32, ph), ring("fh1_", 2, [128, 1024], F32, ph)]
        PL = [psring([0]), psring([1]), psring([2])]

        def ffn_chunk(c):
            cs = slice(c * 1024, (c + 1) * 1024)
            stages = []
            prev = [None]
            for layer, (wmat, kk) in enumerate(((fw1, 66), (fw2, 128), (fw3, 128))):
                ps, pb = PL[layer].next()
                a_t, a_b = AR.next()
                r_t, r_b = RR.next()
                h_t, h_b = HS[layer].next() if layer < 2 else (None, None)

                def mm(layer=layer, wmat=wmat, ps=ps, pb=pb, src=prev[0]):
                    for hh in range(2):
                        hs = slice(hh * 512, (hh + 1) * 512)
                        if layer == 0:
                            rhs, rb = zT[:, c * 1024 + hh * 512:c * 1024 + (hh + 1) * 512], bF
                        else:
                            rhs, rb = src[0][:, hs], src[1]
                        k.op("pe", lambda e: e.matmul(ps[:, hs], wmat[:, :], rhs, start=True, stop=True), r=[bF, rb], w=[pb])

                def red(layer=layer, ps=ps, pb=pb, a_t=a_t, a_b=a_b, r_t=r_t, r_b=r_b):
                    k.op("dve", lambda e: e.tensor_scalar(out=a_t[:], in0=ps[:], scalar1=fvec[:, layer:layer + 1],
                                                           scalar2=fvec[:, 3:4], op0=ALU.add, op1=ALU.mult), r=[pb, bF], w=[a_b])
                    k.op("dve", lambda e: e.tensor_scalar(out=r_t[:], in0=a_t[:], scalar1=1.0 / TWO_PI, scalar2=MAGIC,
                                                           op0=ALU.mult, op1=ALU.add), r=[a_b], w=[r_b])
                    k.op("dve", lambda e: e.tensor_scalar(out=r_t[:], in0=r_t[:], scalar1=MAGIC, scalar2=TWO_PI,
                                                           op0=ALU.subtract, op1=ALU.mult), r=[r_b], w=[r_b])
                    k.op("dve", lambda e: e.tensor_tensor(out=a_t[:], in0=a_t[:], in1=r_t[:], op=ALU.subtract),
                         r=[r_b, a_b], w=[a_b])
                    k.op("dve", lambda e: e.tensor_scalar(out=a_t[:], in0=a_t[:], scalar1=-3.1415925, scalar2=3.1415925,
                                                           op0=ALU.max, op1=ALU.min), r=[a_b], w=[a_b])

                def sn(layer=layer, a_t=a_t, a_b=a_b, h_t=h_t, h_b=h_b):
                    if layer < 2:
                        k.op("act", lambda e: e.activation(out=h_t[:], in_=a_t[:], func=AF.Sin), r=[a_b], w=[h_b])
                    else:
                        k.op("act", lambda e: e.activation(out=h3[0:64, cs], in_=a_t[0:64, :], func=AF.Sin), r=[a_b], w=[bh3])
                        k.op("act", lambda e: e.activation(out=h3[64:128, 4096 + c * 1024:4096 + (c + 1) * 1024],
                                                            in_=a_t[64:128, :], func=AF.Sin), r=[a_b], w=[bh3])
                stages += [mm, red, sn]
                prev[0] = (h_t, h_b)
            return stages
        pipe(4, ffn_chunk)
        k.op("pool", lambda e: e.memset(h3[64:128, 4096:4097], 0.0), w=[bh3])
        if "DBG_h3" in DEBUG_OUT:
            dbg_h3 = nc.dram_tensor("DBG_h3", [128, 8192], BF16, kind="ExternalOutput").ap()
            k.dma(dbg_h3[:, :], h3[:], r=[bh3])
        h3v = h3[:].rearrange("p (a b) -> p a b", b=64)
        Es = ring("fE", 2, [128, D], F32, ph)
        KT = ring("fkT", 3, [128, D], BF16, ph)
        KF = (sb("fkF", [128, D], F32, ph), k.buf())
        AK = ring("fAk", 3, [128, D], BF16, ph)
        PA_, PB_ = psring([0, 1]), psring([2, 3])

        def st1(n2):
            ps, pb = PA_.next()
            ps2, pb2 = PB_.next()
            E_t, E_b = Es.next()
            kt, ktb = KT.next()
            ak, akb = AK.next()

            def s0():
                for cc in range(2):
                    k.op("pe", lambda e, cc=cc: e.matmul(ps[:, cc * 512:(cc + 1) * 512], h3v[:, :, n2],
                                                          fw4[:, cc * 512:(cc + 1) * 512], start=True, stop=True),
                         r=[bh3, bF2], w=[pb])
                k.op("act", lambda e: e.activation(out=E_t[:], in_=decb[:], func=AF.Exp, scale=negt[:, n2:n2 + 1]),
                     r=[bF], w=[E_b])

            def s1():
                if n2 == 0:
                    kf, kfb = KF
                    k.op("dve", lambda e: e.tensor_tensor(out=kf[:], in0=ps[:], in1=E_t[:], op=ALU.mult), r=[pb, E_b], w=[kfb])
                    k.op("dve", lambda e: e.tensor_tensor(out=kf[0:1, :], in0=kf[0:1, :], in1=skipr[0:1, :], op=ALU.add),
                         r=[kfb, bF], w=[kfb])
                    k.op("dve", lambda e: e.tensor_copy(out=kt[:], in_=kf[:]), r=[kfb], w=[ktb])
                else:
                    k.op("dve", lambda e: e.tensor_tensor(out=kt[:], in0=ps[:], in1=E_t[:], op=ALU.mult), r=[pb, E_b], w=[ktb])

            def s2():
                for cc in range(2):
                    k.op("pe", lambda e, cc=cc: e.matmul(ps2[:, cc * 512:(cc + 1) * 512], S1k[:, n2, :],
                                                          kt[:, cc * 512:(cc + 1) * 512], start=True, stop=True),
                         r=[ktb, bF], w=[pb2])

            def s3():
                if n2 % 2:
                    k.op("act", lambda e: e.copy(out=ak[:], in_=ps2[:]), r=[pb2], w=[akb])
                else:
                    k.op("dve", lambda e: e.tensor_copy(out=ak[:], in_=ps2[:]), r=[pb2], w=[akb])

            def s4():
                k.dma(SAk[:, n2, :], ak[:], r=[akb], w=[dSAk])
            return [s0, s1, s2, s3, s4]
        pipe(64, st1)
        k.barrier()

    with contextlib.ExitStack() as ph:
        hT = sb("hT", [128, 8, SEQ], BF16, ph)
        bhT = k.bufs(32, "hT")
        junk = (sb("junk", [128, D], BF16, ph), k.buf())
        SSs = Ring([(sb("ss%d" % i, [128, 1], F32, ph), sb("sst%d" % i, [128, 1], F32, ph),
                     sb("srs%d" % i, [128, 1], F32, ph), k.buf(), k.buf()) for i in range(4)])
        WF = ring("wf", 3, [128, 8, 128], F32, ph)
        WB = ring("wb", 3, [128, 8, 128], BF16, ph)
        phA = contextlib.ExitStack()
        XT = ring("xt", 4, [128, D], F32, phA)
        XS = ring("xs", 3, [128, D], BF16, phA)
        g1b = g1T[:, :].unsqueeze(2).to_broadcast([128, 8, 128])
        PT = ptring([2, 3])

        def pa(i):
            xt, xtb = XT.next()
            ss, sst, srs, bss, brs = SSs.next()
            xs, xsb = XS.next()
            pt, ptb = PT.next()

            def s0():
                k.dma(xt[:], x_in[i * 128:(i + 1) * 128, :], w=[xtb])

            def s1():
                ssq(junk, xt[:], ss, xtb, bss)

            def s2():
                rstd_ops(ss, sst, srs, bss, brs, D)

            def s3():
                if i % 2:
                    k.op("act", lambda e: e.activation(out=xs[:], in_=xt[:], func=AF.Copy, scale=srs[:, 0:1]),
                         r=[xtb, brs], w=[xsb])
                else:
                    k.op("dve", lambda e: e.tensor_scalar(out=xs[:], in0=xt[:], scalar1=srs[:, 0:1], scalar2=None, op0=ALU.mult),
                         r=[xtb, brs], w=[xsb])

            def s4():
                for kc in range(8):
                    k.op("pe", lambda e, kc=kc: e.transpose(pt[:, kc * 128:(kc + 1) * 128], xs[:, kc * 128:(kc + 1) * 128], ident[:]),
                         r=[xsb, bC], w=[ptb])

            def s5():
                k.op("dve", lambda e: e.tensor_tensor(out=hT[:, :, i * 128:(i + 1) * 128],
                                                       in0=pt.rearrange("p (a b) -> p a b", b=128), in1=g1b, op=ALU.mult),
                     r=[ptb, bC], w=[bhT[i]])
            return [s0, s1, s2, s3, s4, s5]
        pipe(32, pa)
        k.barrier()
        phA.close()
        w_in_v = w_in.rearrange("(kc p) c -> p kc c", p=128)
        PSM = psring([0, 1, 2, 3])

        def proj_fm(wb, wbb, t0, tn, ps, pb, half):
            deps = [bhT[j] for j in range(t0 // 128, (t0 + tn + 127) // 128)]
            for kc in range(8):
                k.op("pe", lambda e, kc=kc: e.matmul(ps[:, half * 512:half * 512 + tn], wb[:, kc, :], hT[:, kc, t0:t0 + tn],
                                                      start=(kc == 0), stop=(kc == 7)),
                     r=[wbb] + deps, w=[pb])

        EXT3 = [([(0, 512), (512, 512)], slice(0, 1024), slice(0, 1024)),
                ([(1024, 512), (1536, 512)], slice(1024, 2048), slice(0, 1024)),
                ([(2048, 128)], slice(2048, NTOK), slice(0, 128))]

        phB = contextlib.ExitStack()
        PR = [(sb("praw%d" % i, [128, SEQ], F32, phB), k.buf()) for i in range(2)]
        XC = [(sb("xc%d" % i, [128, SEQ], F32, phB), k.buf()) for i in range(3)]
        UC = ring("uc", 2, [128, SEQ], BF16, phB)
        X0 = ring("x0o", 2, [128, NTOK], BF16, phB)
        GO = ring("go", 2, [128, NTOK], BF16, phB)

        def pb1(it):
            ct, which = it // 2, it % 2
            col0 = (1024 if which == 0 else 2048) + ct * 128
            ci = (8 if which == 0 else 16) + ct
            wf, wfb = WF.next()
            wb, wbb = WB.next()
            praw, bpr = PR[which]
            x1slot = 0 if ct % 2 == 0 else 2
            xc, bxc = XC[1] if which == 1 else XC[x1slot]
            uc, ucb = UC.next() if which == 1 else (None, None)

            def s0():
                k.dma(wf[:], w_in_v[:, :, col0:col0 + 128], w=[wfb])

            def s1():
                k.op("dve", lambda e: e.tensor_copy(out=wb[:], in_=wf[:]), r=[wfb], w=[wbb])

            def s2():
                for tt in range(4):
                    ps, pb = PSM.next()
                    for half in range(2):
                        proj_fm(wb, wbb, tt * 1024 + half * 512, 512, ps, pb, half)
                    sl = slice(tt * 1024, (tt + 1) * 1024)
                    k.op("act", lambda e: e.copy(out=praw[:, sl], in_=ps[:]), r=[pb], w=[bpr])
                    k.op("act", lambda e: e.activation(out=xc[:, sl], in_=ps[:], func=AF.Identity,
                                                        scale=hcw[:, ci, 1:2], bias=hcw[:, ci, 3:4]), r=[pb, bC], w=[bxc])

            def s3():
                conv_taps(xc, praw, bxc, bpr, hcw, ci, [(0, SEQ)])
                for (dst, src, mi) in ((NTOK, NTOK - 1, 0), (NTOK - 1, NTOK, 1), (0, SEQ - 1, 2), (SEQ - 1, 0, 3)):
                    mac(xc[:, dst:dst + 1], praw[:, src:src + 1], hcm[:, ci, mi:mi + 1], bxc, bpr)
                if which == 1:
                    k.op("dve", lambda e: e.tensor_tensor(out=uc[:], in0=XC[x1slot][0][:], in1=XC[1][0][:], op=ALU.mult),
                         r=[XC[x1slot][1], XC[1][1]], w=[ucb])

            def s4():
                if which == 1:
                    k.dma(SU[ct, :, :], uc[:], r=[ucb], w=[dSU])
            return [s0, s1, s2, s3, s4]
        pipe(16, pb1)

        def pb2(ct):
            wf, wfb = WF.next()
            wb, wbb = WB.next()
            praw, bpr = PR[ct % 2]
            xc, bxc = XC[ct % 2]
            xo, xob = X0.next()

            def s0():
                k.dma(wf[:], w_in_v[:, :, ct * 128:(ct + 1) * 128], w=[wfb])

            def s1():
                k.op("dve", lambda e: e.tensor_copy(out=wb[:], in_=wf[:]), r=[wfb], w=[wbb])

            def s2():
                for (halves, sl, psl) in EXT3:
                    ps, pb = PSM.next()
                    for hi, (t0, tn) in enumerate(halves):
                        proj_fm(wb, wbb, t0, tn, ps, pb, hi)
                    k.op("act", lambda e: e.copy(out=praw[:, sl], in_=ps[:, psl]), r=[pb], w=[bpr])
                    k.op("act", lambda e: e.activation(out=xc[:, sl], in_=ps[:, psl], func=AF.Identity,
                                                        scale=hcw[:, ct, 1:2], bias=hcw[:, ct, 3:4]), r=[pb, bC], w=[bxc])

            def s3():
                conv_taps(xc, praw, bxc, bpr, hcw, ct, [(0, NTOK)])
                k.op("pool", lambda e: e.tensor_copy(out=xo[:], in_=xc[:, 0:NTOK]), r=[bxc], w=[xob])

            def s4():
                k.dma(SX0[ct, :, :], xo[:], r=[xob], w=[dSX0])
            return [s0, s1, s2, s3, s4]
        pipe(8, pb2)

        def pb3(mt):
            if mt < 16:
                col0, func, dst, dbuf = 5120 + mt * 128, AF.Sigmoid, SG[mt, :, :], dSG
            else:
                col0, func, dst, dbuf = 3072 + (mt - 16) * 128, AF.Gelu, SUS[mt - 16, :, :], dSUS
            wf, wfb = WF.next()
            wb, wbb = WB.next()
            go, gob = GO.next()

            def s0():
                k.dma(wf[:], w_in_v[:, :, col0:col0 + 128], w=[wfb])

            def s1():
                k.op("dve", lambda e: e.tensor_copy(out=wb[:], in_=wf[:]), r=[wfb], w=[wbb])

            def s2():
                for (halves, sl, psl) in EXT3:
                    ps, pb = PSM.next()
                    for hi, (t0, tn) in enumerate(halves):
                        proj_fm(wb, wbb, t0, tn, ps, pb, hi)
                    k.op("act", lambda e: e.activation(out=go[:, sl], in_=ps[:, psl], func=func), r=[pb], w=[gob])

            def s3():
                k.dma(dst, go[:], r=[gob], w=[dbuf])
            return [s0, s1, s2, s3]
        pipe(24, pb3)
        k.barrier()
        phB.close()

        WV = ring("wvf", 2, [128, 8, 512], F32, ph)
        WVb = (sb("wvb", [128, 8, D], BF16, ph), k.buf())
        for hh in range(2):
            wv, wvb_ = WV.next()
            k.dma(wv[:], w_in_v[:, :, 4096 + hh * 512:4096 + (hh + 1) * 512], w=[wvb_])
            k.op("dve", lambda e, hh=hh: e.tensor_copy(out=WVb[0][:, :, hh * 512:(hh + 1) * 512], in_=wv[:]),
                 r=[wvb_], w=[WVb[1]])
        gsg = (sb("gsg", [128, D], F32, ph), k.buf())
        k.dma(gsg[0][:], gsgu_d[0, :].partition_broadcast(128), w=[gsg[1]])
        GV = ring("gv", 4, [128, D], F32, ph)
        VO = ring("vo", 3, [128, D], BF16, ph)
        PV = psring([0, 1])

        def pb5(tchunk):
            ps, pb = PV.next()
            gv, gvb = GV.next()
            ss, sst, srs, bss, brs = SSs.next()
            vo, vob = VO.next()

            def s0():
                for hh in range(2):
                    for kc in range(8):
                        k.op("pe", lambda e, kc=kc, hh=hh: e.matmul(ps[:, hh * 512:(hh + 1) * 512],
                                                                      hT[:, kc, tchunk * 128:(tchunk + 1) * 128],
                                                                      WVb[0][:, kc, hh * 512:(hh + 1) * 512],
                                                                      start=(kc == 0), stop=(kc == 7)),
                             r=[WVb[1], bhT[tchunk]], w=[pb])

            def s1():
                k.op("act", lambda e: e.activation(out=gv[:], in_=ps[:], func=AF.Gelu), r=[pb], w=[gvb])

            def s2():
                ssq(junk, gv[:], ss, gvb, bss)

            def s3():
                rstd_ops(ss, sst, srs, bss, brs, D)

            def s4():
                k.op("dve", lambda e: e.scalar_tensor_tensor(out=vo[:], in0=gv[:], scalar=srs[:, 0:1], in1=gsg[0][:],
                                                              op0=ALU.mult, op1=ALU.mult), r=[gvb, brs, gsg[1]], w=[vob])

            def s5():
                k.dma(SVS[tchunk, :, :], vo[:], r=[vob], w=[dSVS])
            return [s0, s1, s2, s3, s4, s5]
        pipe(NCH, pb5)
        k.barrier()

    mid = contextlib.ExitStack()
    yaa = sb("yaa", [128, 8, NTOK], BF16, mid)
    ybT = sb("ybT", [128, 8, NTOK], BF16, mid)
    bya, byb = k.bufs(8, "yaa"), k.bufs(8, "ybT")
    with contextlib.ExitStack() as ph:
        S1u = sb("S1u", [128, 64, 128], BF16, ph)
        bS = k.buf("S1u")
        k.dma(S1u[:].rearrange("p a b -> p (a b)"), c_S1u[:, :], w=[bS], q="cast")
        with contextlib.ExitStack() as ph2:
            uall = sb("uall", [128, 8, SEQ], BF16, ph2)
            bU = k.buf("uall")
            for ct in range(8):
                k.dma(uall[:, ct, :], SU[ct, :, :], r=[dSU], w=[bU], acc=(ct > 0))
            uv2 = uall[:].rearrange("p c (m b) -> p c m b", b=32)
            UT = ring("uT", 3, [128, D], BF16, ph2)
            AO = ring("Ao", 4, [128, D], BF16, ph2)
            PT = ptring([2, 3])
            PSA, PSB2 = psring([0]), psring([1])

            def pc1(pi):
                pt, ptb = PT.next()
                ut, utb = UT.next()
                psa, pab = PSA.next()
                psb, pbb = PSB2.next()
                aoa, aoab = AO.next()
                aob_, aobb = AO.next()
                n2a, n2b = pi, pi + 32

                def s0():
                    for ct in range(8):
                        k.op("pe", lambda e, ct=ct: e.transpose(pt[:, ct * 128:(ct + 1) * 128], uv2[:, ct, :, pi], ident[:]),
                             r=[bU, bC], w=[ptb])

                def s1():
                    k.op("dve", lambda e: e.tensor_copy(out=ut[:], in_=pt[:, :]), r=[ptb], w=[utb])

                def s2():
                    for (ps, pb, n2) in ((psa, pab, n2a), (psb, pbb, n2b)):
                        for cc in range(2):
                            k.op("pe", lambda e, cc=cc, ps=ps, n2=n2: e.matmul(
                                ps[:, cc * 512:(cc + 1) * 512], S1u[:, n2, :],
                                ut[:, cc * 512:(cc + 1) * 512], start=True, stop=True), r=[utb, bS], w=[pb])

                def s3():
                    k.op("act", lambda e: e.copy(out=aoa[:], in_=psa[:]), r=[pab], w=[aoab])
                    k.op("dve", lambda e: e.tensor_copy(out=aob_[:], in_=psb[:]), r=[pbb], w=[aobb])

                def s4():
                    k.dma(SA[:, n2a, :], aoa[:], r=[aoab], w=[dSA])
                    k.dma(SA[:, n2b, :], aob_[:], r=[aobb], w=[dSA])
                return [s0, s1, s2, s3, s4]
            pipe(32, pc1)
            k.barrier()
        x0a = sb("x0a", [128, 8, NTOK], BF16, ph)
        bx0 = k.buf("x0a")
        for ct in range(8):
            k.dma(x0a[:, ct, :], SX0[ct, :, :], r=[dSX0], w=[bx0], acc=(ct > 0))
        BT = ring("Bt", 3, [128, D], BF16, ph)
        BKT = ring("Bkt", 3, [128, D], BF16, ph)
        KC = ring("Kc", 3, [128, 512], BF16, ph)
        Q1 = ring("Q1_", 3, [128, D], BF16, ph)
        CO = ring("Co", 3, [128, D], BF16, ph)
        QK, QC = hring([0, 1]), hring([2, 3])
        QZ = Ring([(pst[2], k.buf()), (pst[3], k.buf())])
        tiles = {}

        def get_tiles(t):
            if t not in tiles:
                tiles[t] = (BT.next(), BKT.next(), CO.next())
            return tiles[t]

        def load_t(t):
            (bt, btb), (bkt, bktb), _ = get_tiles(t)
            k.dma(bt[0:64, :], SA[t, :, :], r=[dSA], w=[btb])
            k.dma(bt[64:128, :], SA[64 + t, :, :], r=[dSA], w=[btb], acc=True)
            k.dma(bkt[0:64, :], SAk[t, :, :], r=[dSAk], w=[bktb])
            k.dma(bkt[64:128, :], SAk[64 + t, :, :], r=[dSAk], w=[bktb], acc=True)

        def front(t, cc, ma, mb):
            (bt, btb), (bkt, bktb), _ = get_tiles(t)
            hs = slice(cc * 512, (cc + 1) * 512)
            qk, qkb = QK.next()
            zz, zzb = QZ.next()
            kc, kcb = KC.next()
            qq, qqb = Q1.next()

            def mm():
                k.op("pe", lambda e: e.matmul(qk[:], mats[:, ma, :], bkt[:, hs], start=True, stop=True), r=[bktb, bC], w=[qkb])
                k.op("pe", lambda e: e.matmul(zz[:, 0:512], mats[:, ma, :], bt[:, hs], start=True, stop=True), r=[btb, bC], w=[zzb])
                k.op("pe", lambda e: e.matmul(zz[:, 512:1024], mats[:, mb, :], bt[:, hs], start=True, stop=True), r=[btb, bC], w=[zzb])

            def cp():
                k.op("act", lambda e: e.copy(out=kc[:], in_=qk[:]), r=[qkb], w=[kcb])

            def pr():
                k.op("dve", lambda e: e.tensor_tensor(out=qq[:].rearrange("p (a b) -> p a b", a=2),
                                                       in0=zz[:].rearrange("p (a b) -> p a b", a=2),
                                                       in1=kc[:].unsqueeze(1).to_broadcast([128, 2, 512]), op=ALU.mult),
                     r=[zzb, kcb], w=[qqb])
            return mm, cp, pr, (qq[:, 0:512], qqb), (qq[:, 512:1024], qqb)

        def back(t, cc, plist):
            _, _, (co, cob) = get_tiles(t)
            hs = slice(cc * 512, (cc + 1) * 512)
            qc, qcb = QC.next()

            def imm():
                for pi, (pp, ppb, tm) in enumerate(plist):
                    k.op("pe", lambda e, pp=pp, tm=tm, pi=pi: e.matmul(qc[:], mats[:, tm, :], pp, start=(pi == 0),
                                                                        stop=(pi == len(plist) - 1)), r=[ppb, bC], w=[qcb])

            def cpo():
                k.op("act", lambda e: e.copy(out=co[:, hs], in_=qc[:]), r=[qcb], w=[cob])

            def st():
                if cc == 1:
                    k.dma(SC[t, :, :], co[:], r=[cob], w=[dSC])
            return imm, cpo, st

        load_t(0)
        for cc in range(2):
            plist = []
            for (ma, mb, t1, t2) in ((4, 5, 14, 15), (8, 9, 16, 17)):
                mm, cp, pr, (q1, q1b), (q2, q2b) = front(0, cc, ma, mb)
                mm(); cp(); pr()
                plist += [(q1, q1b, t1), (q2, q2b, t2)]
            imm, cpo, st = back(0, cc, plist)
            imm(); cpo(); st()

        def pc2(it):
            t, cc = 1 + it // 2, it % 2
            mm, cp, pr, (q1, q1b), (q2, q2b) = front(t, cc, 0, 1)
            imm, cpo, st = back(t, cc, [(q1, q1b, 12), (q2, q2b, 13)])

            def s0():
                if cc == 0:
                    load_t(t)
            return [s0, mm, cp, pr, imm, cpo, st]
        pipe(126, pc2)
        k.barrier()
        DT = ring("Dt", 3, [128, 8, D], BF16, ph)
        x0v = x0a[:].rearrange("p c (i m) -> p c i m", m=64)
        yav = yaa[:].rearrange("p c (i m) -> p c i m", m=64)
        PSM = psring([0, 1, 2, 3])

        def pc3(mg):
            dt_, dtb = DT.next()

            def s0():
                for mm in range(8):
                    m2 = mg * 8 + mm
                    k.dma(dt_[0:64, mm, :], SC[:, m2, :], r=[dSC], w=[dtb], acc=(mm > 0))
                    k.dma(dt_[64:128, mm, :], SC[:, 64 + m2, :], r=[dSC], w=[dtb], acc=True)

            def s1():
                for ct in range(8):
                    ps, pb = PSM.next()
                    for mm in range(8):
                        m2 = mg * 8 + mm
                        k.op("pe", lambda e, mm=mm, m2=m2: e.matmul(ps[:, mm * NM1:(mm + 1) * NM1], dt_[:, mm, ct * 128:(ct + 1) * 128],
                                                                     Lm[:, m2, :], start=True, stop=True), r=[dtb, bC], w=[pb])
                    k.op("dve", lambda e: e.tensor_tensor(
                        out=yav[:, ct, :, mg * 8:(mg + 1) * 8],
                        in0=ps[:, 0:8 * NM1].rearrange("p (m i) -> p i m", i=NM1),
                        in1=x0v[:, ct, :, mg * 8:(mg + 1) * 8], op=ALU.mult), r=[pb, bx0], w=[bya[ct]])
            return [s0, s1]
        pipe(8, pc3)
        k.barrier()

    with contextlib.ExitStack() as ph:
        vh = sb("vh", [128, NCH, D], BF16, ph)
        bvh = k.buf("vh")
        for c_ in range(NCH):
            k.dma(vh[:, c_, :], SVS[c_, :, :], r=[dSVS], w=[bvh], acc=(c_ > 0))
        wsf = sb("wsf", [128, 8 * 128], F32, ph)
        wsb = sb("wsb", [128, 8, 128], BF16, ph)
        sbf = sb("sbf", [1, 8 * 128], F32, ph)
        sbb = sb("sbb", [1, 8, 128], BF16, ph)
        ones = sb("ones", [1, 128], BF16, ph)
        bws = k.buf("ws")
        k.dma(wsf[:], wsT_d[:, :], w=[bws])
        k.dma(sbf[:], sgub_d[:, :], w=[bws], acc=True)
        bws2 = k.buf("ws2")
        k.op("dve", lambda e: e.tensor_copy(out=wsb[:].rearrange("p a b -> p (a b)"), in_=wsf[:]), r=[bws], w=[bws2])
        k.op("dve", lambda e: e.tensor_copy(out=sbb[:].rearrange("p a b -> p (a b)"), in_=sbf[:]), r=[bws], w=[bws2])
        k.op("pool", lambda e: e.memset(ones[:], 1.0), w=[bws2])
        USr = ring("usr", 3, [128, NTOK], BF16, ph)
        PSM = psring([0, 1, 2, 3])

        def pd1(g):
            us, usb = USr.next()

            def s0():
                k.dma(us[:], SUS[g, :, :], r=[dSUS], w=[usb])

            def s1():
                for cg in range(5):
                    chunks = list(range(cg * 4, min(cg * 4 + 4, NCH)))
                    ps, pb = PSM.next()
                    for ci_, ch in enumerate(chunks):
                        k.op("pe", lambda e, ci_=ci_, ch=ch: e.matmul(ps[:, ci_ * 128:(ci_ + 1) * 128], vh[:, ch, g * 128:(g + 1) * 128],
                                                                       wsb[:, g, :], start=True, stop=False), r=[bvh, bws2], w=[pb])
                        k.op("pe", lambda e, ci_=ci_: e.matmul(ps[:, ci_ * 128:(ci_ + 1) * 128], ones[0:1, :], sbb[0:1, g, :],
                                                                start=False, stop=True), r=[bws2], w=[pb])
                    n = len(chunks) * 128
                    sl = slice(cg * 512, cg * 512 + n)
                    k.op("dve", lambda e: e.tensor_tensor(out=ybT[:, g, sl], in0=ps[:, 0:n], in1=us[:, sl], op=ALU.mult),
                         r=[pb, usb], w=[byb[g]])
            return [s0, s1]
        pipe(8, pd1)
        k.barrier()

    with contextlib.ExitStack() as ph:
        yaT = yaa
        WF = ring("pwf", 6, [128, 8, 128], F32, ph)
        WB = ring("pwb", 6, [128, 8, 128], BF16, ph)
        GA = ring("ga", 3, [128, NTOK], BF16, ph)
        GB = ring("gb", 3, [128, NTOK], BF16, ph)
        M1 = ring("m1_", 2, [128, 512], F32, ph)
        M2 = ring("m2_", 2, [128, 512], F32, ph)
        MO = ring("mo", 3, [128, NTOK], BF16, ph)
        wphv = w_ph.rearrange("(kc p) c -> p kc c", p=128)
        wpsv = w_ps.rearrange("(kc p) c -> p kc c", p=128)
        PSM = psring([0, 1, 2, 3])

        def pd2(dtile):
            wl = [(WF.next(), WB.next()) for _ in range(2)]
            ga, gab = GA.next()
            gb, gbb = GB.next()
            mo, mob = MO.next()

            def s0():
                for ((wf, wfb), _), wv in zip(wl, (wphv, wpsv)):
                    k.dma(wf[:], wv[:, :, dtile * 128:(dtile + 1) * 128], w=[wfb])
                k.dma(ga[:], SG[dtile, :, :], r=[dSG], w=[gab])
                k.dma(gb[:], SG[8 + dtile, :, :], r=[dSG], w=[gbb])

            def s1():
                for ((wf, wfb), (wb, wbb)) in wl:
                    k.op("dve", lambda e, wf=wf, wb=wb: e.tensor_copy(out=wb[:], in_=wf[:]), r=[wfb], w=[wbb])

            def s2():
                for (t0, tn) in TOKT:
                    psA, pbA = PSM.next()
                    psB, pbB = PSM.next()
                    for (ps, pb, (_, (wb, wbb)), yT, by) in ((psA, pbA, wl[0], yaT, bya), (psB, pbB, wl[1], ybT, byb)):
                        for kc in range(8):
                            k.op("pe", lambda e, kc=kc, ps=ps, wb=wb, yT=yT: e.matmul(ps[:, 0:tn], wb[:, kc, :], yT[:, kc, t0:t0 + tn],
                                                                                         start=(kc == 0), stop=(kc == 7)),
                                 r=[wbb, by[kc]], w=[pb])
                    m1, m1b = M1.next()
                    m2_, m2b = M2.next()
                    k.op("dve", lambda e: e.tensor_tensor(out=m1[:, 0:tn], in0=psA[:, 0:tn], in1=ga[:, t0:t0 + tn], op=ALU.mult),
                         r=[pbA, gab], w=[m1b])
                    k.op("dve", lambda e: e.tensor_tensor(out=m2_[:, 0:tn], in0=psB[:, 0:tn], in1=gb[:, t0:t0 + tn], op=ALU.mult),
                         r=[pbB, gbb], w=[m2b])
                    k.op("pool", lambda e: e.tensor_tensor(out=mo[:, t0:t0 + tn], in0=m1[:, 0:tn], in1=m2_[:, 0:tn], op=ALU.add),
                         r=[m1b, m2b], w=[mob])

            def s3():
                k.dma(SM[dtile, :, :], mo[:], r=[mob], w=[dSM])
            return [s0, s1, s2, s3]
        pipe(8, pd2)
        k.barrier()
    mid.close()

    tail = contextlib.ExitStack()
    wdb = sb("wdb", [128, NFT, D], BF16, tail)
    bwd = k.bufs(NFT, "wdb")
    h2s = contextlib.ExitStack()
    h2T = sb("h2T", [128, 8, NTOK], BF16, h2s)
    with contextlib.ExitStack() as ph:
        mT = sb("mT", [128, 8, NTOK], BF16, ph)
        bmT, bh2 = k.buf(), k.bufs(NCH, "h2")
        bmTg = k.bufs(len(TOKT), "mTg")
        smv = SM.rearrange("m p t -> p m t")
        for gi, (t0, tn) in enumerate(TOKT):
            k.dma(mT[:, :, t0:t0 + tn], smv[:, :, t0:t0 + tn], r=[dSM], w=[bmTg[gi]])
        WOF = ring("wof", 2, [128, 8, 512], F32, ph)
        wob = sb("wob", [128, 8, D], BF16, ph)
        bwob = k.buf()
        wov = w_o.rearrange("(kc p) c -> p kc c", p=128)
        for hh in range(2):
            wof, bwof = WOF.next()
            k.dma(wof[:], wov[:, :, hh * 512:(hh + 1) * 512], w=[bwof])
            k.op("dve", lambda e, hh=hh: e.tensor_copy(out=wob[:, :, hh * 512:(hh + 1) * 512], in_=wof[:]), r=[bwof], w=[bwob])
        XT = ring("xt2_", 3, [128, D], F32, ph)
        X1 = ring("x1_", 4, [128, D], F32, ph)
        XS = ring("xs2_", 3, [128, D], BF16, ph)
        junk = (sb("junk2", [128, D], BF16, ph), k.buf())
        SSs = Ring([(sb("ss2_%d" % i, [128, 1], F32, ph), sb("sst2_%d" % i, [128, 1], F32, ph),
                     sb("srs2_%d" % i, [128, 1], F32, ph), k.buf(), k.buf()) for i in range(4)])
        g2b = g2T[:, :].unsqueeze(2).to_broadcast([128, 8, 128])
        PT = ptring([2, 3])
        PSA = psring([0, 1])

        def pd3(i):
            xt, xtb = XT.next()
            ps, pb = PSA.next()
            x1, x1b = X1.next()
            ss, sst, srs, bss, brs = SSs.next()
            xs, xsb = XS.next()
            pt, ptb = PT.next()

            def s0():
                k.dma(xt[:], x_in[i * 128:(i + 1) * 128, :], w=[xtb])

            def s1():
                for hh in range(2):
                    for kc in range(8):
                        k.op("pe", lambda e, kc=kc, hh=hh: e.matmul(ps[:, hh * 512:(hh + 1) * 512], mT[:, kc, i * 128:(i + 1) * 128],
                                                                      wob[:, kc, hh * 512:(hh + 1) * 512], start=(kc == 0), stop=(kc == 7)),
                             r=[bmTg[min(i // 4, 4)], bwob], w=[pb])

            def s2():
                k.op("dve", lambda e: e.tensor_tensor(out=x1[:], in0=ps[:], in1=xt[:], op=ALU.add), r=[pb, xtb], w=[x1b])

            def s3():
                k.dma(SX1[i, :, :], x1[:], r=[x1b], w=[dSX1])
                ssq(junk, x1[:], ss, x1b, bss)

            def s4():
                rstd_ops(ss, sst, srs, bss, brs, D)

            def s5():
                if i % 2:
                    k.op("act", lambda e: e.activation(out=xs[:], in_=x1[:], func=AF.Copy, scale=srs[:, 0:1]),
                         r=[x1b, brs], w=[xsb])
                else:
                    k.op("dve", lambda e: e.tensor_scalar(out=xs[:], in0=x1[:], scalar1=srs[:, 0:1], scalar2=None, op0=ALU.mult),
                         r=[x1b, brs], w=[xsb])

            def s6():
                for kc in range(8):
                    k.op("pe", lambda e, kc=kc: e.transpose(pt[:, kc * 128:(kc + 1) * 128], xs[:, kc * 128:(kc + 1) * 128], ident[:]),
                         r=[xsb, bC], w=[ptb])

            def s7():
                k.op("dve", lambda e: e.tensor_tensor(out=h2T[:, :, i * 128:(i + 1) * 128],
                                                       in0=pt.rearrange("p (a b) -> p a b", b=128), in1=g2b, op=ALU.mult),
                     r=[ptb, bC], w=[bh2[i]])
            return [s0, s1, s2, s3, s4, s5, s6, s7]
        pipe(NCH, pd3)
        k.barrier()

    with contextlib.ExitStack() as ph:
        bh2 = k.buf()
        WF = ring("uwf", 6, [128, 8, 128], F32, ph)
        WB = ring("uwb", 6, [128, 8, 128], BF16, ph)
        WDF = ring("wdf", 3, [128, D], F32, ph)
        AR_ = ring("araw", 2, [128, NTOK], F32, ph)
        AC_ = ring("acv", 2, [128, NTOK], F32, ph)
        GT_ = ring("gte", 3, [128, NTOK], BF16, ph)
        SL_ = ring("sil", 2, [128, NTOK], BF16, ph)
        AO_ = ring("aco", 3, [128, NTOK], BF16, ph)
        wupv = w_up.rearrange("(kc p) c -> p kc c", p=128)
        PSM = psring([0, 1, 2, 3])

        def pe1(mt):
            wl = [(WF.next(), WB.next()) for _ in range(2)]
            araw, arb = AR_.next()
            acv, acb = AC_.next()
            gte, gtb = GT_.next()
            sil, slb = SL_.next()
            aco, aob = AO_.next()
            wdf, wdfb = WDF.next()

            def s0():
                for ((wf, wfb), _), col0 in zip(wl, (mt * 128, FFN + mt * 128)):
                    k.dma(wf[:], wupv[:, :, col0:col0 + 128], w=[wfb])
                k.dma(wdf[:], w_dn[mt * 128:(mt + 1) * 128, :], w=[wdfb])

            def s1():
                for ((wf, wfb), (wb, wbb)) in wl:
                    k.op("dve", lambda e, wf=wf, wb=wb: e.tensor_copy(out=wb[:], in_=wf[:]), r=[wfb], w=[wbb])
                k.op("act", lambda e: e.copy(out=wdb[:, mt, :], in_=wdf[:]), r=[wdfb], w=[bwd[mt]])

            def s2():
                for (t0, tn) in TOKT:
                    psA, pbA = PSM.next()
                    psG, pbG = PSM.next()
                    for (ps, pb, (_, (wb, wbb))) in ((psA, pbA, wl[0]), (psG, pbG, wl[1])):
                        for kc in range(8):
                            k.op("pe", lambda e, kc=kc, ps=ps, wb=wb: e.matmul(ps[:, 0:tn], wb[:, kc, :], h2T[:, kc, t0:t0 + tn],
                                                                                 start=(kc == 0), stop=(kc == 7)), r=[wbb, bh2], w=[pb])
                    k.op("act", lambda e: e.copy(out=araw[:, t0:t0 + tn], in_=psA[:, 0:tn]), r=[pbA], w=[arb])
                    k.op("act", lambda e: e.activation(out=acv[:, t0:t0 + tn], in_=psA[:, 0:tn], func=AF.Identity,
                                                        scale=fcw[:, mt, 1:2], bias=fcw[:, mt, 3:4]), r=[pbA, bC], w=[acb])
                    k.op("dve", lambda e: e.tensor_copy(out=gte[:, t0:t0 + tn], in_=psG[:, 0:tn]), r=[pbG], w=[gtb])

            def s3():
                mac(acv[:, 1:NTOK], araw[:, 0:NTOK - 1], fcw[:, mt, 0:1], acb, arb)
                mac(acv[:, 0:NTOK - 1], araw[:, 1:NTOK], fcw[:, mt, 2:3], acb, arb)

            def s4():
                k.op("act", lambda e: e.activation(out=sil[:], in_=acv[:], func=AF.Silu), r=[acb], w=[slb])

            def s5():
                k.op("dve", lambda e: e.tensor_tensor(out=aco[:], in0=sil[:], in1=gte[:], op=ALU.mult), r=[slb, gtb], w=[aob])

            def s6():
                k.dma(SACT[mt, :, :], aco[:], r=[aob], w=[dSACT])
            return [s0, s1, s2, s3, s4, s5, s6]
        pipe(NFT, pe1)
        k.barrier()
    h2s.close()

    with contextlib.ExitStack() as ph:
        aT = sb("aT", [128, NFT, NTOK], BF16, ph)
        baT = k.buf()
        baTg = k.bufs(len(TOKT), "aTg")
        sactv = SACT.rearrange("m p t -> p m t")
        for gi, (t0, tn) in enumerate(TOKT):
            for mh in range(2):
                k.dma(aT[:, mh * 11:(mh + 1) * 11, t0:t0 + tn], sactv[:, mh * 11:(mh + 1) * 11, t0:t0 + tn],
                      r=[dSACT], w=[baTg[gi]], acc=(mh > 0))
        gfb = (sb("gfb", [128, D], F32, ph), k.buf())
        k.dma(gfb[0][:], gfin_d[0, :].partition_broadcast(128), w=[gfb[1]])
        X1 = ring("x1b_", 3, [128, D], F32, ph)
        X2 = ring("x2_", 4, [128, D], F32, ph)
        OT = ring("ot", 3, [128, D], F32, ph)
        junk = (sb("junk3", [128, D], BF16, ph), k.buf())
        SSs = Ring([(sb("ss3_%d" % i, [128, 1], F32, ph), sb("sst3_%d" % i, [128, 1], F32, ph),
                     sb("srs3_%d" % i, [128, 1], F32, ph), k.buf(), k.buf()) for i in range(4)])
        PSA = psring([0, 1, 2, 3])

        def pe2(i):
            x1, x1b = X1.next()
            ps, pb = PSA.next()
            x2, x2b = X2.next()
            ss, sst, srs, bss, brs = SSs.next()
            ot, otb = OT.next()

            def s0():
                k.dma(x1[:], SX1[i, :, :], r=[dSX1], w=[x1b])

            def s1():
                for hh in range(2):
                    for mt in range(NFT):
                        k.op("pe", lambda e, mt=mt, hh=hh: e.matmul(ps[:, hh * 512:(hh + 1) * 512], aT[:, mt, i * 128:(i + 1) * 128],
                                                                      wdb[:, mt, hh * 512:(hh + 1) * 512], start=(mt == 0), stop=(mt == NFT - 1)),
                             r=[baTg[min(i // 4, 4)], bwd[mt]], w=[pb])

            def s2():
                k.op("dve", lambda e: e.tensor_tensor(out=x2[:], in0=ps[:], in1=x1[:], op=ALU.add), r=[pb, x1b], w=[x2b])

            def s3():
                ssq(junk, x2[:], ss, x2b, bss)

            def s4():
                rstd_ops(ss, sst, srs, bss, brs, D)

            def s5():
                k.op("dve", lambda e: e.scalar_tensor_tensor(out=ot[:], in0=x2[:], scalar=srs[:, 0:1], in1=gfb[0][:],
                                                              op0=ALU.mult, op1=ALU.mult), r=[x2b, brs, gfb[1]], w=[otb])

            def s6():
                k.dma(out_d[i * 128:(i + 1) * 128, :], ot[:], r=[otb])
            return [s0, s1, s2, s3, s4, s5, s6]
        pipe(NCH, pe2)
        k.barrier()
    tail.close()
    es.close()
    return nc


_PROG = {}


def _prep_common(inp):
    f32 = np.float32
    c = {}
    c["w_in"] = np.ascontiguousarray(inp["w_in"][0], dtype=f32)
    c["w_ph"] = np.ascontiguousarray(inp["w_proj_hyena"][0], dtype=f32)
    c["w_ps"] = np.ascontiguousarray(inp["w_proj_sgu"][0], dtype=f32)
    c["w_o"] = np.ascontiguousarray(inp["w_out"][0], dtype=f32)
    c["w_up"] = np.ascontiguousarray(inp["w_up"][0], dtype=f32)
    c["w_dn"] = np.ascontiguousarray(inp["w_down"][0], dtype=f32)
    c["g1T"] = np.ascontiguousarray(inp["norm1_g"][0].reshape(8, 128).T, dtype=f32)
    c["g2T"] = np.ascontiguousarray(inp["norm2_g"][0].reshape(8, 128).T, dtype=f32)
    hw = np.concatenate([inp["hy_conv_w"][0], inp["hy_conv_b"][0][None, :]], axis=0)
    c["hcw"] = np.ascontiguousarray(hw.reshape(4, 24, 128).transpose(2, 1, 0).reshape(128, 96), dtype=f32)
    fw = np.concatenate([inp["ffn_conv_w"][0], inp["ffn_conv_b"][0][None, :]], axis=0)
    c["fcw"] = np.ascontiguousarray(fw.reshape(4, NFT, 128).transpose(2, 1, 0).reshape(128, NFT * 4), dtype=f32)
    c["gfin"] = np.ascontiguousarray(inp["final_g"].reshape(1, D), dtype=f32)
    c["gsgu"] = np.ascontiguousarray(inp["sgu_norm_g"][0].reshape(1, D), dtype=f32)
    c["wsT"] = np.ascontiguousarray(inp["sgu_w"][0].transpose(2, 0, 1).reshape(128, 8 * 128), dtype=f32)
    c["sgub"] = np.ascontiguousarray(inp["sgu_b"][0].reshape(1, 8 * 128), dtype=f32)
    def bd(w):
        r, c_ = w.shape
        o = np.zeros((2 * r, 2 * c_), dtype=f32)
        o[:r, :c_] = w
        o[r:, c_:] = w
        return o
    c["fw1"] = bd(inp["filt_w1"][0])
    c["fw2"] = bd(inp["filt_w2"][0])
    c["fw3"] = bd(inp["filt_w3"][0])
    fv = np.stack([inp["filt_b1"][0], inp["filt_b2"][0], inp["filt_b3"][0], inp["filt_freq"][0]], axis=1)
    c["fvec"] = np.ascontiguousarray(np.concatenate([fv, fv], axis=0), dtype=f32)
    w4 = inp["filt_w4"][0]
    c["fw4"] = np.ascontiguousarray(np.concatenate([w4[:, :D], w4[:, D:]], axis=0), dtype=f32)
    c["dec"] = np.ascontiguousarray(inp["hy_decay"][0].reshape(2, D), dtype=f32)
    c["skip"] = np.ascontiguousarray(inp["hy_skip"][0].reshape(1, D), dtype=f32)
    return c


def kernel(**inputs):
    inp = {k_: np.asarray(v) for k_, v in inputs.items()}
    if "nc" not in _PROG:
        _PROG["nc"] = build_program()
        _PROG["consts"] = [make_consts(h) for h in (0, 1)]
    nc = _PROG["nc"]
    common = _prep_common(inp)
    x = np.asarray(inp["x"], dtype=np.float32)
    in_maps = []
    for core in range(8):
        b, h = core // 2, core % 2
        m = dict(common)
        m["x"] = np.ascontiguousarray(np.roll(x[b], -1920 * h, axis=0))
        cs = _PROG["consts"][h]
        for nm in ("S1u", "S1k", "mats", "Lm", "zT", "negt", "ident", "msk"):
            m["c_" + nm] = cs[nm]
        in_maps.append(m)
    res = run_bass_kernel_spmd(nc, in_maps, core_ids=list(range(8)))
    out = np.empty((4, SEQ, D), dtype=np.float32)
    for core in range(8):
        b, h = core // 2, core % 2
        o = res.results[core]["out"]
        if h == 0:
            out[b, 0:2048] = o[0:2048]
        else:
            out[b, 2048:4096] = o[128:2176]
    return out
```

```python
import contextlib
import numpy as np
import concourse.bass as bass
import concourse.mybir as mybir
from concourse.bass_utils import run_bass_kernel_spmd

F32 = mybir.dt.float32
BF16 = mybir.dt.bfloat16
AF = mybir.ActivationFunctionType
ALU = mybir.AluOpType

D = 1024
SEQ = 4096
NTOK = 2176
NCH = 17
NM1 = 34
FFN = 2816
NFT = 22
EPS = 1e-6
MAGIC = 12582912.0
TWO_PI = float(2 * np.pi)
TOKT = [(0, 512), (512, 512), (1024, 512), (1536, 512), (2048, 128)]


def _blk(a, b, c, d):
    return np.block([[a, b], [c, d]])


def make_consts(h):
    rot = 30 * h
    q = np.arange(64)[:, None, None]
    n2 = np.arange(64)[None, :, None]

    def s1(n1):
        out = np.zeros((n1.shape[0], 64, 128))
        f1 = np.arange(65)[None, None, :]
        phi = 2 * np.pi * (n1 * f1 / 128.0 + n2 * f1 / 8192.0)
        re = np.cos(phi)
        re[:, :, 64] = np.cos(np.pi * n1[:, :, 0]) * np.ones((1, 64))
        out[:, :, 0:65] = re
        f1i = np.arange(1, 64)[None, None, :]
        phii = 2 * np.pi * (n1 * f1i / 128.0 + n2 * f1i / 8192.0)
        out[:, :, 65:128] = -np.sin(phii)
        return out

    S1u0 = s1((q + rot) % 64)
    S1u = np.zeros((128, 64, 128))
    S1u[0::2, 0:32, :] = S1u0[:, 0:32, :]
    S1u[1::2, 32:64, :] = S1u0[:, 32:64, :]
    S1k = s1(np.arange(128)[:, None, None])
    n2v = np.arange(64)[:, None]
    f2 = np.arange(64)[None, :]
    th = 2 * np.pi * n2v * f2 / 64.0
    Fc, Fs = np.cos(th), np.sin(th)
    thp = th + np.pi * n2v / 64.0
    Fcp, Fsp = np.cos(thp), np.sin(thp)
    Z = np.zeros((64, 64))
    S2 = [
        _blk(Fc, -Fs, Fs, Fc), _blk(-Fs, Fc, Fc, Fs), _blk(Fc, Fc, Fs, Fs), _blk(-Fs, -Fs, Fc, Fc),
        _blk(Fc, -Fs, Z, Z), _blk(-Fs, Fc, Z, Z), _blk(Fc, Fc, Z, Z), _blk(-Fs, -Fs, Z, Z),
        _blk(Z, Z, Fcp, -Fsp), _blk(Z, Z, -Fsp, Fcp), _blk(Z, Z, Fcp, Fcp), _blk(Z, Z, -Fsp, -Fsp),
    ]
    f2v = np.arange(64)[:, None]
    m2 = np.arange(64)[None, :]
    ga = 2 * np.pi * m2 * f2v / 64.0
    Gr, Gi = np.cos(ga), np.sin(ga)
    gp = ga + np.pi * m2 / 64.0
    Gc, Gs = np.cos(gp), np.sin(gp)
    T = [
        _blk(Gr, Gi, -Gr, -Gi), _blk(-Gi, Gr, -Gi, Gr),
        _blk(Gr, Z, -Gr, Z), _blk(-Gi, Z, -Gi, Z),
        _blk(Z, Gc, Z, -Gc), _blk(Z, -Gs, Z, -Gs),
    ]
    mats = np.stack(S2 + T, axis=1)
    m1 = (np.arange(NM1) + 30 * h)[None, None, :]
    f1 = np.arange(64)[:, None, None]
    m2b = np.arange(64)[None, :, None]
    psi = 2 * np.pi * (m1 * f1 / 128.0 + m2b * f1 / 8192.0)
    Lre = 2 * np.cos(psi) / 8192.0
    Lim = -2 * np.sin(psi) / 8192.0
    Lre[0] = 1.0 / 8192.0
    Lim[0] = ((-1.0) ** m1[0]) / 8192.0 * np.ones((64, 1))
    Lm = np.concatenate([Lre, Lim], axis=0)
    f32 = np.float32
    t = np.linspace(0.0, 1.0, SEQ, dtype=f32)[:, None]
    bands = np.linspace(1e-4, 15.0, 16, dtype=f32)[None, :]
    phase = (f32(2.0 * np.pi / SEQ) * np.arange(SEQ, dtype=f32)[:, None]) * bands
    z = np.concatenate([t, np.cos(phase), -np.sin(phase)], axis=-1).astype(f32)
    j = np.arange(8192)
    l = np.where(j < 4096, j, 8192 - j)
    l = np.where(j == 4096, 0, l)
    zc = z[l]
    zT2 = np.concatenate([zc[:4096].T, zc[4096:].T], axis=0)
    negt = (-t[l, 0]).reshape(128, 64)
    return dict(
        S1u=S1u.reshape(128, 64 * 128).astype(f32), S1k=S1k.reshape(128, 64 * 128).astype(f32),
        mats=mats.reshape(128, 18 * 128).astype(f32), Lm=Lm.reshape(128, 64 * NM1).astype(f32),
        zT=np.ascontiguousarray(zT2).astype(f32), negt=negt.astype(f32),
        ident=np.eye(128, dtype=f32),
        msk=np.tile(np.array([[1.0 - h, float(h), -float(h)]], dtype=f32), (128, 1)),
    )


class Buf:
    __slots__ = ("name", "w", "r")

    def __init__(self, name):
        self.name = name
        self.w = []
        self.r = []


class Eng:
    def __init__(self, name, e, sem):
        self.name = name
        self.e = e
        self.sem = sem
        self.key = "E" + name
        self.count = 0
        self.waited = {}
        self.slots = []
        self.rr = 0

    def cur(self):
        return (self.key, self.sem, self.count)


class K:
    def __init__(self, nc, es):
        self.nc = nc
        self.es = es
        self.eng = {}
        for name, e in (("pe", nc.tensor), ("act", nc.scalar), ("dve", nc.vector),
                        ("pool", nc.gpsimd), ("sp", nc.sync)):
            self.eng[name] = Eng(name, e, es.enter_context(nc.semaphore("s_" + name)))
        for qn, nslots in (("sp", 24), ("pool", 24)):
            q = self.eng[qn]
            for i in range(nslots):
                q.slots.append([es.enter_context(nc.semaphore("d_%s%d" % (qn, i))), 0, "D%s%d" % (qn, i)])
        self.nbuf = 0

    def buf(self, name=None):
        self.nbuf += 1
        return Buf(name or ("b%d" % self.nbuf))

    def bufs(self, n, name="b"):
        return [self.buf("%s%d" % (name, i)) for i in range(n)]

    def _wait(self, en, ev):
        if ev[2] <= 0:
            return
        if en.name == "pe" and ev[0] == en.key:
            return
        if en.waited.get(ev[0], 0) >= ev[2]:
            return
        en.e.wait_ge(ev[1], ev[2])
        en.waited[ev[0]] = ev[2]

    def _deps(self, en, r, w, acc=False):
        for b in r:
            for ev in b.w:
                self._wait(en, ev)
        for b in w:
            if not acc:
                for ev in b.w:
                    self._wait(en, ev)
            for ev in b.r:
                self._wait(en, ev)

    @staticmethod
    def _addr(lst, ev):
        for i, o in enumerate(lst):
            if o[0] == ev[0]:
                if o[2] < ev[2]:
                    lst[i] = ev
                return
        lst.append(ev)

    def _commit(self, ev, r, w, acc=False):
        for b in r:
            self._addr(b.r, ev)
        for b in w:
            if acc:
                self._addr(b.w, ev)
            else:
                b.w = [ev]
                b.r = []

    def op(self, engname, fn, r=(), w=()):
        en = self.eng[engname]
        self._deps(en, r, w)
        inst = fn(en.e)
        en.count += 1
        inst.then_inc(en.sem, 1)
        self._commit(en.cur(), r, w)

    def dma(self, out, in_, r=(), w=(), q=None, acc=False):
        to_dram = "DRam" in type(out.tensor).__name__
        q = "pool" if (q == "cast" or to_dram) else "sp"
        acc = acc or to_dram
        en = self.eng[q]
        self._deps(en, r, w, acc)
        slot = en.slots[en.rr]
        en.rr = (en.rr + 1) % len(en.slots)
        self._wait(en, (slot[2], slot[0], slot[1]))
        en.e.dma_start(out=out, in_=in_).then_inc(slot[0], 16)
        slot[1] += 16
        ev = (slot[2], slot[0], slot[1])
        self._commit(ev, r, w, acc)
        return ev

    def all_events(self):
        evs = [e.cur() for e in self.eng.values()]
        for e in self.eng.values():
            for s in e.slots:
                evs.append((s[2], s[0], s[1]))
        return evs

    def barrier(self, engs=("pe", "act", "dve", "pool", "sp")):
        evs = self.all_events()
        for n in engs:
            for ev in evs:
                if ev[0] != self.eng[n].key:
                    self._wait(self.eng[n], ev)


class Ring:
    def __init__(self, items):
        self.items = items
        self.i = 0

    def next(self):
        it = self.items[self.i]
        self.i = (self.i + 1) % len(self.items)
        return it


def pipe(n, make, first=()):
    its = [make(i) for i in range(n)]
    ns = max(len(x) for x in its)
    order = list(first) + [st for st in range(ns - 1, -1, -1) if st not in first]
    for step in range(n + ns - 1):
        for st in order:
            i = step - st
            if 0 <= i < n and st < len(its[i]) and its[i][st] is not None:
                its[i][st]()


DEBUG_OUT = set()


def build_program():
    nc = bass.Bass("TRN2", target_bir_lowering=False)
    es = contextlib.ExitStack()
    k = K(nc, es)

    def din(name, shape, dt=F32):
        return nc.dram_tensor(name, list(shape), dt, kind="ExternalInput").ap()

    def dscr(name, shape, dt=BF16):
        return nc.dram_tensor(name, list(shape), dt, kind=("ExternalOutput" if name in DEBUG_OUT else "Internal")).ap()

    x_in = din("x", [SEQ, D])
    w_in = din("w_in", [D, 7168])
    w_ph = din("w_ph", [D, D])
    w_ps = din("w_ps", [D, D])
    w_o = din("w_o", [D, D])
    w_up = din("w_up", [D, 2 * FFN])
    w_dn = din("w_dn", [FFN, D])
    g1T_d = din("g1T", [128, 8])
    g2T_d = din("g2T", [128, 8])
    hcw_d = din("hcw", [128, 24 * 4])
    fcw_d = din("fcw", [128, NFT * 4])
    gfin_d = din("gfin", [1, D])
    gsgu_d = din("gsgu", [1, D])
    wsT_d = din("wsT", [128, 8 * 128])
    sgub_d = din("sgub", [1, 8 * 128])
    fw1_d = din("fw1", [66, 128])
    fw2_d = din("fw2", [128, 128])
    fw3_d = din("fw3", [128, 128])
    fvec_d = din("fvec", [128, 4])
    fw4_d = din("fw4", [128, D])
    dec_d = din("dec", [2, D])
    skip_d = din("skip", [1, D])
    c_S1u = din("c_S1u", [128, 64 * 128])
    c_S1k = din("c_S1k", [128, 64 * 128])
    c_mats = din("c_mats", [128, 18 * 128])
    c_Lm = din("c_Lm", [128, 64 * NM1])
    c_zT = din("c_zT", [66, 4096])
    c_negt = din("c_negt", [128, 64])
    c_ident = din("c_ident", [128, 128])
    c_msk = din("c_msk", [128, 3])
    out_d = nc.dram_tensor("out", [NTOK, D], F32, kind="ExternalOutput").ap()

    SAk = dscr("SAk", [128, 64, D])
    SU = dscr("SU", [8, 128, SEQ])
    SX0 = dscr("SX0", [8, 128, NTOK])
    SG = dscr("SG", [16, 128, NTOK])
    SUS = dscr("SUS", [8, 128, NTOK])
    SVS = dscr("SVS", [NCH, 128, D])
    SA = dscr("SA", [128, 64, D])
    SC = dscr("SC", [64, 128, D])
    SYA = dscr("SYA", [8, 128, NTOK])
    SYB = dscr("SYB", [8, 128, NTOK])
    SM = dscr("SM", [8, 128, NTOK])
    SX1 = dscr("SX1", [NCH, 128, D], F32)
    SH2 = dscr("SH2", [8, 128, NTOK])
    SACT = dscr("SACT", [NFT, 128, NTOK])
    dSAk, dSK, dSU, dSX0, dSG, dSUS, dSVS, dSA, dSC, dSYA, dSYB, dSM, dSX1, dSH2, dSACT = k.bufs(15, "dram")

    def sb(name, shape, dt=F32, stack=es):
        return stack.enter_context(nc.sbuf_tensor("sb_" + name, list(shape), dt))

    def ring(name, n, shape, dt, stack):
        return Ring([(sb("%s%d" % (name, i), shape, dt, stack), k.buf()) for i in range(n)])

    pst = [es.enter_context(nc.psum_tensor("ps%d" % i, [128, 1024], F32)) for i in range(4)]
    PS = [(pst[i], k.buf("ps%d" % i)) for i in range(4)]

    def psring(idx):
        return Ring([PS[i] for i in idx])

    PSHB = k.bufs(8, "psh")

    def hring(idx):
        return Ring([(pst[i // 2][:, (i % 2) * 512:(i % 2 + 1) * 512], PSHB[i]) for i in idx])

    def ptring(idx):
        return Ring([(pst[i][:].bitcast(BF16)[:, hh * 1024:(hh + 1) * 1024], PS[i][1]) for i in idx for hh in range(1)])

    ident = sb("ident", [128, 128], BF16)
    mats = sb("mats", [128, 18, 128], BF16)
    Lm = sb("Lm", [128, 64, NM1], BF16)
    g1T = sb("g1T", [128, 8])
    g2T = sb("g2T", [128, 8])
    hcw = sb("hcw", [128, 24, 4])
    hcm = sb("hcm", [128, 24, 4])
    fcw = sb("fcw", [128, NFT, 4])
    msk = sb("msk", [128, 3])
    bC = k.buf("consts")
    k.dma(ident[:], c_ident[:, :], w=[bC], q="cast")
    k.dma(mats[:].rearrange("p a b -> p (a b)"), c_mats[:, :], w=[bC], q="cast", acc=True)
    k.dma(Lm[:].rearrange("p a b -> p (a b)"), c_Lm[:, :], w=[bC], q="cast", acc=True)
    for t_, d_ in ((g1T, g1T_d), (g2T, g2T_d), (msk, c_msk)):
        k.dma(t_[:], d_[:, :], w=[bC], acc=True)
    k.dma(hcw[:].rearrange("p a b -> p (a b)"), hcw_d[:, :], w=[bC], acc=True)
    k.dma(fcw[:].rearrange("p a b -> p (a b)"), fcw_d[:, :], w=[bC], acc=True)
    bC2 = k.buf("consts2")
    for i_, (wi, mi) in enumerate(((0, 2), (2, 2), (0, 1), (2, 1))):
        k.op("dve", lambda e, i_=i_, wi=wi, mi=mi: e.tensor_scalar(
            out=hcm[:, :, i_:i_ + 1], in0=hcw[:, :, wi:wi + 1], scalar1=msk[:, mi:mi + 1], scalar2=None,
            op0=ALU.mult), r=[bC], w=[bC2])

    def rstd_ops(ss, tmp, rs, bss, brs, n):
        k.op("dve", lambda e: e.tensor_scalar(out=tmp[:], in0=ss[:], scalar1=1.0 / n, scalar2=EPS,
                                               op0=ALU.mult, op1=ALU.add), r=[bss], w=[brs])
        k.op("act", lambda e: e.activation(out=tmp[:], in_=tmp[:], func=AF.Sqrt), r=[brs], w=[brs])
        k.op("dve", lambda e: e.reciprocal(out=rs[:], in_=tmp[:]), r=[brs], w=[brs])

    def mac(acc_ap, in_ap, sc_ap, bacc, bin_):
        k.op("dve", lambda e: e.scalar_tensor_tensor(out=acc_ap, in0=in_ap, scalar=sc_ap, in1=acc_ap,
                                                      op0=ALU.mult, op1=ALU.add), r=[bin_, bacc, bC, bC2], w=[bacc])

    def conv_taps(xc, praw, bxc, bpr, cw, ci, segs):
        for (a, b) in segs:
            mac(xc[:, a + 1:b], praw[:, a:b - 1], cw[:, ci, 0:1], bxc, bpr)
            mac(xc[:, a:b - 1], praw[:, a + 1:b], cw[:, ci, 2:3], bxc, bpr)

    def ssq(junk, src_ap, ss, bsrc, bss):
        k.op("act", lambda e: e.activation(out=junk[0][:], in_=src_ap, func=AF.Square, accum_out=ss[:]),
             r=[bsrc], w=[junk[1], bss])

    with contextlib.ExitStack() as ph:
        zT = sb("zT", [66, 4096], F32, ph)
        fw1 = sb("fw1", [66, 128], F32, ph)
        fw2 = sb("fw2", [128, 128], F32, ph)
        fw3 = sb("fw3", [128, 128], F32, ph)
        fvec = sb("fvec", [128, 4], F32, ph)
        fw4f = sb("fw4f", [128, D], F32, ph)
        fw4 = sb("fw4", [128, D], BF16, ph)
        decb = sb("decb", [128, D], F32, ph)
        skipr = sb("skipr", [1, D], F32, ph)
        negt = sb("negt", [128, 64], F32, ph)
        S1k = sb("S1k", [128, 64, 128], BF16, ph)
        h3 = sb("h3", [128, 8192], BF16, ph)
        bF = k.buf("fconst")
        k.dma(zT[:], c_zT[:, :], w=[bF])
        k.dma(fw1[:], fw1_d[:, :], w=[bF], acc=True)
        k.dma(fw2[:], fw2_d[:, :], w=[bF], acc=True)
        k.dma(fw3[:], fw3_d[:, :], w=[bF], acc=True)
        k.dma(fvec[:], fvec_d[:, :], w=[bF], acc=True)
        k.dma(fw4f[:], fw4_d[:, :], w=[bF], acc=True)
        k.dma(decb[0:64, :], dec_d[0, :].partition_broadcast(64), w=[bF], acc=True)
        k.dma(decb[64:128, :], dec_d[1, :].partition_broadcast(64), w=[bF], acc=True)
        k.dma(skipr[:], skip_d[:, :], w=[bF], acc=True)
        k.dma(negt[:], c_negt[:, :], w=[bF], acc=True)
        k.dma(S1k[:].rearrange("p a b -> p (a b)"), c_S1k[:, :], w=[bF], q="cast", acc=True)
        bF2 = k.buf("fconst2")
        bh3 = k.buf("h3")
        k.op("dve", lambda e: e.tensor_copy(out=fw4[:], in_=fw4f[:]), r=[bF], w=[bF2])
        k.op("act", lambda e: e.activation(out=decb[:], in_=decb[:], func=AF.Abs), r=[bF], w=[bF])
        k.op("pool", lambda e: e.memset(h3[:], 0.0), w=[bh3])
        AR = ring("farg", 3, [128, 1024], F32, ph)
        RR = ring("frr", 3, [128, 1024], F32, ph)
        HS = [ring("fh0_", 2, [128, 1024], F32, ph), ring("fh1_", 2, [128, 1024], F32, ph)]
        PL = [psring([0]), psring([1]), psring([2])]

        def ffn_chunk(c):
            cs = slice(c * 1024, (c + 1) * 1024)
            stages = []
            prev = [None]
            for layer, (wmat, kk) in enumerate(((fw1, 66), (fw2, 128), (fw3, 128))):
                ps, pb = PL[layer].next()
                a_t, a_b = AR.next()
                r_t, r_b = RR.next()
                h_t, h_b = HS[layer].next() if layer < 2 else (None, None)

                def mm(layer=layer, wmat=wmat, ps=ps, pb=pb, src=prev[0]):
                    for hh in range(2):
                        hs = slice(hh * 512, (hh + 1) * 512)
                        if layer == 0:
                            rhs, rb = zT[:, c * 1024 + hh * 512:c * 1024 + (hh + 1) * 512], bF
                        else:
                            rhs, rb = src[0][:, hs], src[1]
                        k.op("pe", lambda e: e.matmul(ps[:, hs], wmat[:, :], rhs, start=True, stop=True), r=[bF, rb], w=[pb])

                def red(layer=layer, ps=ps, pb=pb, a_t=a_t, a_b=a_b, r_t=r_t, r_b=r_b):
                    k.op("dve", lambda e: e.tensor_scalar(out=a_t[:], in0=ps[:], scalar1=fvec[:, layer:layer + 1],
                                                           scalar2=fvec[:, 3:4], op0=ALU.add, op1=ALU.mult), r=[pb, bF], w=[a_b])
                    k.op("dve", lambda e: e.tensor_scalar(out=r_t[:], in0=a_t[:], scalar1=1.0 / TWO_PI, scalar2=MAGIC,
                                                           op0=ALU.mult, op1=ALU.add), r=[a_b], w=[r_b])
                    k.op("dve", lambda e: e.tensor_scalar(out=r_t[:], in0=r_t[:], scalar1=MAGIC, scalar2=TWO_PI,
                                                           op0=ALU.subtract, op1=ALU.mult), r=[r_b], w=[r_b])
                    k.op("dve", lambda e: e.tensor_tensor(out=a_t[:], in0=a_t[:], in1=r_t[:], op=ALU.subtract),
                         r=[r_b, a_b], w=[a_b])
                    k.op("dve", lambda e: e.tensor_scalar(out=a_t[:], in0=a_t[:], scalar1=-3.1415925, scalar2=3.1415925,
                                                           op0=ALU.max, op1=ALU.min), r=[a_b], w=[a_b])

                def sn(layer=layer, a_t=a_t, a_b=a_b, h_t=h_t, h_b=h_b):
                    if layer < 2:
                        k.op("act", lambda e: e.activation(out=h_t[:], in_=a_t[:], func=AF.Sin), r=[a_b], w=[h_b])
                    else:
                        k.op("act", lambda e: e.activation(out=h3[0:64, cs], in_=a_t[0:64, :], func=AF.Sin), r=[a_b], w=[bh3])
                        k.op("act", lambda e: e.activation(out=h3[64:128, 4096 + c * 1024:4096 + (c + 1) * 1024],
                                                            in_=a_t[64:128, :], func=AF.Sin), r=[a_b], w=[bh3])
                stages += [mm, red, sn]
                prev[0] = (h_t, h_b)
            return stages
        pipe(4, ffn_chunk)
        k.op("pool", lambda e: e.memset(h3[64:128, 4096:4097], 0.0), w=[bh3])
        if "DBG_h3" in DEBUG_OUT:
            dbg_h3 = nc.dram_tensor("DBG_h3", [128, 8192], BF16, kind="ExternalOutput").ap()
            k.dma(dbg_h3[:, :], h3[:], r=[bh3])
        h3v = h3[:].rearrange("p (a b) -> p a b", b=64)
        Es = ring("fE", 2, [128, D], F32, ph)
        KT = ring("fkT", 3, [128, D], BF16, ph)
        KF = (sb("fkF", [128, D], F32, ph), k.buf())
        AK = ring("fAk", 3, [128, D], BF16, ph)
        PA_, PB_ = psring([0, 1]), psring([2, 3])

        def st1(n2):
            ps, pb = PA_.next()
            ps2, pb2 = PB_.next()
            E_t, E_b = Es.next()
            kt, ktb = KT.next()
            ak, akb = AK.next()

            def s0():
                for cc in range(2):
                    k.op("pe", lambda e, cc=cc: e.matmul(ps[:, cc * 512:(cc + 1) * 512], h3v[:, :, n2],
                                                          fw4[:, cc * 512:(cc + 1) * 512], start=True, stop=True),
                         r=[bh3, bF2], w=[pb])
                k.op("act", lambda e: e.activation(out=E_t[:], in_=decb[:], func=AF.Exp, scale=negt[:, n2:n2 + 1]),
                     r=[bF], w=[E_b])

            def s1():
                if n2 == 0:
                    kf, kfb = KF
                    k.op("dve", lambda e: e.tensor_tensor(out=kf[:], in0=ps[:], in1=E_t[:], op=ALU.mult), r=[pb, E_b], w=[kfb])
                    k.op("dve", lambda e: e.tensor_tensor(out=kf[0:1, :], in0=kf[0:1, :], in1=skipr[0:1, :], op=ALU.add),
                         r=[kfb, bF], w=[kfb])
                    k.op("dve", lambda e: e.tensor_copy(out=kt[:], in_=kf[:]), r=[kfb], w=[ktb])
                else:
                    k.op("dve", lambda e: e.tensor_tensor(out=kt[:], in0=ps[:], in1=E_t[:], op=ALU.mult), r=[pb, E_b], w=[ktb])

            def s2():
                for cc in range(2):
                    k.op("pe", lambda e, cc=cc: e.matmul(ps2[:, cc * 512:(cc + 1) * 512], S1k[:, n2, :],
                                                          kt[:, cc * 512:(cc + 1) * 512], start=True, stop=True),
                         r=[ktb, bF], w=[pb2])

            def s3():
                if n2 % 2:
                    k.op("act", lambda e: e.copy(out=ak[:], in_=ps2[:]), r=[pb2], w=[akb])
                else:
                    k.op("dve", lambda e: e.tensor_copy(out=ak[:], in_=ps2[:]), r=[pb2], w=[akb])

            def s4():
                k.dma(SAk[:, n2, :], ak[:], r=[akb], w=[dSAk])
            return [s0, s1, s2, s3, s4]
        pipe(64, st1)
        k.barrier()

    with contextlib.ExitStack() as ph:
        hT = sb("hT", [128, 8, SEQ], BF16, ph)
        bhT = k.bufs(32, "hT")
        junk = (sb("junk", [128, D], BF16, ph), k.buf())
        SSs = Ring([(sb("ss%d" % i, [128, 1], F32, ph), sb("sst%d" % i, [128, 1], F32, ph),
                     sb("srs%d" % i, [128, 1], F32, ph), k.buf(), k.buf()) for i in range(4)])
        WF = ring("wf", 3, [128, 8, 128], F32, ph)
        WB = ring("wb", 3, [128, 8, 128], BF16, ph)
        phA = contextlib.ExitStack()
        XT = ring("xt", 4, [128, D], F32, phA)
        XS = ring("xs", 3, [128, D], BF16, phA)
        g1b = g1T[:, :].unsqueeze(2).to_broadcast([128, 8, 128])
        PT = ptring([2, 3])

        def pa(i):
            xt, xtb = XT.next()
            ss, sst, srs, bss, brs = SSs.next()
            xs, xsb = XS.next()
            pt, ptb = PT.next()

            def s0():
                k.dma(xt[:], x_in[i * 128:(i + 1) * 128, :], w=[xtb])

            def s1():
                ssq(junk, xt[:], ss, xtb, bss)

            def s2():
                rstd_ops(ss, sst, srs, bss, brs, D)

            def s3():
                if i % 2:
                    k.op("act", lambda e: e.activation(out=xs[:], in_=xt[:], func=AF.Copy, scale=srs[:, 0:1]),
                         r=[xtb, brs], w=[xsb])
                else:
                    k.op("dve", lambda e: e.tensor_scalar(out=xs[:], in0=xt[:], scalar1=srs[:, 0:1], scalar2=None, op0=ALU.mult),
                         r=[xtb, brs], w=[xsb])

            def s4():
                for kc in range(8):
                    k.op("pe", lambda e, kc=kc: e.transpose(pt[:, kc * 128:(kc + 1) * 128], xs[:, kc * 128:(kc + 1) * 128], ident[:]),
                         r=[xsb, bC], w=[ptb])

            def s5():
                k.op("dve", lambda e: e.tensor_tensor(out=hT[:, :, i * 128:(i + 1) * 128],
                                                       in0=pt.rearrange("p (a b) -> p a b", b=128), in1=g1b, op=ALU.mult),
                     r=[ptb, bC], w=[bhT[i]])
            return [s0, s1, s2, s3, s4, s5]
        pipe(32, pa)
        k.barrier()
        phA.close()
        w_in_v = w_in.rearrange("(kc p) c -> p kc c", p=128)
        PSM = psring([0, 1, 2, 3])

        def proj_fm(wb, wbb, t0, tn, ps, pb, half):
            deps = [bhT[j] for j in range(t0 // 128, (t0 + tn + 127) // 128)]
            for kc in range(8):
                k.op("pe", lambda e, kc=kc: e.matmul(ps[:, half * 512:half * 512 + tn], wb[:, kc, :], hT[:, kc, t0:t0 + tn],
                                                      start=(kc == 0), stop=(kc == 7)),
                     r=[wbb] + deps, w=[pb])

        EXT3 = [([(0, 512), (512, 512)], slice(0, 1024), slice(0, 1024)),
                ([(1024, 512), (1536, 512)], slice(1024, 2048), slice(0, 1024)),
                ([(2048, 128)], slice(2048, NTOK), slice(0, 128))]

        phB = contextlib.ExitStack()
        PR = [(sb("praw%d" % i, [128, SEQ], F32, phB), k.buf()) for i in range(2)]
        XC = [(sb("xc%d" % i, [128, SEQ], F32, phB), k.buf()) for i in range(2)]
        UC = ring("uc", 2, [128, SEQ], BF16, phB)
        X0 = ring("x0o", 2, [128, NTOK], BF16, phB)
        GO = ring("go", 3, [128, NTOK], BF16, phB)

        def pb1(it):
            ct, which = it // 2, it % 2
            col0 = (1024 if which == 0 else 2048) + ct * 128
            ci = (8 if which == 0 else 16) + ct
            wf, wfb = WF.next()
            wb, wbb = WB.next()
            praw, bpr = PR[which]
            xc, bxc = XC[which]
            uc, ucb = UC.next() if which == 1 else (None, None)

            def s0():
                k.dma(wf[:], w_in_v[:, :, col0:col0 + 128], w=[wfb])

            def s1():
                k.op("dve", lambda e: e.tensor_copy(out=wb[:], in_=wf[:]), r=[wfb], w=[wbb])

            def s2():
                for tt in range(4):
                    ps, pb = PSM.next()
                    for half in range(2):
                        proj_fm(wb, wbb, tt * 1024 + half * 512, 512, ps, pb, half)
                    sl = slice(tt * 1024, (tt + 1) * 1024)
                    k.op("act", lambda e: e.copy(out=praw[:, sl], in_=ps[:]), r=[pb], w=[bpr])
                    k.op("act", lambda e: e.activation(out=xc[:, sl], in_=ps[:], func=AF.Identity,
                                                        scale=hcw[:, ci, 1:2], bias=hcw[:, ci, 3:4]), r=[pb, bC], w=[bxc])

            def s3():
                conv_taps(xc, praw, bxc, bpr, hcw, ci, [(0, SEQ)])
                for (dst, src, mi) in ((NTOK, NTOK - 1, 0), (NTOK - 1, NTOK, 1), (0, SEQ - 1, 2), (SEQ - 1, 0, 3)):
                    mac(xc[:, dst:dst + 1], praw[:, src:src + 1], hcm[:, ci, mi:mi + 1], bxc, bpr)
                if which == 1:
                    k.op("dve", lambda e: e.tensor_tensor(out=uc[:], in0=XC[0][0][:], in1=XC[1][0][:], op=ALU.mult),
                         r=[XC[0][1], XC[1][1]], w=[ucb])

            def s4():
                if which == 1:
                    k.dma(SU[ct, :, :], uc[:], r=[ucb], w=[dSU])
            return [s0, s1, s2, s3, s4]
        pipe(16, pb1, first=(1,))

        def pb2(ct):
            wf, wfb = WF.next()
            wb, wbb = WB.next()
            praw, bpr = PR[ct % 2]
            xc, bxc = XC[ct % 2]
            xo, xob = X0.next()

            def s0():
                k.dma(wf[:], w_in_v[:, :, ct * 128:(ct + 1) * 128], w=[wfb])

            def s1():
                k.op("dve", lambda e: e.tensor_copy(out=wb[:], in_=wf[:]), r=[wfb], w=[wbb])

            def s2():
                for (halves, sl, psl) in EXT3:
                    ps, pb = PSM.next()
                    for hi, (t0, tn) in enumerate(halves):
                        proj_fm(wb, wbb, t0, tn, ps, pb, hi)
                    k.op("act", lambda e: e.copy(out=praw[:, sl], in_=ps[:, psl]), r=[pb], w=[bpr])
                    k.op("act", lambda e: e.activation(out=xc[:, sl], in_=ps[:, psl], func=AF.Identity,
                                                        scale=hcw[:, ct, 1:2], bias=hcw[:, ct, 3:4]), r=[pb, bC], w=[bxc])

            def s3():
                conv_taps(xc, praw, bxc, bpr, hcw, ct, [(0, NTOK)])
                k.op("dve", lambda e: e.tensor_copy(out=xo[:], in_=xc[:, 0:NTOK]), r=[bxc], w=[xob])

            def s4():
                k.dma(SX0[ct, :, :], xo[:], r=[xob], w=[dSX0])
            return [s0, s1, s2, s3, s4]
        pipe(8, pb2, first=(1,))

        def pb3(mt):
            if mt < 16:
                col0, func, dst, dbuf = 5120 + mt * 128, AF.Sigmoid, SG[mt, :, :], dSG
            else:
                col0, func, dst, dbuf = 3072 + (mt - 16) * 128, AF.Gelu, SUS[mt - 16, :, :], dSUS
            wf, wfb = WF.next()
            wb, wbb = WB.next()
            go, gob = GO.next()

            def s0():
                k.dma(wf[:], w_in_v[:, :, col0:col0 + 128], w=[wfb])

            def s1():
                k.op("dve", lambda e: e.tensor_copy(out=wb[:], in_=wf[:]), r=[wfb], w=[wbb])

            def s2():
                for (halves, sl, psl) in EXT3:
                    ps, pb = PSM.next()
                    for hi, (t0, tn) in enumerate(halves):
                        proj_fm(wb, wbb, t0, tn, ps, pb, hi)
                    k.op("act", lambda e: e.activation(out=go[:, sl], in_=ps[:, psl], func=func), r=[pb], w=[gob])

            def s3():
                k.dma(dst, go[:], r=[gob], w=[dbuf])
            return [s0, s1, s2, s3]
        pipe(24, pb3)
        k.barrier()
        phB.close()

        WV = ring("wvf", 2, [128, 8, 512], F32, ph)
        WVb = (sb("wvb", [128, 8, D], BF16, ph), k.buf())
        for hh in range(2):
            wv, wvb_ = WV.next()
            k.dma(wv[:], w_in_v[:, :, 4096 + hh * 512:4096 + (hh + 1) * 512], w=[wvb_])
            k.op("dve", lambda e, hh=hh: e.tensor_copy(out=WVb[0][:, :, hh * 512:(hh + 1) * 512], in_=wv[:]),
                 r=[wvb_], w=[WVb[1]])
        gsg = (sb("gsg", [128, D], F32, ph), k.buf())
        k.dma(gsg[0][:], gsgu_d[0, :].partition_broadcast(128), w=[gsg[1]])
        GV = ring("gv", 4, [128, D], F32, ph)
        VO = ring("vo", 3, [128, D], BF16, ph)
        PV = psring([0, 1])

        def pb5(tchunk):
            ps, pb = PV.next()
            gv, gvb = GV.next()
            ss, sst, srs, bss, brs = SSs.next()
            vo, vob = VO.next()

            def s0():
                for hh in range(2):
                    for kc in range(8):
                        k.op("pe", lambda e, kc=kc, hh=hh: e.matmul(ps[:, hh * 512:(hh + 1) * 512],
                                                                      hT[:, kc, tchunk * 128:(tchunk + 1) * 128],
                                                                      WVb[0][:, kc, hh * 512:(hh + 1) * 512],
                                                                      start=(kc == 0), stop=(kc == 7)),
                             r=[WVb[1], bhT[tchunk]], w=[pb])

            def s1():
                k.op("act", lambda e: e.activation(out=gv[:], in_=ps[:], func=AF.Gelu), r=[pb], w=[gvb])

            def s2():
                ssq(junk, gv[:], ss, gvb, bss)

            def s3():
                rstd_ops(ss, sst, srs, bss, brs, D)

            def s4():
                k.op("dve", lambda e: e.scalar_tensor_tensor(out=vo[:], in0=gv[:], scalar=srs[:, 0:1], in1=gsg[0][:],
                                                              op0=ALU.mult, op1=ALU.mult), r=[gvb, brs, gsg[1]], w=[vob])

            def s5():
                k.dma(SVS[tchunk, :, :], vo[:], r=[vob], w=[dSVS])
            return [s0, s1, s2, s3, s4, s5]
        pipe(NCH, pb5)
        k.barrier()

    mid = contextlib.ExitStack()
    yaa = sb("yaa", [128, 8, NTOK], BF16, mid)
    ybT = sb("ybT", [128, 8, NTOK], BF16, mid)
    bya, byb = k.bufs(8, "yaa"), k.bufs(8, "ybT")
    with contextlib.ExitStack() as ph:
        S1u = sb("S1u", [128, 64, 128], BF16, ph)
        bS = k.buf("S1u")
        k.dma(S1u[:].rearrange("p a b -> p (a b)"), c_S1u[:, :], w=[bS], q="cast")
        with contextlib.ExitStack() as ph2:
            uall = sb("uall", [128, 8, SEQ], BF16, ph2)
            bU = k.buf("uall")
            for ct in range(8):
                k.dma(uall[:, ct, :], SU[ct, :, :], r=[dSU], w=[bU], acc=(ct > 0))
            uv2 = uall[:].rearrange("p c (m b) -> p c m b", b=32)
            UT = ring("uT", 3, [128, D], BF16, ph2)
            AO = ring("Ao", 4, [128, D], BF16, ph2)
            PT = ptring([2, 3])
            PSA, PSB2 = psring([0]), psring([1])

            def pc1(pi):
                pt, ptb = PT.next()
                ut, utb = UT.next()
                psa, pab = PSA.next()
                psb, pbb = PSB2.next()
                aoa, aoab = AO.next()
                aob_, aobb = AO.next()
                n2a, n2b = pi, pi + 32

                def s0():
                    for ct in range(8):
                        k.op("pe", lambda e, ct=ct: e.transpose(pt[:, ct * 128:(ct + 1) * 128], uv2[:, ct, :, pi], ident[:]),
                             r=[bU, bC], w=[ptb])

                def s1():
                    k.op("dve", lambda e: e.tensor_copy(out=ut[:], in_=pt[:, :]), r=[ptb], w=[utb])

                def s2():
                    for (ps, pb, n2) in ((psa, pab, n2a), (psb, pbb, n2b)):
                        for cc in range(2):
                            k.op("pe", lambda e, cc=cc, ps=ps, n2=n2: e.matmul(
                                ps[:, cc * 512:(cc + 1) * 512], S1u[:, n2, :],
                                ut[:, cc * 512:(cc + 1) * 512], start=True, stop=True), r=[utb, bS], w=[pb])

                def s3():
                    k.op("act", lambda e: e.copy(out=aoa[:], in_=psa[:]), r=[pab], w=[aoab])
                    k.op("dve", lambda e: e.tensor_copy(out=aob_[:], in_=psb[:]), r=[pbb], w=[aobb])

                def s4():
                    k.dma(SA[:, n2a, :], aoa[:], r=[aoab], w=[dSA])
                    k.dma(SA[:, n2b, :], aob_[:], r=[aobb], w=[dSA])
                return [s0, s1, s2, s3, s4]
            pipe(32, pc1)
            k.barrier()
        BT = ring("Bt", 3, [128, D], BF16, ph)
        BKT = ring("Bkt", 3, [128, D], BF16, ph)
        KC = ring("Kc", 3, [128, 512], BF16, ph)
        Q1 = ring("Q1_", 3, [128, D], BF16, ph)
        CO = ring("Co", 3, [128, D], BF16, ph)
        QK, QC = hring([0, 1]), hring([2, 3])
        QZ = Ring([(pst[2], k.buf()), (pst[3], k.buf())])
        tiles = {}

        def get_tiles(t):
            if t not in tiles:
                tiles[t] = (BT.next(), BKT.next(), CO.next())
            return tiles[t]

        def load_t(t):
            (bt, btb), (bkt, bktb), _ = get_tiles(t)
            k.dma(bt[0:64, :], SA[t, :, :], r=[dSA], w=[btb])
            k.dma(bt[64:128, :], SA[64 + t, :, :], r=[dSA], w=[btb], acc=True)
            k.dma(bkt[0:64, :], SAk[t, :, :], r=[dSAk], w=[bktb])
            k.dma(bkt[64:128, :], SAk[64 + t, :, :], r=[dSAk], w=[bktb], acc=True)

        def front(t, cc, ma, mb):
            (bt, btb), (bkt, bktb), _ = get_tiles(t)
            hs = slice(cc * 512, (cc + 1) * 512)
            qk, qkb = QK.next()
            zz, zzb = QZ.next()
            kc, kcb = KC.next()
            qq, qqb = Q1.next()

            def mm():
                k.op("pe", lambda e: e.matmul(qk[:], mats[:, ma, :], bkt[:, hs], start=True, stop=True), r=[bktb, bC], w=[qkb])
                k.op("pe", lambda e: e.matmul(zz[:, 0:512], mats[:, ma, :], bt[:, hs], start=True, stop=True), r=[btb, bC], w=[zzb])
                k.op("pe", lambda e: e.matmul(zz[:, 512:1024], mats[:, mb, :], bt[:, hs], start=True, stop=True), r=[btb, bC], w=[zzb])

            def cp():
                k.op("act", lambda e: e.copy(out=kc[:], in_=qk[:]), r=[qkb], w=[kcb])

            def pr():
                k.op("dve", lambda e: e.tensor_tensor(out=qq[:].rearrange("p (a b) -> p a b", a=2),
                                                       in0=zz[:].rearrange("p (a b) -> p a b", a=2),
                                                       in1=kc[:].unsqueeze(1).to_broadcast([128, 2, 512]), op=ALU.mult),
                     r=[zzb, kcb], w=[qqb])
            return mm, cp, pr, (qq[:, 0:512], qqb), (qq[:, 512:1024], qqb)

        def back(t, cc, plist):
            _, _, (co, cob) = get_tiles(t)
            hs = slice(cc * 512, (cc + 1) * 512)
            qc, qcb = QC.next()

            def imm():
                for pi, (pp, ppb, tm) in enumerate(plist):
                    k.op("pe", lambda e, pp=pp, tm=tm, pi=pi: e.matmul(qc[:], mats[:, tm, :], pp, start=(pi == 0),
                                                                        stop=(pi == len(plist) - 1)), r=[ppb, bC], w=[qcb])

            def cpo():
                k.op("act", lambda e: e.copy(out=co[:, hs], in_=qc[:]), r=[qcb], w=[cob])

            def st():
                if cc == 1:
                    k.dma(SC[t, :, :], co[:], r=[cob], w=[dSC])
            return imm, cpo, st

        load_t(0)
        for cc in range(2):
            plist = []
            for (ma, mb, t1, t2) in ((4, 5, 14, 15), (8, 9, 16, 17)):
                mm, cp, pr, (q1, q1b), (q2, q2b) = front(0, cc, ma, mb)
                mm(); cp(); pr()
                plist += [(q1, q1b, t1), (q2, q2b, t2)]
            imm, cpo, st = back(0, cc, plist)
            imm(); cpo(); st()

        def pc2(it):
            t, cc = 1 + it // 2, it % 2
            mm, cp, pr, (q1, q1b), (q2, q2b) = front(t, cc, 0, 1)
            imm, cpo, st = back(t, cc, [(q1, q1b, 12), (q2, q2b, 13)])

            def s0():
                if cc == 0:
                    load_t(t)
            return [s0, mm, cp, pr, imm, cpo, st]
        pipe(126, pc2)
        k.barrier()
        x0a = sb("x0a", [128, 8, NTOK], BF16, ph)
        bx0 = k.buf("x0a")
        for ct in range(8):
            k.dma(x0a[:, ct, :], SX0[ct, :, :], r=[dSX0], w=[bx0], acc=(ct > 0))
        DT = ring("Dt", 3, [128, 8, D], BF16, ph)
        x0v = x0a[:].rearrange("p c (i m) -> p c i m", m=64)
        yav = yaa[:].rearrange("p c (i m) -> p c i m", m=64)
        PSM = psring([0, 1, 2, 3])

        def pc3(mg):
            dt_, dtb = DT.next()

            def s0():
                for mm in range(8):
                    m2 = mg * 8 + mm
                    k.dma(dt_[0:64, mm, :], SC[:, m2, :], r=[dSC], w=[dtb], acc=(mm > 0))
                    k.dma(dt_[64:128, mm, :], SC[:, 64 + m2, :], r=[dSC], w=[dtb], acc=True)

            def s1():
                for ct in range(8):
                    ps, pb = PSM.next()
                    for mm in range(8):
                        m2 = mg * 8 + mm
                        k.op("pe", lambda e, mm=mm, m2=m2: e.matmul(ps[:, mm * NM1:(mm + 1) * NM1], dt_[:, mm, ct * 128:(ct + 1) * 128],
                                                                     Lm[:, m2, :], start=True, stop=True), r=[dtb, bC], w=[pb])
                    k.op("dve", lambda e: e.tensor_tensor(
                        out=yav[:, ct, :, mg * 8:(mg + 1) * 8],
                        in0=ps[:, 0:8 * NM1].rearrange("p (m i) -> p i m", i=NM1),
                        in1=x0v[:, ct, :, mg * 8:(mg + 1) * 8], op=ALU.mult), r=[pb, bx0], w=[bya[ct]])
            return [s0, s1]
        pipe(8, pc3)
        k.barrier()

    with contextlib.ExitStack() as ph:
        vh = sb("vh", [128, NCH, D], BF16, ph)
        bvh = k.buf("vh")
        for c_ in range(NCH):
            k.dma(vh[:, c_, :], SVS[c_, :, :], r=[dSVS], w=[bvh], acc=(c_ > 0))
        wsf = sb("wsf", [128, 8 * 128], F32, ph)
        wsb = sb("wsb", [128, 8, 128], BF16, ph)
        sbf = sb("sbf", [1, 8 * 128], F32, ph)
        sbb = sb("sbb", [1, 8, 128], BF16, ph)
        ones = sb("ones", [1, 128], BF16, ph)
        bws = k.buf("ws")
        k.dma(wsf[:], wsT_d[:, :], w=[bws])
        k.dma(sbf[:], sgub_d[:, :], w=[bws], acc=True)
        bws2 = k.buf("ws2")
        k.op("dve", lambda e: e.tensor_copy(out=wsb[:].rearrange("p a b -> p (a b)"), in_=wsf[:]), r=[bws], w=[bws2])
        k.op("dve", lambda e: e.tensor_copy(out=sbb[:].rearrange("p a b -> p (a b)"), in_=sbf[:]), r=[bws], w=[bws2])
        k.op("pool", lambda e: e.memset(ones[:], 1.0), w=[bws2])
        USr = ring("usr", 3, [128, NTOK], BF16, ph)
        PSM = psring([0, 1, 2, 3])

        def pd1(g):
            us, usb = USr.next()

            def s0():
                k.dma(us[:], SUS[g, :, :], r=[dSUS], w=[usb])

            def s1():
                for cg in range(5):
                    chunks = list(range(cg * 4, min(cg * 4 + 4, NCH)))
                    ps, pb = PSM.next()
                    for ci_, ch in enumerate(chunks):
                        k.op("pe", lambda e, ci_=ci_, ch=ch: e.matmul(ps[:, ci_ * 128:(ci_ + 1) * 128], vh[:, ch, g * 128:(g + 1) * 128],
                                                                       wsb[:, g, :], start=True, stop=False), r=[bvh, bws2], w=[pb])
                        k.op("pe", lambda e, ci_=ci_: e.matmul(ps[:, ci_ * 128:(ci_ + 1) * 128], ones[0:1, :], sbb[0:1, g, :],
                                                                start=False, stop=True), r=[bws2], w=[pb])
                    n = len(chunks) * 128
                    sl = slice(cg * 512, cg * 512 + n)
                    k.op("dve", lambda e: e.tensor_tensor(out=ybT[:, g, sl], in0=ps[:, 0:n], in1=us[:, sl], op=ALU.mult),
                         r=[pb, usb], w=[byb[g]])
            return [s0, s1]
        pipe(8, pd1)
        k.barrier()

    with contextlib.ExitStack() as ph:
        yaT = yaa
        WF = ring("pwf", 6, [128, 8, 128], F32, ph)
        WB = ring("pwb", 6, [128, 8, 128], BF16, ph)
        GA = ring("ga", 3, [128, NTOK], BF16, ph)
        GB = ring("gb", 3, [128, NTOK], BF16, ph)
        M1 = ring("m1_", 2, [128, 512], F32, ph)
        M2 = ring("m2_", 2, [128, 512], F32, ph)
        MO = ring("mo", 3, [128, NTOK], BF16, ph)
        wphv = w_ph.rearrange("(kc p) c -> p kc c", p=128)
        wpsv = w_ps.rearrange("(kc p) c -> p kc c", p=128)
        PSM = psring([0, 1, 2, 3])

        def pd2(dtile):
            wl = [(WF.next(), WB.next()) for _ in range(2)]
            ga, gab = GA.next()
            gb, gbb = GB.next()
            mo, mob = MO.next()

            def s0():
                for ((wf, wfb), _), wv in zip(wl, (wphv, wpsv)):
                    k.dma(wf[:], wv[:, :, dtile * 128:(dtile + 1) * 128], w=[wfb])
                k.dma(ga[:], SG[dtile, :, :], r=[dSG], w=[gab])
                k.dma(gb[:], SG[8 + dtile, :, :], r=[dSG], w=[gbb])

            def s1():
                for ((wf, wfb), (wb, wbb)) in wl:
                    k.op("dve", lambda e, wf=wf, wb=wb: e.tensor_copy(out=wb[:], in_=wf[:]), r=[wfb], w=[wbb])

            def s2():
                for (t0, tn) in TOKT:
                    psA, pbA = PSM.next()
                    psB, pbB = PSM.next()
                    for (ps, pb, (_, (wb, wbb)), yT, by) in ((psA, pbA, wl[0], yaT, bya), (psB, pbB, wl[1], ybT, byb)):
                        for kc in range(8):
                            k.op("pe", lambda e, kc=kc, ps=ps, wb=wb, yT=yT: e.matmul(ps[:, 0:tn], wb[:, kc, :], yT[:, kc, t0:t0 + tn],
                                                                                         start=(kc == 0), stop=(kc == 7)),
                                 r=[wbb, by[kc]], w=[pb])
                    m1, m1b = M1.next()
                    m2_, m2b = M2.next()
                    k.op("dve", lambda e: e.tensor_tensor(out=m1[:, 0:tn], in0=psA[:, 0:tn], in1=ga[:, t0:t0 + tn], op=ALU.mult),
                         r=[pbA, gab], w=[m1b])
                    k.op("dve", lambda e: e.tensor_tensor(out=m2_[:, 0:tn], in0=psB[:, 0:tn], in1=gb[:, t0:t0 + tn], op=ALU.mult),
                         r=[pbB, gbb], w=[m2b])
                    k.op("pool", lambda e: e.tensor_tensor(out=mo[:, t0:t0 + tn], in0=m1[:, 0:tn], in1=m2_[:, 0:tn], op=ALU.add),
                         r=[m1b, m2b], w=[mob])

            def s3():
                k.dma(SM[dtile, :, :], mo[:], r=[mob], w=[dSM])
            return [s0, s1, s2, s3]
        pipe(8, pd2)
        k.barrier()
    mid.close()

    tail = contextlib.ExitStack()
    wdb = sb("wdb", [128, NFT, D], BF16, tail)
    bwd = k.bufs(NFT, "wdb")
    h2s = contextlib.ExitStack()
    h2T = sb("h2T", [128, 8, NTOK], BF16, h2s)
    with contextlib.ExitStack() as ph:
        mT = sb("mT", [128, 8, NTOK], BF16, ph)
        bmT, bh2 = k.buf(), k.bufs(NCH, "h2")
        bmTg = k.bufs(len(TOKT), "mTg")
        smv = SM.rearrange("m p t -> p m t")
        for gi, (t0, tn) in enumerate(TOKT):
            k.dma(mT[:, :, t0:t0 + tn], smv[:, :, t0:t0 + tn], r=[dSM], w=[bmTg[gi]])
        WOF = ring("wof", 2, [128, 8, 512], F32, ph)
        wob = sb("wob", [128, 8, D], BF16, ph)
        bwob = k.buf()
        wov = w_o.rearrange("(kc p) c -> p kc c", p=128)
        for hh in range(2):
            wof, bwof = WOF.next()
            k.dma(wof[:], wov[:, :, hh * 512:(hh + 1) * 512], w=[bwof])
            k.op("dve", lambda e, hh=hh: e.tensor_copy(out=wob[:, :, hh * 512:(hh + 1) * 512], in_=wof[:]), r=[bwof], w=[bwob])
        XT = ring("xt2_", 3, [128, D], F32, ph)
        X1 = ring("x1_", 4, [128, D], F32, ph)
        XS = ring("xs2_", 3, [128, D], BF16, ph)
        junk = (sb("junk2", [128, D], BF16, ph), k.buf())
        SSs = Ring([(sb("ss2_%d" % i, [128, 1], F32, ph), sb("sst2_%d" % i, [128, 1], F32, ph),
                     sb("srs2_%d" % i, [128, 1], F32, ph), k.buf(), k.buf()) for i in range(4)])
        g2b = g2T[:, :].unsqueeze(2).to_broadcast([128, 8, 128])
        PT = ptring([2, 3])
        PSA = psring([0, 1])

        def pd3(i):
            xt, xtb = XT.next()
            ps, pb = PSA.next()
            x1, x1b = X1.next()
            ss, sst, srs, bss, brs = SSs.next()
            xs, xsb = XS.next()
            pt, ptb = PT.next()

            def s0():
                k.dma(xt[:], x_in[i * 128:(i + 1) * 128, :], w=[xtb])

            def s1():
                for hh in range(2):
                    for kc in range(8):
                        k.op("pe", lambda e, kc=kc, hh=hh: e.matmul(ps[:, hh * 512:(hh + 1) * 512], mT[:, kc, i * 128:(i + 1) * 128],
                                                                      wob[:, kc, hh * 512:(hh + 1) * 512], start=(kc == 0), stop=(kc == 7)),
                             r=[bmTg[min(i // 4, 4)], bwob], w=[pb])

            def s2():
                k.op("dve", lambda e: e.tensor_tensor(out=x1[:], in0=ps[:], in1=xt[:], op=ALU.add), r=[pb, xtb], w=[x1b])

            def s3():
                k.dma(SX1[i, :, :], x1[:], r=[x1b], w=[dSX1])
                ssq(junk, x1[:], ss, x1b, bss)

            def s4():
                rstd_ops(ss, sst, srs, bss, brs, D)

            def s5():
                k.op("act", lambda e: e.activation(out=xs[:], in_=x1[:], func=AF.Copy, scale=srs[:, 0:1]),
                     r=[x1b, brs], w=[xsb])

            def s6():
                for kc in range(8):
                    k.op("pe", lambda e, kc=kc: e.transpose(pt[:, kc * 128:(kc + 1) * 128], xs[:, kc * 128:(kc + 1) * 128], ident[:]),
                         r=[xsb, bC], w=[ptb])

            def s7():
                k.op("dve", lambda e: e.tensor_tensor(out=h2T[:, :, i * 128:(i + 1) * 128],
                                                       in0=pt.rearrange("p (a b) -> p a b", b=128), in1=g2b, op=ALU.mult),
                     r=[ptb, bC], w=[bh2[i]])
            return [s0, s1, s2, s3, s4, s5, s6, s7]
        pipe(NCH, pd3)
        k.barrier()

    with contextlib.ExitStack() as ph:
        bh2 = k.buf()
        WF = ring("uwf", 6, [128, 8, 128], F32, ph)
        WB = ring("uwb", 6, [128, 8, 128], BF16, ph)
        WDF = ring("wdf", 3, [128, D], F32, ph)
        AR_ = ring("araw", 2, [128, NTOK], F32, ph)
        AC_ = ring("acv", 2, [128, NTOK], F32, ph)
        GT_ = ring("gte", 3, [128, NTOK], BF16, ph)
        SL_ = ring("sil", 2, [128, NTOK], BF16, ph)
        AO_ = ring("aco", 3, [128, NTOK], BF16, ph)
        wupv = w_up.rearrange("(kc p) c -> p kc c", p=128)
        PSM = psring([0, 1, 2, 3])

        def pe1(mt):
            wl = [(WF.next(), WB.next()) for _ in range(2)]
            araw, arb = AR_.next()
            acv, acb = AC_.next()
            gte, gtb = GT_.next()
            sil, slb = SL_.next()
            aco, aob = AO_.next()
            wdf, wdfb = WDF.next()

            def s0():
                for ((wf, wfb), _), col0 in zip(wl, (mt * 128, FFN + mt * 128)):
                    k.dma(wf[:], wupv[:, :, col0:col0 + 128], w=[wfb])
                k.dma(wdf[:], w_dn[mt * 128:(mt + 1) * 128, :], w=[wdfb])

            def s1():
                for ((wf, wfb), (wb, wbb)) in wl:
                    k.op("dve", lambda e, wf=wf, wb=wb: e.tensor_copy(out=wb[:], in_=wf[:]), r=[wfb], w=[wbb])
                k.op("act", lambda e: e.copy(out=wdb[:, mt, :], in_=wdf[:]), r=[wdfb], w=[bwd[mt]])

            def s2():
                for (t0, tn) in TOKT:
                    psA, pbA = PSM.next()
                    psG, pbG = PSM.next()
                    for (ps, pb, (_, (wb, wbb))) in ((psA, pbA, wl[0]), (psG, pbG, wl[1])):
                        for kc in range(8):
                            k.op("pe", lambda e, kc=kc, ps=ps, wb=wb: e.matmul(ps[:, 0:tn], wb[:, kc, :], h2T[:, kc, t0:t0 + tn],
                                                                                 start=(kc == 0), stop=(kc == 7)), r=[wbb, bh2], w=[pb])
                    k.op("act", lambda e: e.copy(out=araw[:, t0:t0 + tn], in_=psA[:, 0:tn]), r=[pbA], w=[arb])
                    k.op("act", lambda e: e.activation(out=acv[:, t0:t0 + tn], in_=psA[:, 0:tn], func=AF.Identity,
                                                        scale=fcw[:, mt, 1:2], bias=fcw[:, mt, 3:4]), r=[pbA, bC], w=[acb])
                    k.op("dve", lambda e: e.tensor_copy(out=gte[:, t0:t0 + tn], in_=psG[:, 0:tn]), r=[pbG], w=[gtb])

            def s3():
                mac(acv[:, 1:NTOK], araw[:, 0:NTOK - 1], fcw[:, mt, 0:1], acb, arb)
                mac(acv[:, 0:NTOK - 1], araw[:, 1:NTOK], fcw[:, mt, 2:3], acb, arb)

            def s4():
                k.op("act", lambda e: e.activation(out=sil[:], in_=acv[:], func=AF.Silu), r=[acb], w=[slb])

            def s5():
                k.op("dve", lambda e: e.tensor_tensor(out=aco[:], in0=sil[:], in1=gte[:], op=ALU.mult), r=[slb, gtb], w=[aob])

            def s6():
                k.dma(SACT[mt, :, :], aco[:], r=[aob], w=[dSACT])
            return [s0, s1, s2, s3, s4, s5, s6]
        pipe(NFT, pe1)
        k.barrier()
    h2s.close()

    with contextlib.ExitStack() as ph:
        aT = sb("aT", [128, NFT, NTOK], BF16, ph)
        baT = k.buf()
        baTg = k.bufs(len(TOKT), "aTg")
        sactv = SACT.rearrange("m p t -> p m t")
        for gi, (t0, tn) in enumerate(TOKT):
            for mh in range(2):
                k.dma(aT[:, mh * 11:(mh + 1) * 11, t0:t0 + tn], sactv[:, mh * 11:(mh + 1) * 11, t0:t0 + tn],
                      r=[dSACT], w=[baTg[gi]], acc=(mh > 0))
        gfb = (sb("gfb", [128, D], F32, ph), k.buf())
        k.dma(gfb[0][:], gfin_d[0, :].partition_broadcast(128), w=[gfb[1]])
        X1 = ring("x1b_", 3, [128, D], F32, ph)
        X2 = ring("x2_", 4, [128, D], F32, ph)
        OT = ring("ot", 3, [128, D], F32, ph)
        junk = (sb("junk3", [128, D], BF16, ph), k.buf())
        SSs = Ring([(sb("ss3_%d" % i, [128, 1], F32, ph), sb("sst3_%d" % i, [128, 1], F32, ph),
                     sb("srs3_%d" % i, [128, 1], F32, ph), k.buf(), k.buf()) for i in range(4)])
        PSA = psring([0, 1, 2, 3])

        def pe2(i):
            x1, x1b = X1.next()
            ps, pb = PSA.next()
            x2, x2b = X2.next()
            ss, sst, srs, bss, brs = SSs.next()
            ot, otb = OT.next()

            def s0():
                k.dma(x1[:], SX1[i, :, :], r=[dSX1], w=[x1b])

            def s1():
                for hh in range(2):
                    for mt in range(NFT):
                        k.op("pe", lambda e, mt=mt, hh=hh: e.matmul(ps[:, hh * 512:(hh + 1) * 512], aT[:, mt, i * 128:(i + 1) * 128],
                                                                      wdb[:, mt, hh * 512:(hh + 1) * 512], start=(mt == 0), stop=(mt == NFT - 1)),
                             r=[baTg[min(i // 4, 4)], bwd[mt]], w=[pb])

            def s2():
                k.op("dve", lambda e: e.tensor_tensor(out=x2[:], in0=ps[:], in1=x1[:], op=ALU.add), r=[pb, x1b], w=[x2b])

            def s3():
                ssq(junk, x2[:], ss, x2b, bss)

            def s4():
                rstd_ops(ss, sst, srs, bss, brs, D)

            def s5():
                k.op("dve", lambda e: e.scalar_tensor_tensor(out=ot[:], in0=x2[:], scalar=srs[:, 0:1], in1=gfb[0][:],
                                                              op0=ALU.mult, op1=ALU.mult), r=[x2b, brs, gfb[1]], w=[otb])

            def s6():
                k.dma(out_d[i * 128:(i + 1) * 128, :], ot[:], r=[otb])
            return [s0, s1, s2, s3, s4, s5, s6]
        pipe(NCH, pe2)
        k.barrier()
    tail.close()
    es.close()
    return nc


_PROG = {}


def _prep_common(inp):
    f32 = np.float32
    c = {}
    c["w_in"] = np.ascontiguousarray(inp["w_in"][0], dtype=f32)
    c["w_ph"] = np.ascontiguousarray(inp["w_proj_hyena"][0], dtype=f32)
    c["w_ps"] = np.ascontiguousarray(inp["w_proj_sgu"][0], dtype=f32)
    c["w_o"] = np.ascontiguousarray(inp["w_out"][0], dtype=f32)
    c["w_up"] = np.ascontiguousarray(inp["w_up"][0], dtype=f32)
    c["w_dn"] = np.ascontiguousarray(inp["w_down"][0], dtype=f32)
    c["g1T"] = np.ascontiguousarray(inp["norm1_g"][0].reshape(8, 128).T, dtype=f32)
    c["g2T"] = np.ascontiguousarray(inp["norm2_g"][0].reshape(8, 128).T, dtype=f32)
    hw = np.concatenate([inp["hy_conv_w"][0], inp["hy_conv_b"][0][None, :]], axis=0)
    c["hcw"] = np.ascontiguousarray(hw.reshape(4, 24, 128).transpose(2, 1, 0).reshape(128, 96), dtype=f32)
    fw = np.concatenate([inp["ffn_conv_w"][0], inp["ffn_conv_b"][0][None, :]], axis=0)
    c["fcw"] = np.ascontiguousarray(fw.reshape(4, NFT, 128).transpose(2, 1, 0).reshape(128, NFT * 4), dtype=f32)
    c["gfin"] = np.ascontiguousarray(inp["final_g"].reshape(1, D), dtype=f32)
    c["gsgu"] = np.ascontiguousarray(inp["sgu_norm_g"][0].reshape(1, D), dtype=f32)
    c["wsT"] = np.ascontiguousarray(inp["sgu_w"][0].transpose(2, 0, 1).reshape(128, 8 * 128), dtype=f32)
    c["sgub"] = np.ascontiguousarray(inp["sgu_b"][0].reshape(1, 8 * 128), dtype=f32)
    def bd(w):
        r, c_ = w.shape
        o = np.zeros((2 * r, 2 * c_), dtype=f32)
        o[:r, :c_] = w
        o[r:, c_:] = w
        return o
    c["fw1"] = bd(inp["filt_w1"][0])
    c["fw2"] = bd(inp["filt_w2"][0])
    c["fw3"] = bd(inp["filt_w3"][0])
    fv = np.stack([inp["filt_b1"][0], inp["filt_b2"][0], inp["filt_b3"][0], inp["filt_freq"][0]], axis=1)
    c["fvec"] = np.ascontiguousarray(np.concatenate([fv, fv], axis=0), dtype=f32)
    w4 = inp["filt_w4"][0]
    c["fw4"] = np.ascontiguousarray(np.concatenate([w4[:, :D], w4[:, D:]], axis=0), dtype=f32)
    c["dec"] = np.ascontiguousarray(inp["hy_decay"][0].reshape(2, D), dtype=f32)
    c["skip"] = np.ascontiguousarray(inp["hy_skip"][0].reshape(1, D), dtype=f32)
    return c


def kernel(**inputs):
    inp = {k_: np.asarray(v) for k_, v in inputs.items()}
    if "nc" not in _PROG:
        _PROG["nc"] = build_program()
        _PROG["consts"] = [make_consts(h) for h in (0, 1)]
    nc = _PROG["nc"]
    common = _prep_common(inp)
    x = np.asarray(inp["x"], dtype=np.float32)
    in_maps = []
    for core in range(8):
        b, h = core // 2, core % 2
        m = dict(common)
        m["x"] = np.ascontiguousarray(np.roll(x[b], -1920 * h, axis=0))
        cs = _PROG["consts"][h]
        for nm in ("S1u", "S1k", "mats", "Lm", "zT", "negt", "ident", "msk"):
            m["c_" + nm] = cs[nm]
        in_maps.append(m)
    res = run_bass_kernel_spmd(nc, in_maps, core_ids=list(range(8)))
    out = np.empty((4, SEQ, D), dtype=np.float32)
    for core in range(8):
        b, h = core // 2, core % 2
        o = res.results[core]["out"]
        if h == 0:
            out[b, 0:2048] = o[0:2048]
        else:
            out[b, 2048:4096] = o[128:2176]
    return out
```

```python
import contextlib
import numpy as np
import concourse.bass as bass
import concourse.mybir as mybir
from concourse.bass_utils import run_bass_kernel_spmd

F32 = mybir.dt.float32
BF16 = mybir.dt.bfloat16
AF = mybir.ActivationFunctionType
ALU = mybir.AluOpType

D = 1024
SEQ = 4096
NTOK = 2176
NCH = 17
NM1 = 34
FFN = 2816
NFT = 22
EPS = 1e-6
MAGIC = 12582912.0
TWO_PI = float(2 * np.pi)
TOKT = [(0, 512), (512, 512), (1024, 512), (1536, 512), (2048, 128)]


def _blk(a, b, c, d):
    return np.block([[a, b], [c, d]])


def make_consts(h):
    rot = 30 * h
    q = np.arange(64)[:, None, None]
    n2 = np.arange(64)[None, :, None]

    def s1(n1):
        out = np.zeros((n1.shape[0], 64, 128))
        f1 = np.arange(65)[None, None, :]
        phi = 2 * np.pi * (n1 * f1 / 128.0 + n2 * f1 / 8192.0)
        re = np.cos(phi)
        re[:, :, 64] = np.cos(np.pi * n1[:, :, 0]) * np.ones((1, 64))
        out[:, :, 0:65] = re
        f1i = np.arange(1, 64)[None, None, :]
        phii = 2 * np.pi * (n1 * f1i / 128.0 + n2 * f1i / 8192.0)
        out[:, :, 65:128] = -np.sin(phii)
        return out

    S1u0 = s1((q + rot) % 64)
    S1u = np.zeros((128, 64, 128))
    S1u[0::2, 0:32, :] = S1u0[:, 0:32, :]
    S1u[1::2, 32:64, :] = S1u0[:, 32:64, :]
    S1k = s1(np.arange(128)[:, None, None])
    n2v = np.arange(64)[:, None]
    f2 = np.arange(64)[None, :]
    th = 2 * np.pi * n2v * f2 / 64.0
    Fc, Fs = np.cos(th), np.sin(th)
    thp = th + np.pi * n2v / 64.0
    Fcp, Fsp = np.cos(thp), np.sin(thp)
    Z = np.zeros((64, 64))
    S2 = [
        _blk(Fc, -Fs, Fs, Fc), _blk(-Fs, Fc, Fc, Fs), _blk(Fc, Fc, Fs, Fs), _blk(-Fs, -Fs, Fc, Fc),
        _blk(Fc, -Fs, Z, Z), _blk(-Fs, Fc, Z, Z), _blk(Fc, Fc, Z, Z), _blk(-Fs, -Fs, Z, Z),
        _blk(Z, Z, Fcp, -Fsp), _blk(Z, Z, -Fsp, Fcp), _blk(Z, Z, Fcp, Fcp), _blk(Z, Z, -Fsp, -Fsp),
    ]
    f2v = np.arange(64)[:, None]
    m2 = np.arange(64)[None, :]
    ga = 2 * np.pi * m2 * f2v / 64.0
    Gr, Gi = np.cos(ga), np.sin(ga)
    gp = ga + np.pi * m2 / 64.0
    Gc, Gs = np.cos(gp), np.sin(gp)
    T = [
        _blk(Gr, Gi, -Gr, -Gi), _blk(-Gi, Gr, -Gi, Gr),
        _blk(Gr, Z, -Gr, Z), _blk(-Gi, Z, -Gi, Z),
        _blk(Z, Gc, Z, -Gc), _blk(Z, -Gs, Z, -Gs),
    ]
    mats = np.stack(S2 + T, axis=1)
    m1 = (np.arange(NM1) + 30 * h)[None, None, :]
    f1 = np.arange(64)[:, None, None]
    m2b = np.arange(64)[None, :, None]
    psi = 2 * np.pi * (m1 * f1 / 128.0 + m2b * f1 / 8192.0)
    Lre = 2 * np.cos(psi) / 8192.0
    Lim = -2 * np.sin(psi) / 8192.0
    Lre[0] = 1.0 / 8192.0
    Lim[0] = ((-1.0) ** m1[0]) / 8192.0 * np.ones((64, 1))
    Lm = np.concatenate([Lre, Lim], axis=0)
    f32 = np.float32
    t = np.linspace(0.0, 1.0, SEQ, dtype=f32)[:, None]
    bands = np.linspace(1e-4, 15.0, 16, dtype=f32)[None, :]
    phase = (f32(2.0 * np.pi / SEQ) * np.arange(SEQ, dtype=f32)[:, None]) * bands
    z = np.concatenate([t, np.cos(phase), -np.sin(phase)], axis=-1).astype(f32)
    j = np.arange(8192)
    l = np.where(j < 4096, j, 8192 - j)
    l = np.where(j == 4096, 0, l)
    zc = z[l]
    zT2 = np.concatenate([zc[:4096].T, zc[4096:].T], axis=0)
    negt = (-t[l, 0]).reshape(128, 64)
    return dict(
        S1u=S1u.reshape(128, 64 * 128).astype(f32), S1k=S1k.reshape(128, 64 * 128).astype(f32),
        mats=mats.reshape(128, 18 * 128).astype(f32), Lm=Lm.reshape(128, 64 * NM1).astype(f32),
        zT=np.ascontiguousarray(zT2).astype(f32), negt=negt.astype(f32),
        ident=np.eye(128, dtype=f32),
        msk=np.tile(np.array([[1.0 - h, float(h), -float(h)]], dtype=f32), (128, 1)),
    )


class Buf:
    __slots__ = ("name", "w", "r")

    def __init__(self, name):
        self.name = name
        self.w = []
        self.r = []


class Eng:
    def __init__(self, name, e, sem):
        self.name = name
        self.e = e
        self.sem = sem
        self.key = "E" + name
        self.count = 0
        self.waited = {}
        self.slots = []
        self.rr = 0

    def cur(self):
        return (self.key, self.sem, self.count)


class K:
    def __init__(self, nc, es):
        self.nc = nc
        self.es = es
        self.eng = {}
        for name, e in (("pe", nc.tensor), ("act", nc.scalar), ("dve", nc.vector),
                        ("pool", nc.gpsimd), ("sp", nc.sync)):
            self.eng[name] = Eng(name, e, es.enter_context(nc.semaphore("s_" + name)))
        for qn, nslots in (("sp", 24), ("pool", 24)):
            q = self.eng[qn]
            for i in range(nslots):
                q.slots.append([es.enter_context(nc.semaphore("d_%s%d" % (qn, i))), 0, "D%s%d" % (qn, i)])
        self.nbuf = 0

    def buf(self, name=None):
        self.nbuf += 1
        return Buf(name or ("b%d" % self.nbuf))

    def bufs(self, n, name="b"):
        return [self.buf("%s%d" % (name, i)) for i in range(n)]

    def _wait(self, en, ev):
        if ev[2] <= 0:
            return
        if en.name == "pe" and ev[0] == en.key:
            return
        if en.waited.get(ev[0], 0) >= ev[2]:
            return
        en.e.wait_ge(ev[1], ev[2])
        en.waited[ev[0]] = ev[2]

    def _deps(self, en, r, w, acc=False):
        for b in r:
            for ev in b.w:
                self._wait(en, ev)
        for b in w:
            if not acc:
                for ev in b.w:
                    self._wait(en, ev)
            for ev in b.r:
                self._wait(en, ev)

    @staticmethod
    def _addr(lst, ev):
        for i, o in enumerate(lst):
            if o[0] == ev[0]:
                if o[2] < ev[2]:
                    lst[i] = ev
                return
        lst.append(ev)

    def _commit(self, ev, r, w, acc=False):
        for b in r:
            self._addr(b.r, ev)
        for b in w:
            if acc:
                self._addr(b.w, ev)
            else:
                b.w = [ev]
                b.r = []

    def op(self, engname, fn, r=(), w=()):
        en = self.eng[engname]
        self._deps(en, r, w)
        inst = fn(en.e)
        en.count += 1
        inst.then_inc(en.sem, 1)
        self._commit(en.cur(), r, w)

    def dma(self, out, in_, r=(), w=(), q=None, acc=False):
        to_dram = "DRam" in type(out.tensor).__name__
        q = "pool" if (q == "cast" or to_dram) else "sp"
        acc = acc or to_dram
        en = self.eng[q]
        self._deps(en, r, w, acc)
        slot = en.slots[en.rr]
        en.rr = (en.rr + 1) % len(en.slots)
        self._wait(en, (slot[2], slot[0], slot[1]))
        en.e.dma_start(out=out, in_=in_).then_inc(slot[0], 16)
        slot[1] += 16
        ev = (slot[2], slot[0], slot[1])
        self._commit(ev, r, w, acc)
        return ev

    def all_events(self):
        evs = [e.cur() for e in self.eng.values()]
        for e in self.eng.values():
            for s in e.slots:
                evs.append((s[2], s[0], s[1]))
        return evs

    def barrier(self, engs=("pe", "act", "dve", "pool", "sp")):
        evs = self.all_events()
        for n in engs:
            for ev in evs:
                if ev[0] != self.eng[n].key:
                    self._wait(self.eng[n], ev)


class Ring:
    def __init__(self, items):
        self.items = items
        self.i = 0

    def next(self):
        it = self.items[self.i]
        self.i = (self.i + 1) % len(self.items)
        return it


def pipe(n, make, first=()):
    its = [make(i) for i in range(n)]
    ns = max(len(x) for x in its)
    order = list(first) + [st for st in range(ns - 1, -1, -1) if st not in first]
    for step in range(n + ns - 1):
        for st in order:
            i = step - st
            if 0 <= i < n and st < len(its[i]) and its[i][st] is not None:
                its[i][st]()


DEBUG_OUT = set()


def build_program():
    nc = bass.Bass("TRN2", target_bir_lowering=False)
    es = contextlib.ExitStack()
    k = K(nc, es)

    def din(name, shape, dt=F32):
        return nc.dram_tensor(name, list(shape), dt, kind="ExternalInput").ap()

    def dscr(name, shape, dt=BF16):
        return nc.dram_tensor(name, list(shape), dt, kind=("ExternalOutput" if name in DEBUG_OUT else "Internal")).ap()

    x_in = din("x", [SEQ, D])
    w_in = din("w_in", [D, 7168])
    w_ph = din("w_ph", [D, D])
    w_ps = din("w_ps", [D, D])
    w_o = din("w_o", [D, D])
    w_up = din("w_up", [D, 2 * FFN])
    w_dn = din("w_dn", [FFN, D])
    g1T_d = din("g1T", [128, 8])
    g2T_d = din("g2T", [128, 8])
    hcw_d = din("hcw", [128, 24 * 4])
    fcw_d = din("fcw", [128, NFT * 4])
    gfin_d = din("gfin", [1, D])
    gsgu_d = din("gsgu", [1, D])
    wsT_d = din("wsT", [128, 8 * 128])
    sgub_d = din("sgub", [1, 8 * 128])
    fw1_d = din("fw1", [66, 128])
    fw2_d = din("fw2", [128, 128])
    fw3_d = din("fw3", [128, 128])
    fvec_d = din("fvec", [128, 4])
    fw4_d = din("fw4", [128, D])
    dec_d = din("dec", [2, D])
    skip_d = din("skip", [1, D])
    c_S1u = din("c_S1u", [128, 64 * 128])
    c_S1k = din("c_S1k", [128, 64 * 128])
    c_mats = din("c_mats", [128, 18 * 128])
    c_Lm = din("c_Lm", [128, 64 * NM1])
    c_zT = din("c_zT", [66, 4096])
    c_negt = din("c_negt", [128, 64])
    c_ident = din("c_ident", [128, 128])
    c_msk = din("c_msk", [128, 3])
    out_d = nc.dram_tensor("out", [NTOK, D], F32, kind="ExternalOutput").ap()

    SAk = dscr("SAk", [128, 64, D])
    SU = dscr("SU", [8, 128, SEQ])
    SX0 = dscr("SX0", [8, 128, NTOK])
    SG = dscr("SG", [16, 128, NTOK])
    SUS = dscr("SUS", [8, 128, NTOK])
    SVS = dscr("SVS", [NCH, 128, D])
    SA = dscr("SA", [128, 64, D])
    SC = dscr("SC", [64, 128, D])
    SYA = dscr("SYA", [8, 128, NTOK])
    SYB = dscr("SYB", [8, 128, NTOK])
    SM = dscr("SM", [8, 128, NTOK])
    SX1 = dscr("SX1", [NCH, 128, D], F32)
    SH2 = dscr("SH2", [8, 128, NTOK])
    SACT = dscr("SACT", [NFT, 128, NTOK])
    dSAk, dSK, dSU, dSX0, dSG, dSUS, dSVS, dSA, dSC, dSYA, dSYB, dSM, dSX1, dSH2, dSACT = k.bufs(15, "dram")

    def sb(name, shape, dt=F32, stack=es):
        return stack.enter_context(nc.sbuf_tensor("sb_" + name, list(shape), dt))

    def ring(name, n, shape, dt, stack):
        return Ring([(sb("%s%d" % (name, i), shape, dt, stack), k.buf()) for i in range(n)])

    pst = [es.enter_context(nc.psum_tensor("ps%d" % i, [128, 1024], F32)) for i in range(4)]
    PS = [(pst[i], k.buf("ps%d" % i)) for i in range(4)]

    def psring(idx):
        return Ring([PS[i] for i in idx])

    PSHB = k.bufs(8, "psh")

    def hring(idx):
        return Ring([(pst[i // 2][:, (i % 2) * 512:(i % 2 + 1) * 512], PSHB[i]) for i in idx])

    def ptring(idx):
        return Ring([(pst[i][:].bitcast(BF16)[:, hh * 1024:(hh + 1) * 1024], PS[i][1]) for i in idx for hh in range(1)])

    ident = sb("ident", [128, 128], BF16)
    mats = sb("mats", [128, 18, 128], BF16)
    Lm = sb("Lm", [128, 64, NM1], BF16)
    g1T = sb("g1T", [128, 8])
    g2T = sb("g2T", [128, 8])
    hcw = sb("hcw", [128, 24, 4])
    hcm = sb("hcm", [128, 24, 4])
    fcw = sb("fcw", [128, NFT, 4])
    msk = sb("msk", [128, 3])
    bC = k.buf("consts")
    k.dma(ident[:], c_ident[:, :], w=[bC], q="cast")
    k.dma(mats[:].rearrange("p a b -> p (a b)"), c_mats[:, :], w=[bC], q="cast", acc=True)
    k.dma(Lm[:].rearrange("p a b -> p (a b)"), c_Lm[:, :], w=[bC], q="cast", acc=True)
    for t_, d_ in ((g1T, g1T_d), (g2T, g2T_d), (msk, c_msk)):
        k.dma(t_[:], d_[:, :], w=[bC], acc=True)
    k.dma(hcw[:].rearrange("p a b -> p (a b)"), hcw_d[:, :], w=[bC], acc=True)
    k.dma(fcw[:].rearrange("p a b -> p (a b)"), fcw_d[:, :], w=[bC], acc=True)
    bC2 = k.buf("consts2")
    for i_, (wi, mi) in enumerate(((0, 2), (2, 2), (0, 1), (2, 1))):
        k.op("dve", lambda e, i_=i_, wi=wi, mi=mi: e.tensor_scalar(
            out=hcm[:, :, i_:i_ + 1], in0=hcw[:, :, wi:wi + 1], scalar1=msk[:, mi:mi + 1], scalar2=None,
            op0=ALU.mult), r=[bC], w=[bC2])

    def rstd_ops(ss, tmp, rs, bss, brs, n):
        k.op("dve", lambda e: e.tensor_scalar(out=tmp[:], in0=ss[:], scalar1=1.0 / n, scalar2=EPS,
                                               op0=ALU.mult, op1=ALU.add), r=[bss], w=[brs])
        k.op("act", lambda e: e.activation(out=tmp[:], in_=tmp[:], func=AF.Sqrt), r=[brs], w=[brs])
        k.op("dve", lambda e: e.reciprocal(out=rs[:], in_=tmp[:]), r=[brs], w=[brs])

    def mac(acc_ap, in_ap, sc_ap, bacc, bin_):
        k.op("dve", lambda e: e.scalar_tensor_tensor(out=acc_ap, in0=in_ap, scalar=sc_ap, in1=acc_ap,
                                                      op0=ALU.mult, op1=ALU.add), r=[bin_, bacc, bC, bC2], w=[bacc])

    def conv_taps(xc, praw, bxc, bpr, cw, ci, segs):
        for (a, b) in segs:
            mac(xc[:, a + 1:b], praw[:, a:b - 1], cw[:, ci, 0:1], bxc, bpr)
            mac(xc[:, a:b - 1], praw[:, a + 1:b], cw[:, ci, 2:3], bxc, bpr)

    def ssq(junk, src_ap, ss, bsrc, bss):
        k.op("act", lambda e: e.activation(out=junk[0][:], in_=src_ap, func=AF.Square, accum_out=ss[:]),
             r=[bsrc], w=[junk[1], bss])

    with contextlib.ExitStack() as ph:
        zT = sb("zT", [66, 4096], F32, ph)
        fw1 = sb("fw1", [66, 128], F32, ph)
        fw2 = sb("fw2", [128, 128], F32, ph)
        fw3 = sb("fw3", [128, 128], F32, ph)
        fvec = sb("fvec", [128, 4], F32, ph)
        fw4f = sb("fw4f", [128, D], F32, ph)
        fw4 = sb("fw4", [128, D], BF16, ph)
        decb = sb("decb", [128, D], F32, ph)
        skipr = sb("skipr", [1, D], F32, ph)
        negt = sb("negt", [128, 64], F32, ph)
        S1k = sb("S1k", [128, 64, 128], BF16, ph)
        h3 = sb("h3", [128, 8192], BF16, ph)
        bF = k.buf("fconst")
        k.dma(zT[:], c_zT[:, :], w=[bF])
        k.dma(fw1[:], fw1_d[:, :], w=[bF], acc=True)
        k.dma(fw2[:], fw2_d[:, :], w=[bF], acc=True)
        k.dma(fw3[:], fw3_d[:, :], w=[bF], acc=True)
        k.dma(fvec[:], fvec_d[:, :], w=[bF], acc=True)
        k.dma(fw4f[:], fw4_d[:, :], w=[bF], acc=True)
        k.dma(decb[0:64, :], dec_d[0, :].partition_broadcast(64), w=[bF], acc=True)
        k.dma(decb[64:128, :], dec_d[1, :].partition_broadcast(64), w=[bF], acc=True)
        k.dma(skipr[:], skip_d[:, :], w=[bF], acc=True)
        k.dma(negt[:], c_negt[:, :], w=[bF], acc=True)
        k.dma(S1k[:].rearrange("p a b -> p (a b)"), c_S1k[:, :], w=[bF], q="cast", acc=True)
        bF2 = k.buf("fconst2")
        bh3 = k.buf("h3")
        k.op("dve", lambda e: e.tensor_copy(out=fw4[:], in_=fw4f[:]), r=[bF], w=[bF2])
        k.op("act", lambda e: e.activation(out=decb[:], in_=decb[:], func=AF.Abs), r=[bF], w=[bF])
        k.op("pool", lambda e: e.memset(h3[:], 0.0), w=[bh3])
        fab = sb("fab", [128, 3], F32, ph)
        bfab = k.buf("fab")
        k.op("dve", lambda e: e.tensor_scalar(out=fab[:], in0=fvec[:, 0:3], scalar1=fvec[:, 3:4], scalar2=None, op0=ALU.mult),
             r=[bF], w=[bfab])
        AR = ring("farg", 3, [128, 1024], F32, ph)
        RR = ring("frr", 3, [128, 1024], F32, ph)
        HS = [ring("fh0_", 2, [128, 1024], F32, ph), ring("fh1_", 2, [128, 1024], F32, ph)]
        PL = [psring([0]), psring([1]), psring([2])]

        def ffn_chunk(c):
            cs = slice(c * 1024, (c + 1) * 1024)
            stages = []
            prev = [None]
            for layer, (wmat, kk) in enumerate(((fw1, 66), (fw2, 128), (fw3, 128))):
                ps, pb = PL[layer].next()
                a_t, a_b = AR.next()
                r_t, r_b = RR.next()
                h_t, h_b = HS[layer].next() if layer < 2 else (None, None)

                def mm(layer=layer, wmat=wmat, ps=ps, pb=pb, src=prev[0]):
                    for hh in range(2):
                        hs = slice(hh * 512, (hh + 1) * 512)
                        if layer == 0:
                            rhs, rb = zT[:, c * 1024 + hh * 512:c * 1024 + (hh + 1) * 512], bF
                        else:
                            rhs, rb = src[0][:, hs], src[1]
                        k.op("pe", lambda e: e.matmul(ps[:, hs], wmat[:, :], rhs, start=True, stop=True), r=[bF, rb], w=[pb])

                def red(layer=layer, ps=ps, pb=pb, a_t=a_t, a_b=a_b, r_t=r_t, r_b=r_b):
                    k.op("act", lambda e: e.activation(out=a_t[:], in_=ps[:], func=AF.Identity, scale=fvec[:, 3:4],
                                                        bias=fab[:, layer:layer + 1]), r=[pb, bF, bfab], w=[a_b])
                    k.op("dve", lambda e: e.tensor_scalar(out=r_t[:], in0=a_t[:], scalar1=1.0 / TWO_PI, scalar2=MAGIC,
                                                           op0=ALU.mult, op1=ALU.add), r=[a_b], w=[r_b])
                    k.op("dve", lambda e: e.tensor_scalar(out=r_t[:], in0=r_t[:], scalar1=MAGIC, scalar2=TWO_PI,
                                                           op0=ALU.subtract, op1=ALU.mult), r=[r_b], w=[r_b])
                    k.op("dve", lambda e: e.tensor_tensor(out=a_t[:], in0=a_t[:], in1=r_t[:], op=ALU.subtract),
                         r=[r_b, a_b], w=[a_b])
                    k.op("dve", lambda e: e.tensor_scalar(out=a_t[:], in0=a_t[:], scalar1=-3.1415925, scalar2=3.1415925,
                                                           op0=ALU.max, op1=ALU.min), r=[a_b], w=[a_b])

                def sn(layer=layer, a_t=a_t, a_b=a_b, h_t=h_t, h_b=h_b):
                    if layer < 2:
                        k.op("act", lambda e: e.activation(out=h_t[:], in_=a_t[:], func=AF.Sin), r=[a_b], w=[h_b])
                    else:
                        k.op("act", lambda e: e.activation(out=h3[0:64, cs], in_=a_t[0:64, :], func=AF.Sin), r=[a_b], w=[bh3])
                        k.op("act", lambda e: e.activation(out=h3[64:128, 4096 + c * 1024:4096 + (c + 1) * 1024],
                                                            in_=a_t[64:128, :], func=AF.Sin), r=[a_b], w=[bh3])
                stages += [mm, red, sn]
                prev[0] = (h_t, h_b)
            return stages
        pipe(4, ffn_chunk)
        k.op("pool", lambda e: e.memset(h3[64:128, 4096:4097], 0.0), w=[bh3])
        if "DBG_h3" in DEBUG_OUT:
            dbg_h3 = nc.dram_tensor("DBG_h3", [128, 8192], BF16, kind="ExternalOutput").ap()
            k.dma(dbg_h3[:, :], h3[:], r=[bh3])
        h3v = h3[:].rearrange("p (a b) -> p a b", b=64)
        Es = ring("fE", 2, [128, D], F32, ph)
        KT = ring("fkT", 3, [128, D], BF16, ph)
        KF = (sb("fkF", [128, D], F32, ph), k.buf())
        AK = ring("fAk", 3, [128, D], BF16, ph)
        PA_, PB_ = psring([0, 1]), psring([2, 3])

        def st1(n2):
            ps, pb = PA_.next()
            ps2, pb2 = PB_.next()
            E_t, E_b = Es.next()
            kt, ktb = KT.next()
            ak, akb = AK.next()

            def s0():
                for cc in range(2):
                    k.op("pe", lambda e, cc=cc: e.matmul(ps[:, cc * 512:(cc + 1) * 512], h3v[:, :, n2],
                                                          fw4[:, cc * 512:(cc + 1) * 512], start=True, stop=True),
                         r=[bh3, bF2], w=[pb])
                k.op("act", lambda e: e.activation(out=E_t[:], in_=decb[:], func=AF.Exp, scale=negt[:, n2:n2 + 1]),
                     r=[bF], w=[E_b])

            def s1():
                if n2 == 0:
                    kf, kfb = KF
                    k.op("dve", lambda e: e.tensor_tensor(out=kf[:], in0=ps[:], in1=E_t[:], op=ALU.mult), r=[pb, E_b], w=[kfb])
                    k.op("dve", lambda e: e.tensor_tensor(out=kf[0:1, :], in0=kf[0:1, :], in1=skipr[0:1, :], op=ALU.add),
                         r=[kfb, bF], w=[kfb])
                    k.op("dve", lambda e: e.tensor_copy(out=kt[:], in_=kf[:]), r=[kfb], w=[ktb])
                else:
                    k.op("dve", lambda e: e.tensor_tensor(out=kt[:], in0=ps[:], in1=E_t[:], op=ALU.mult), r=[pb, E_b], w=[ktb])

            def s2():
                for cc in range(2):
                    k.op("pe", lambda e, cc=cc: e.matmul(ps2[:, cc * 512:(cc + 1) * 512], S1k[:, n2, :],
                                                          kt[:, cc * 512:(cc + 1) * 512], start=True, stop=True),
                         r=[ktb, bF], w=[pb2])

            def s3():
                if n2 % 2:
                    k.op("act", lambda e: e.copy(out=ak[:], in_=ps2[:]), r=[pb2], w=[akb])
                else:
                    k.op("dve", lambda e: e.tensor_copy(out=ak[:], in_=ps2[:]), r=[pb2], w=[akb])

            def s4():
                k.dma(SAk[:, n2, :], ak[:], r=[akb], w=[dSAk])
            return [s0, s1, s2, s3, s4]
        pipe(64, st1)
        k.barrier()

    with contextlib.ExitStack() as ph:
        hT = sb("hT", [128, 8, SEQ], BF16, ph)
        bhT = k.bufs(32, "hT")
        junk = (sb("junk", [128, D], BF16, ph), k.buf())
        SSs = Ring([(sb("ss%d" % i, [128, 1], F32, ph), sb("sst%d" % i, [128, 1], F32, ph),
                     sb("srs%d" % i, [128, 1], F32, ph), k.buf(), k.buf()) for i in range(4)])
        WF = ring("wf", 3, [128, 8, 128], F32, ph)
        WB = ring("wb", 3, [128, 8, 128], BF16, ph)
        phA = contextlib.ExitStack()
        XT = ring("xt", 4, [128, D], F32, phA)
        XS = ring("xs", 3, [128, D], BF16, phA)
        g1b = g1T[:, :].unsqueeze(2).to_broadcast([128, 8, 128])
        PT = ptring([2, 3])

        def pa(i):
            xt, xtb = XT.next()
            ss, sst, srs, bss, brs = SSs.next()
            xs, xsb = XS.next()
            pt, ptb = PT.next()

            def s0():
                k.dma(xt[:], x_in[i * 128:(i + 1) * 128, :], w=[xtb])

            def s1():
                ssq(junk, xt[:], ss, xtb, bss)

            def s2():
                rstd_ops(ss, sst, srs, bss, brs, D)

            def s3():
                if i % 2:
                    k.op("act", lambda e: e.activation(out=xs[:], in_=xt[:], func=AF.Copy, scale=srs[:, 0:1]),
                         r=[xtb, brs], w=[xsb])
                else:
                    k.op("dve", lambda e: e.tensor_scalar(out=xs[:], in0=xt[:], scalar1=srs[:, 0:1], scalar2=None, op0=ALU.mult),
                         r=[xtb, brs], w=[xsb])

            def s4():
                for kc in range(8):
                    k.op("pe", lambda e, kc=kc: e.transpose(pt[:, kc * 128:(kc + 1) * 128], xs[:, kc * 128:(kc + 1) * 128], ident[:]),
                         r=[xsb, bC], w=[ptb])

            def s5():
                k.op("dve", lambda e: e.tensor_tensor(out=hT[:, :, i * 128:(i + 1) * 128],
                                                       in0=pt.rearrange("p (a b) -> p a b", b=128), in1=g1b, op=ALU.mult),
                     r=[ptb, bC], w=[bhT[i]])
            return [s0, s1, s2, s3, s4, s5]
        pipe(32, pa)
        k.barrier()
        phA.close()
        w_in_v = w_in.rearrange("(kc p) c -> p kc c", p=128)
        PSM = psring([0, 1, 2, 3])

        def proj_fm(wb, wbb, t0, tn, ps, pb, half):
            deps = [bhT[j] for j in range(t0 // 128, (t0 + tn + 127) // 128)]
            for kc in range(8):
                k.op("pe", lambda e, kc=kc: e.matmul(ps[:, half * 512:half * 512 + tn], wb[:, kc, :], hT[:, kc, t0:t0 + tn],
                                                      start=(kc == 0), stop=(kc == 7)),
                     r=[wbb] + deps, w=[pb])

        EXT3 = [([(0, 512), (512, 512)], slice(0, 1024), slice(0, 1024)),
                ([(1024, 512), (1536, 512)], slice(1024, 2048), slice(0, 1024)),
                ([(2048, 128)], slice(2048, NTOK), slice(0, 128))]

        phB = contextlib.ExitStack()
        PR = [(sb("praw%d" % i, [128, SEQ], F32, phB), k.buf()) for i in range(2)]
        XC = [(sb("xc%d" % i, [128, SEQ], F32, phB), k.buf()) for i in range(2)]
        UC = ring("uc", 2, [128, SEQ], BF16, phB)
        X0 = ring("x0o", 2, [128, NTOK], BF16, phB)
        GO = ring("go", 3, [128, NTOK], BF16, phB)

        def pb1(it):
            ct, which = it // 2, it % 2
            col0 = (1024 if which == 0 else 2048) + ct * 128
            ci = (8 if which == 0 else 16) + ct
            wf, wfb = WF.next()
            wb, wbb = WB.next()
            praw, bpr = PR[which]
            xc, bxc = XC[which]
            uc, ucb = UC.next() if which == 1 else (None, None)

            def s0():
                k.dma(wf[:], w_in_v[:, :, col0:col0 + 128], w=[wfb])

            def s1():
                k.op("dve", lambda e: e.tensor_copy(out=wb[:], in_=wf[:]), r=[wfb], w=[wbb])

            def s2():
                for tt in range(4):
                    ps, pb = PSM.next()
                    for half in range(2):
                        proj_fm(wb, wbb, tt * 1024 + half * 512, 512, ps, pb, half)
                    sl = slice(tt * 1024, (tt + 1) * 1024)
                    k.op("act", lambda e: e.copy(out=praw[:, sl], in_=ps[:]), r=[pb], w=[bpr])
                    k.op("act", lambda e: e.activation(out=xc[:, sl], in_=ps[:], func=AF.Identity,
                                                        scale=hcw[:, ci, 1:2], bias=hcw[:, ci, 3:4]), r=[pb, bC], w=[bxc])

            def s3():
                conv_taps(xc, praw, bxc, bpr, hcw, ci, [(0, SEQ)])
                for (dst, src, mi) in ((NTOK, NTOK - 1, 0), (NTOK - 1, NTOK, 1), (0, SEQ - 1, 2), (SEQ - 1, 0, 3)):
                    mac(xc[:, dst:dst + 1], praw[:, src:src + 1], hcm[:, ci, mi:mi + 1], bxc, bpr)
                if which == 1:
                    k.op("dve", lambda e: e.tensor_tensor(out=uc[:], in0=XC[0][0][:], in1=XC[1][0][:], op=ALU.mult),
                         r=[XC[0][1], XC[1][1]], w=[ucb])

            def s4():
                if which == 1:
                    k.dma(SU[ct, :, :], uc[:], r=[ucb], w=[dSU])
            return [s0, s1, s2, s3, s4]
        pipe(16, pb1, first=(1,))

        def pb2(ct):
            wf, wfb = WF.next()
            wb, wbb = WB.next()
            praw, bpr = PR[ct % 2]
            xc, bxc = XC[ct % 2]
            xo, xob = X0.next()

            def s0():
                k.dma(wf[:], w_in_v[:, :, ct * 128:(ct + 1) * 128], w=[wfb])

            def s1():
                k.op("dve", lambda e: e.tensor_copy(out=wb[:], in_=wf[:]), r=[wfb], w=[wbb])

            def s2():
                for (halves, sl, psl) in EXT3:
                    ps, pb = PSM.next()
                    for hi, (t0, tn) in enumerate(halves):
                        proj_fm(wb, wbb, t0, tn, ps, pb, hi)
                    k.op("act", lambda e: e.copy(out=praw[:, sl], in_=ps[:, psl]), r=[pb], w=[bpr])
                    k.op("act", lambda e: e.activation(out=xc[:, sl], in_=ps[:, psl], func=AF.Identity,
                                                        scale=hcw[:, ct, 1:2], bias=hcw[:, ct, 3:4]), r=[pb, bC], w=[bxc])

            def s3():
                conv_taps(xc, praw, bxc, bpr, hcw, ct, [(0, NTOK)])
                k.op("dve", lambda e: e.tensor_copy(out=xo[:], in_=xc[:, 0:NTOK]), r=[bxc], w=[xob])

            def s4():
                k.dma(SX0[ct, :, :], xo[:], r=[xob], w=[dSX0])
            return [s0, s1, s2, s3, s4]
        pipe(8, pb2, first=(1,))

        def pb3(mt):
            if mt < 16:
                col0, func, dst, dbuf = 5120 + mt * 128, AF.Sigmoid, SG[mt, :, :], dSG
            else:
                col0, func, dst, dbuf = 3072 + (mt - 16) * 128, AF.Gelu, SUS[mt - 16, :, :], dSUS
            wf, wfb = WF.next()
            wb, wbb = WB.next()
            go, gob = GO.next()

            def s0():
                k.dma(wf[:], w_in_v[:, :, col0:col0 + 128], w=[wfb])

            def s1():
                k.op("dve", lambda e: e.tensor_copy(out=wb[:], in_=wf[:]), r=[wfb], w=[wbb])

            def s2():
                for (halves, sl, psl) in EXT3:
                    ps, pb = PSM.next()
                    for hi, (t0, tn) in enumerate(halves):
                        proj_fm(wb, wbb, t0, tn, ps, pb, hi)
                    k.op("act", lambda e: e.activation(out=go[:, sl], in_=ps[:, psl], func=func), r=[pb], w=[gob])

            def s3():
                k.dma(dst, go[:], r=[gob], w=[dbuf])
            return [s0, s1, s2, s3]
        pipe(24, pb3)
        k.barrier()
        phB.close()

        WV = ring("wvf", 2, [128, 8, 512], F32, ph)
        WVb = (sb("wvb", [128, 8, D], BF16, ph), k.buf())
        for hh in range(2):
            wv, wvb_ = WV.next()
            k.dma(wv[:], w_in_v[:, :, 4096 + hh * 512:4096 + (hh + 1) * 512], w=[wvb_])
            k.op("dve", lambda e, hh=hh: e.tensor_copy(out=WVb[0][:, :, hh * 512:(hh + 1) * 512], in_=wv[:]),
                 r=[wvb_], w=[WVb[1]])
        gsg = (sb("gsg", [128, D], F32, ph), k.buf())
        k.dma(gsg[0][:], gsgu_d[0, :].partition_broadcast(128), w=[gsg[1]])
        GV = ring("gv", 4, [128, D], F32, ph)
        VO = ring("vo", 3, [128, D], BF16, ph)
        PV = psring([0, 1])

        def pb5(tchunk):
            ps, pb = PV.next()
            gv, gvb = GV.next()
            ss, sst, srs, bss, brs = SSs.next()
            vo, vob = VO.next()

            def s0():
                for hh in range(2):
                    for kc in range(8):
                        k.op("pe", lambda e, kc=kc, hh=hh: e.matmul(ps[:, hh * 512:(hh + 1) * 512],
                                                                      hT[:, kc, tchunk * 128:(tchunk + 1) * 128],
                                                                      WVb[0][:, kc, hh * 512:(hh + 1) * 512],
                                                                      start=(kc == 0), stop=(kc == 7)),
                             r=[WVb[1], bhT[tchunk]], w=[pb])

            def s1():
                k.op("act", lambda e: e.activation(out=gv[:], in_=ps[:], func=AF.Gelu), r=[pb], w=[gvb])

            def s2():
                ssq(junk, gv[:], ss, gvb, bss)

            def s3():
                rstd_ops(ss, sst, srs, bss, brs, D)

            def s4():
                k.op("dve", lambda e: e.scalar_tensor_tensor(out=vo[:], in0=gv[:], scalar=srs[:, 0:1], in1=gsg[0][:],
                                                              op0=ALU.mult, op1=ALU.mult), r=[gvb, brs, gsg[1]], w=[vob])

            def s5():
                k.dma(SVS[tchunk, :, :], vo[:], r=[vob], w=[dSVS])
            return [s0, s1, s2, s3, s4, s5]
        pipe(NCH, pb5)
        k.barrier()

    mid = contextlib.ExitStack()
    yaa = sb("yaa", [128, 8, NTOK], BF16, mid)
    ybT = sb("ybT", [128, 8, NTOK], BF16, mid)
    bya, byb = k.bufs(8, "yaa"), k.bufs(8, "ybT")
    with contextlib.ExitStack() as ph:
        S1u = sb("S1u", [128, 64, 128], BF16, ph)
        bS = k.buf("S1u")
        k.dma(S1u[:].rearrange("p a b -> p (a b)"), c_S1u[:, :], w=[bS], q="cast")
        with contextlib.ExitStack() as ph2:
            uall = sb("uall", [128, 8, SEQ], BF16, ph2)
            bU = k.buf("uall")
            for ct in range(8):
                k.dma(uall[:, ct, :], SU[ct, :, :], r=[dSU], w=[bU], acc=(ct > 0))
            uv2 = uall[:].rearrange("p c (m b) -> p c m b", b=32)
            UT = ring("uT", 3, [128, D], BF16, ph2)
            AO = ring("Ao", 4, [128, D], BF16, ph2)
            PT = ptring([2, 3])
            PSA, PSB2 = psring([0]), psring([1])

            def pc1(pi):
                pt, ptb = PT.next()
                ut, utb = UT.next()
                psa, pab = PSA.next()
                psb, pbb = PSB2.next()
                aoa, aoab = AO.next()
                aob_, aobb = AO.next()
                n2a, n2b = pi, pi + 32

                def s0():
                    for ct in range(8):
                        k.op("pe", lambda e, ct=ct: e.transpose(pt[:, ct * 128:(ct + 1) * 128], uv2[:, ct, :, pi], ident[:]),
                             r=[bU, bC], w=[ptb])

                def s1():
                    k.op("dve", lambda e: e.tensor_copy(out=ut[:], in_=pt[:, :]), r=[ptb], w=[utb])

                def s2():
                    for (ps, pb, n2) in ((psa, pab, n2a), (psb, pbb, n2b)):
                        for cc in range(2):
                            k.op("pe", lambda e, cc=cc, ps=ps, n2=n2: e.matmul(
                                ps[:, cc * 512:(cc + 1) * 512], S1u[:, n2, :],
                                ut[:, cc * 512:(cc + 1) * 512], start=True, stop=True), r=[utb, bS], w=[pb])

                def s3():
                    k.op("act", lambda e: e.copy(out=aoa[:], in_=psa[:]), r=[pab], w=[aoab])
                    k.op("dve", lambda e: e.tensor_copy(out=aob_[:], in_=psb[:]), r=[pbb], w=[aobb])

                def s4():
                    k.dma(SA[:, n2a, :], aoa[:], r=[aoab], w=[dSA])
                    k.dma(SA[:, n2b, :], aob_[:], r=[aobb], w=[dSA])
                return [s0, s1, s2, s3, s4]
            pipe(32, pc1)
            k.barrier()
        BT = ring("Bt", 3, [128, D], BF16, ph)
        BKT = ring("Bkt", 3, [128, D], BF16, ph)
        KC = ring("Kc", 3, [128, 512], BF16, ph)
        Q1 = ring("Q1_", 3, [128, D], BF16, ph)
        CO = ring("Co", 3, [128, D], BF16, ph)
        QK, QC = hring([0, 1]), hring([2, 3])
        QZ = Ring([(pst[2], k.buf()), (pst[3], k.buf())])
        tiles = {}

        def get_tiles(t):
            if t not in tiles:
                tiles[t] = (BT.next(), BKT.next(), CO.next())
            return tiles[t]

        def load_t(t):
            (bt, btb), (bkt, bktb), _ = get_tiles(t)
            k.dma(bt[0:64, :], SA[t, :, :], r=[dSA], w=[btb])
            k.dma(bt[64:128, :], SA[64 + t, :, :], r=[dSA], w=[btb], acc=True)
            k.dma(bkt[0:64, :], SAk[t, :, :], r=[dSAk], w=[bktb])
            k.dma(bkt[64:128, :], SAk[64 + t, :, :], r=[dSAk], w=[bktb], acc=True)

        def front(t, cc, ma, mb):
            (bt, btb), (bkt, bktb), _ = get_tiles(t)
            hs = slice(cc * 512, (cc + 1) * 512)
            qk, qkb = QK.next()
            zz, zzb = QZ.next()
            kc, kcb = KC.next()
            qq, qqb = Q1.next()

            def mm():
                k.op("pe", lambda e: e.matmul(qk[:], mats[:, ma, :], bkt[:, hs], start=True, stop=True), r=[bktb, bC], w=[qkb])
                k.op("pe", lambda e: e.matmul(zz[:, 0:512], mats[:, ma, :], bt[:, hs], start=True, stop=True), r=[btb, bC], w=[zzb])
                k.op("pe", lambda e: e.matmul(zz[:, 512:1024], mats[:, mb, :], bt[:, hs], start=True, stop=True), r=[btb, bC], w=[zzb])

            def cp():
                k.op("act", lambda e: e.copy(out=kc[:], in_=qk[:]), r=[qkb], w=[kcb])

            def pr():
                k.op("dve", lambda e: e.tensor_tensor(out=qq[:].rearrange("p (a b) -> p a b", a=2),
                                                       in0=zz[:].rearrange("p (a b) -> p a b", a=2),
                                                       in1=kc[:].unsqueeze(1).to_broadcast([128, 2, 512]), op=ALU.mult),
                     r=[zzb, kcb], w=[qqb])
            return mm, cp, pr, (qq[:, 0:512], qqb), (qq[:, 512:1024], qqb)

        def back(t, cc, plist):
            _, _, (co, cob) = get_tiles(t)
            hs = slice(cc * 512, (cc + 1) * 512)
            qc, qcb = QC.next()

            def imm():
                for pi, (pp, ppb, tm) in enumerate(plist):
                    k.op("pe", lambda e, pp=pp, tm=tm, pi=pi: e.matmul(qc[:], mats[:, tm, :], pp, start=(pi == 0),
                                                                        stop=(pi == len(plist) - 1)), r=[ppb, bC], w=[qcb])

            def cpo():
                k.op("act", lambda e: e.copy(out=co[:, hs], in_=qc[:]), r=[qcb], w=[cob])

            def st():
                if cc == 1:
                    k.dma(SC[t, :, :], co[:], r=[cob], w=[dSC])
            return imm, cpo, st

        load_t(0)
        for cc in range(2):
            plist = []
            for (ma, mb, t1, t2) in ((4, 5, 14, 15), (8, 9, 16, 17)):
                mm, cp, pr, (q1, q1b), (q2, q2b) = front(0, cc, ma, mb)
                mm(); cp(); pr()
                plist += [(q1, q1b, t1), (q2, q2b, t2)]
            imm, cpo, st = back(0, cc, plist)
            imm(); cpo(); st()

        def pc2(it):
            t, cc = 1 + it // 2, it % 2
            mm, cp, pr, (q1, q1b), (q2, q2b) = front(t, cc, 0, 1)
            imm, cpo, st = back(t, cc, [(q1, q1b, 12), (q2, q2b, 13)])

            def s0():
                if cc == 0:
                    load_t(t)
            return [s0, mm, cp, pr, imm, cpo, st]
        pipe(126, pc2)
        k.barrier()
        x0a = sb("x0a", [128, 8, NTOK], BF16, ph)
        bx0 = k.buf("x0a")
        for ct in range(8):
            k.dma(x0a[:, ct, :], SX0[ct, :, :], r=[dSX0], w=[bx0], acc=(ct > 0))
        DT = ring("Dt", 3, [128, 8, D], BF16, ph)
        x0v = x0a[:].rearrange("p c (i m) -> p c i m", m=64)
        yav = yaa[:].rearrange("p c (i m) -> p c i m", m=64)
        PSM = psring([0, 1, 2, 3])

        def pc3(mg):
            dt_, dtb = DT.next()

            def s0():
                for mm in range(8):
                    m2 = mg * 8 + mm
                    k.dma(dt_[0:64, mm, :], SC[:, m2, :], r=[dSC], w=[dtb], acc=(mm > 0))
                    k.dma(dt_[64:128, mm, :], SC[:, 64 + m2, :], r=[dSC], w=[dtb], acc=True)

            def s1():
                for ct in range(8):
                    ps, pb = PSM.next()
                    for mm in range(8):
                        m2 = mg * 8 + mm
                        k.op("pe", lambda e, mm=mm, m2=m2: e.matmul(ps[:, mm * NM1:(mm + 1) * NM1], dt_[:, mm, ct * 128:(ct + 1) * 128],
                                                                     Lm[:, m2, :], start=True, stop=True), r=[dtb, bC], w=[pb])
                    k.op("dve", lambda e: e.tensor_tensor(
                        out=yav[:, ct, :, mg * 8:(mg + 1) * 8],
                        in0=ps[:, 0:8 * NM1].rearrange("p (m i) -> p i m", i=NM1),
                        in1=x0v[:, ct, :, mg * 8:(mg + 1) * 8], op=ALU.mult), r=[pb, bx0], w=[bya[ct]])
            return [s0, s1]
        pipe(8, pc3)
        k.barrier()

    with contextlib.ExitStack() as ph:
        vh = sb("vh", [128, NCH, D], BF16, ph)
        bvh = k.buf("vh")
        for c_ in range(NCH):
            k.dma(vh[:, c_, :], SVS[c_, :, :], r=[dSVS], w=[bvh], acc=(c_ > 0))
        wsf = sb("wsf", [128, 8 * 128], F32, ph)
        wsb = sb("wsb", [128, 8, 128], BF16, ph)
        sbf = sb("sbf", [1, 8 * 128], F32, ph)
        sbb = sb("sbb", [1, 8, 128], BF16, ph)
        ones = sb("ones", [1, 128], BF16, ph)
        bws = k.buf("ws")
        k.dma(wsf[:], wsT_d[:, :], w=[bws])
        k.dma(sbf[:], sgub_d[:, :], w=[bws], acc=True)
        bws2 = k.buf("ws2")
        k.op("dve", lambda e: e.tensor_copy(out=wsb[:].rearrange("p a b -> p (a b)"), in_=wsf[:]), r=[bws], w=[bws2])
        k.op("dve", lambda e: e.tensor_copy(out=sbb[:].rearrange("p a b -> p (a b)"), in_=sbf[:]), r=[bws], w=[bws2])
        k.op("pool", lambda e: e.memset(ones[:], 1.0), w=[bws2])
        USr = ring("usr", 3, [128, NTOK], BF16, ph)
        PSM = psring([0, 1, 2, 3])

        def pd1(g):
            us, usb = USr.next()

            def s0():
                k.dma(us[:], SUS[g, :, :], r=[dSUS], w=[usb])

            def s1():
                for cg in range(5):
                    chunks = list(range(cg * 4, min(cg * 4 + 4, NCH)))
                    ps, pb = PSM.next()
                    for ci_, ch in enumerate(chunks):
                        k.op("pe", lambda e, ci_=ci_, ch=ch: e.matmul(ps[:, ci_ * 128:(ci_ + 1) * 128], vh[:, ch, g * 128:(g + 1) * 128],
                                                                       wsb[:, g, :], start=True, stop=False), r=[bvh, bws2], w=[pb])
                        k.op("pe", lambda e, ci_=ci_: e.matmul(ps[:, ci_ * 128:(ci_ + 1) * 128], ones[0:1, :], sbb[0:1, g, :],
                                                                start=False, stop=True), r=[bws2], w=[pb])
                    n = len(chunks) * 128
                    sl = slice(cg * 512, cg * 512 + n)
                    k.op("dve", lambda e: e.tensor_tensor(out=ybT[:, g, sl], in0=ps[:, 0:n], in1=us[:, sl], op=ALU.mult),
                         r=[pb, usb], w=[byb[g]])
            return [s0, s1]
        pipe(8, pd1)
        k.barrier()

    with contextlib.ExitStack() as ph:
        yaT = yaa
        WF = ring("pwf", 6, [128, 8, 128], F32, ph)
        WB = ring("pwb", 6, [128, 8, 128], BF16, ph)
        GA = ring("ga", 3, [128, NTOK], BF16, ph)
        GB = ring("gb", 3, [128, NTOK], BF16, ph)
        M1 = ring("m1_", 2, [128, 512], F32, ph)
        M2 = ring("m2_", 2, [128, 512], F32, ph)
        MO = ring("mo", 3, [128, NTOK], BF16, ph)
        wphv = w_ph.rearrange("(kc p) c -> p kc c", p=128)
        wpsv = w_ps.rearrange("(kc p) c -> p kc c", p=128)
        PSM = psring([0, 1, 2, 3])

        def pd2(dtile):
            wl = [(WF.next(), WB.next()) for _ in range(2)]
            ga, gab = GA.next()
            gb, gbb = GB.next()
            mo, mob = MO.next()

            def s0():
                for ((wf, wfb), _), wv in zip(wl, (wphv, wpsv)):
                    k.dma(wf[:], wv[:, :, dtile * 128:(dtile + 1) * 128], w=[wfb])
                k.dma(ga[:], SG[dtile, :, :], r=[dSG], w=[gab])
                k.dma(gb[:], SG[8 + dtile, :, :], r=[dSG], w=[gbb])

            def s1():
                for ((wf, wfb), (wb, wbb)) in wl:
                    k.op("dve", lambda e, wf=wf, wb=wb: e.tensor_copy(out=wb[:], in_=wf[:]), r=[wfb], w=[wbb])

            def s2():
                for (t0, tn) in TOKT:
                    psA, pbA = PSM.next()
                    psB, pbB = PSM.next()
                    for (ps, pb, (_, (wb, wbb)), yT, by) in ((psA, pbA, wl[0], yaT, bya), (psB, pbB, wl[1], ybT, byb)):
                        for kc in range(8):
                            k.op("pe", lambda e, kc=kc, ps=ps, wb=wb, yT=yT: e.matmul(ps[:, 0:tn], wb[:, kc, :], yT[:, kc, t0:t0 + tn],
                                                                                         start=(kc == 0), stop=(kc == 7)),
                                 r=[wbb, by[kc]], w=[pb])
                    m1, m1b = M1.next()
                    m2_, m2b = M2.next()
                    k.op("dve", lambda e: e.tensor_tensor(out=m1[:, 0:tn], in0=psA[:, 0:tn], in1=ga[:, t0:t0 + tn], op=ALU.mult),
                         r=[pbA, gab], w=[m1b])
                    k.op("dve", lambda e: e.tensor_tensor(out=m2_[:, 0:tn], in0=psB[:, 0:tn], in1=gb[:, t0:t0 + tn], op=ALU.mult),
                         r=[pbB, gbb], w=[m2b])
                    k.op("pool", lambda e: e.tensor_tensor(out=mo[:, t0:t0 + tn], in0=m1[:, 0:tn], in1=m2_[:, 0:tn], op=ALU.add),
                         r=[m1b, m2b], w=[mob])

            def s3():
                k.dma(SM[dtile, :, :], mo[:], r=[mob], w=[dSM])
            return [s0, s1, s2, s3]
        pipe(8, pd2)
        k.barrier()
    mid.close()

    tail = contextlib.ExitStack()
    wdb = sb("wdb", [128, NFT, D], BF16, tail)
    bwd = k.bufs(NFT, "wdb")
    h2s = contextlib.ExitStack()
    h2T = sb("h2T", [128, 8, NTOK], BF16, h2s)
    with contextlib.ExitStack() as ph:
        mT = sb("mT", [128, 8, NTOK], BF16, ph)
        bmT, bh2 = k.buf(), k.bufs(NCH, "h2")
        bmTg = k.bufs(len(TOKT), "mTg")
        smv = SM.rearrange("m p t -> p m t")
        for gi, (t0, tn) in enumerate(TOKT):
            k.dma(mT[:, :, t0:t0 + tn], smv[:, :, t0:t0 + tn], r=[dSM], w=[bmTg[gi]])
        WOF = ring("wof", 2, [128, 8, 512], F32, ph)
        wob = sb("wob", [128, 8, D], BF16, ph)
        bwob = k.buf()
        wov = w_o.rearrange("(kc p) c -> p kc c", p=128)
        for hh in range(2):
            wof, bwof = WOF.next()
            k.dma(wof[:], wov[:, :, hh * 512:(hh + 1) * 512], w=[bwof])
            k.op("dve", lambda e, hh=hh: e.tensor_copy(out=wob[:, :, hh * 512:(hh + 1) * 512], in_=wof[:]), r=[bwof], w=[bwob])
        XT = ring("xt2_", 3, [128, D], F32, ph)
        X1 = ring("x1_", 4, [128, D], F32, ph)
        XS = ring("xs2_", 3, [128, D], BF16, ph)
        junk = (sb("junk2", [128, D], BF16, ph), k.buf())
        SSs = Ring([(sb("ss2_%d" % i, [128, 1], F32, ph), sb("sst2_%d" % i, [128, 1], F32, ph),
                     sb("srs2_%d" % i, [128, 1], F32, ph), k.buf(), k.buf()) for i in range(4)])
        g2b = g2T[:, :].unsqueeze(2).to_broadcast([128, 8, 128])
        PT = ptring([2, 3])
        PSA = psring([0, 1])

        def pd3(i):
            xt, xtb = XT.next()
            ps, pb = PSA.next()
            x1, x1b = X1.next()
            ss, sst, srs, bss, brs = SSs.next()
            xs, xsb = XS.next()
            pt, ptb = PT.next()

            def s0():
                k.dma(xt[:], x_in[i * 128:(i + 1) * 128, :], w=[xtb])

            def s1():
                for hh in range(2):
                    for kc in range(8):
                        k.op("pe", lambda e, kc=kc, hh=hh: e.matmul(ps[:, hh * 512:(hh + 1) * 512], mT[:, kc, i * 128:(i + 1) * 128],
                                                                      wob[:, kc, hh * 512:(hh + 1) * 512], start=(kc == 0), stop=(kc == 7)),
                             r=[bmTg[min(i // 4, 4)], bwob], w=[pb])

            def s2():
                k.op("dve", lambda e: e.tensor_tensor(out=x1[:], in0=ps[:], in1=xt[:], op=ALU.add), r=[pb, xtb], w=[x1b])

            def s3():
                k.dma(SX1[i, :, :], x1[:], r=[x1b], w=[dSX1])
                ssq(junk, x1[:], ss, x1b, bss)

            def s4():
                rstd_ops(ss, sst, srs, bss, brs, D)

            def s5():
                k.op("act", lambda e: e.activation(out=xs[:], in_=x1[:], func=AF.Copy, scale=srs[:, 0:1]),
                     r=[x1b, brs], w=[xsb])

            def s6():
                for kc in range(8):
                    k.op("pe", lambda e, kc=kc: e.transpose(pt[:, kc * 128:(kc + 1) * 128], xs[:, kc * 128:(kc + 1) * 128], ident[:]),
                         r=[xsb, bC], w=[ptb])

            def s7():
                k.op("dve", lambda e: e.tensor_tensor(out=h2T[:, :, i * 128:(i + 1) * 128],
                                                       in0=pt.rearrange("p (a b) -> p a b", b=128), in1=g2b, op=ALU.mult),
                     r=[ptb, bC], w=[bh2[i]])
            return [s0, s1, s2, s3, s4, s5, s6, s7]
        pipe(NCH, pd3)
        k.barrier()

    with contextlib.ExitStack() as ph:
        bh2 = k.buf()
        WF = ring("uwf", 6, [128, 8, 128], F32, ph)
        WB = ring("uwb", 6, [128, 8, 128], BF16, ph)
        WDF = ring("wdf", 3, [128, D], F32, ph)
        AR_ = ring("araw", 2, [128, NTOK], F32, ph)
        AC_ = ring("acv", 2, [128, NTOK], F32, ph)
        GT_ = ring("gte", 3, [128, NTOK], BF16, ph)
        SL_ = ring("sil", 2, [128, NTOK], BF16, ph)
        AO_ = ring("aco", 3, [128, NTOK], BF16, ph)
        wupv = w_up.rearrange("(kc p) c -> p kc c", p=128)
        PSM = psring([0, 1, 2, 3])

        def pe1(mt):
            wl = [(WF.next(), WB.next()) for _ in range(2)]
            araw, arb = AR_.next()
            acv, acb = AC_.next()
            gte, gtb = GT_.next()
            sil, slb = SL_.next()
            aco, aob = AO_.next()
            wdf, wdfb = WDF.next()

            def s0():
                for ((wf, wfb), _), col0 in zip(wl, (mt * 128, FFN + mt * 128)):
                    k.dma(wf[:], wupv[:, :, col0:col0 + 128], w=[wfb])
                k.dma(wdf[:], w_dn[mt * 128:(mt + 1) * 128, :], w=[wdfb])

            def s1():
                for ((wf, wfb), (wb, wbb)) in wl:
                    k.op("dve", lambda e, wf=wf, wb=wb: e.tensor_copy(out=wb[:], in_=wf[:]), r=[wfb], w=[wbb])
                k.op("act", lambda e: e.copy(out=wdb[:, mt, :], in_=wdf[:]), r=[wdfb], w=[bwd[mt]])

            def s2():
                for (t0, tn) in TOKT:
                    psA, pbA = PSM.next()
                    psG, pbG = PSM.next()
                    for (ps, pb, (_, (wb, wbb))) in ((psA, pbA, wl[0]), (psG, pbG, wl[1])):
                        for kc in range(8):
                            k.op("pe", lambda e, kc=kc, ps=ps, wb=wb: e.matmul(ps[:, 0:tn], wb[:, kc, :], h2T[:, kc, t0:t0 + tn],
                                                                                 start=(kc == 0), stop=(kc == 7)), r=[wbb, bh2], w=[pb])
                    k.op("act", lambda e: e.copy(out=araw[:, t0:t0 + tn], in_=psA[:, 0:tn]), r=[pbA], w=[arb])
                    k.op("act", lambda e: e.activation(out=acv[:, t0:t0 + tn], in_=psA[:, 0:tn], func=AF.Identity,
                                                        scale=fcw[:, mt, 1:2], bias=fcw[:, mt, 3:4]), r=[pbA, bC], w=[acb])
                    k.op("dve", lambda e: e.tensor_copy(out=gte[:, t0:t0 + tn], in_=psG[:, 0:tn]), r=[pbG], w=[gtb])

            def s3():
                mac(acv[:, 1:NTOK], araw[:, 0:NTOK - 1], fcw[:, mt, 0:1], acb, arb)
                mac(acv[:, 0:NTOK - 1], araw[:, 1:NTOK], fcw[:, mt, 2:3], acb, arb)

            def s4():
                k.op("act", lambda e: e.activation(out=sil[:], in_=acv[:], func=AF.Silu), r=[acb], w=[slb])

            def s5():
                k.op("dve", lambda e: e.tensor_tensor(out=aco[:], in0=sil[:], in1=gte[:], op=ALU.mult), r=[slb, gtb], w=[aob])

            def s6():
                k.dma(SACT[mt, :, :], aco[:], r=[aob], w=[dSACT])
            return [s0, s1, s2, s3, s4, s5, s6]
        pipe(NFT, pe1)
        k.barrier()
    h2s.close()

    with contextlib.ExitStack() as ph:
        aT = sb("aT", [128, NFT, NTOK], BF16, ph)
        baT = k.buf()
        baTg = k.bufs(len(TOKT), "aTg")
        sactv = SACT.rearrange("m p t -> p m t")
        for gi, (t0, tn) in enumerate(TOKT):
            for mh in range(2):
                k.dma(aT[:, mh * 11:(mh + 1) * 11, t0:t0 + tn], sactv[:, mh * 11:(mh + 1) * 11, t0:t0 + tn],
                      r=[dSACT], w=[baTg[gi]], acc=(mh > 0))
        gfb = (sb("gfb", [128, D], F32, ph), k.buf())
        k.dma(gfb[0][:], gfin_d[0, :].partition_broadcast(128), w=[gfb[1]])
        X1 = ring("x1b_", 3, [128, D], F32, ph)
        X2 = ring("x2_", 4, [128, D], F32, ph)
        OT = ring("ot", 3, [128, D], F32, ph)
        junk = (sb("junk3", [128, D], BF16, ph), k.buf())
        SSs = Ring([(sb("ss3_%d" % i, [128, 1], F32, ph), sb("sst3_%d" % i, [128, 1], F32, ph),
                     sb("srs3_%d" % i, [128, 1], F32, ph), k.buf(), k.buf()) for i in range(4)])
        PSA = psring([0, 1, 2, 3])

        def pe2(i):
            x1, x1b = X1.next()
            ps, pb = PSA.next()
            x2, x2b = X2.next()
            ss, sst, srs, bss, brs = SSs.next()
            ot, otb = OT.next()

            def s0():
                k.dma(x1[:], SX1[i, :, :], r=[dSX1], w=[x1b])

            def s1():
                for hh in range(2):
                    for mt in range(NFT):
                        k.op("pe", lambda e, mt=mt, hh=hh: e.matmul(ps[:, hh * 512:(hh + 1) * 512], aT[:, mt, i * 128:(i + 1) * 128],
                                                                      wdb[:, mt, hh * 512:(hh + 1) * 512], start=(mt == 0), stop=(mt == NFT - 1)),
                             r=[baTg[min(i // 4, 4)], bwd[mt]], w=[pb])

            def s2():
                k.op("dve", lambda e: e.tensor_tensor(out=x2[:], in0=ps[:], in1=x1[:], op=ALU.add), r=[pb, x1b], w=[x2b])

            def s3():
                ssq(junk, x2[:], ss, x2b, bss)

            def s4():
                rstd_ops(ss, sst, srs, bss, brs, D)

            def s5():
                k.op("dve", lambda e: e.scalar_tensor_tensor(out=ot[:], in0=x2[:], scalar=srs[:, 0:1], in1=gfb[0][:],
                                                              op0=ALU.mult, op1=ALU.mult), r=[x2b, brs, gfb[1]], w=[otb])

            def s6():
                k.dma(out_d[i * 128:(i + 1) * 128, :], ot[:], r=[otb])
            return [s0, s1, s2, s3, s4, s5, s6]
        pipe(NCH, pe2)
        k.barrier()
    tail.close()
    es.close()
    return nc


_PROG = {}


def _prep_common(inp):
    f32 = np.float32
    c = {}
    c["w_in"] = np.ascontiguousarray(inp["w_in"][0], dtype=f32)
    c["w_ph"] = np.ascontiguousarray(inp["w_proj_hyena"][0], dtype=f32)
    c["w_ps"] = np.ascontiguousarray(inp["w_proj_sgu"][0], dtype=f32)
    c["w_o"] = np.ascontiguousarray(inp["w_out"][0], dtype=f32)
    c["w_up"] = np.ascontiguousarray(inp["w_up"][0], dtype=f32)
    c["w_dn"] = np.ascontiguousarray(inp["w_down"][0], dtype=f32)
    c["g1T"] = np.ascontiguousarray(inp["norm1_g"][0].reshape(8, 128).T, dtype=f32)
    c["g2T"] = np.ascontiguousarray(inp["norm2_g"][0].reshape(8, 128).T, dtype=f32)
    hw = np.concatenate([inp["hy_conv_w"][0], inp["hy_conv_b"][0][None, :]], axis=0)
    c["hcw"] = np.ascontiguousarray(hw.reshape(4, 24, 128).transpose(2, 1, 0).reshape(128, 96), dtype=f32)
    fw = np.concatenate([inp["ffn_conv_w"][0], inp["ffn_conv_b"][0][None, :]], axis=0)
    c["fcw"] = np.ascontiguousarray(fw.reshape(4, NFT, 128).transpose(2, 1, 0).reshape(128, NFT * 4), dtype=f32)
    c["gfin"] = np.ascontiguousarray(inp["final_g"].reshape(1, D), dtype=f32)
    c["gsgu"] = np.ascontiguousarray(inp["sgu_norm_g"][0].reshape(1, D), dtype=f32)
    c["wsT"] = np.ascontiguousarray(inp["sgu_w"][0].transpose(2, 0, 1).reshape(128, 8 * 128), dtype=f32)
    c["sgub"] = np.ascontiguousarray(inp["sgu_b"][0].reshape(1, 8 * 128), dtype=f32)
    def bd(w):
        r, c_ = w.shape
        o = np.zeros((2 * r, 2 * c_), dtype=f32)
        o[:r, :c_] = w
        o[r:, c_:] = w
        return o
    c["fw1"] = bd(inp["filt_w1"][0])
    c["fw2"] = bd(inp["filt_w2"][0])
    c["fw3"] = bd(inp["filt_w3"][0])
    fv = np.stack([inp["filt_b1"][0], inp["filt_b2"][0], inp["filt_b3"][0], inp["filt_freq"][0]], axis=1)
    c["fvec"] = np.ascontiguousarray(np.concatenate([fv, fv], axis=0), dtype=f32)
    w4 = inp["filt_w4"][0]
    c["fw4"] = np.ascontiguousarray(np.concatenate([w4[:, :D], w4[:, D:]], axis=0), dtype=f32)
    c["dec"] = np.ascontiguousarray(inp["hy_decay"][0].reshape(2, D), dtype=f32)
    c["skip"] = np.ascontiguousarray(inp["hy_skip"][0].reshape(1, D), dtype=f32)
    return c


def kernel(**inputs):
    inp = {k_: np.asarray(v) for k_, v in inputs.items()}
    if "nc" not in _PROG:
        _PROG["nc"] = build_program()
        _PROG["consts"] = [make_consts(h) for h in (0, 1)]
    nc = _PROG["nc"]
    common = _prep_common(inp)
    x = np.asarray(inp["x"], dtype=np.float32)
    in_maps = []
    for core in range(8):
        b, h = core // 2, core % 2
        m = dict(common)
        m["x"] = np.ascontiguousarray(np.roll(x[b], -1920 * h, axis=0))
        cs = _PROG["consts"][h]
        for nm in ("S1u", "S1k", "mats", "Lm", "zT", "negt", "ident", "msk"):
            m["c_" + nm] = cs[nm]
        in_maps.append(m)
    res = run_bass_kernel_spmd(nc, in_maps, core_ids=list(range(8)))
    out = np.empty((4, SEQ, D), dtype=np.float32)
    for core in range(8):
        b, h = core // 2, core % 2
        o = res.results[core]["out"]
        if h == 0:
            out[b, 0:2048] = o[0:2048]
        else:
            out[b, 2048:4096] = o[128:2176]
    return out
```
